# Optimizing a Trainium2 kernel written in Bass

```python
import jax, jax.numpy as jnp
from jax import lax
import numpy as np

D_MODEL = 1024
BATCH = 2
SEQ = 16384
DEPTH = 2

GRID_W = 64
CTX_LEN = 256
HEAD_DIM = 128
A_HEADS = 4
A_KV_HEADS = 2
B_HEADS = 4
B_KV_HEADS = 2
Q_WIDTH = (A_HEADS + B_HEADS) * HEAD_DIM
KV_WIDTH = 2 * (A_KV_HEADS + B_KV_HEADS) * HEAD_DIM
BLOCK = 128
WINDOW = 128
ROPE_THETA = 10000.0
FNET_GROUPS = 4
D_FF = 2816
CONV_W = 3
N_MOD = 6
N_ATTN_LAYERS = (DEPTH + 1) // 2
N_FOURIER_LAYERS = DEPTH // 2
EPS = 1e-6

kernel_name = "hybrid_dit_attn_fnet_convffn"

F32 = jnp.float32


def rms_norm(x, g):
    xf = x.astype(F32)
    y = xf * lax.rsqrt(jnp.mean(xf * xf, axis=-1, keepdims=True) + EPS)
    return (y * g.astype(F32)).astype(x.dtype)


def modulate(h, shift, scale):
    return h * (1 + scale) + shift


def adaln_params(cond, w, bias):
    m = jax.nn.silu(cond) @ w + bias
    return jnp.split(m[..., None, :], N_MOD, axis=-1)


def axial_rope_tables(row, col):
    n_freq = HEAD_DIM // 4
    inv = ROPE_THETA ** (-jnp.arange(n_freq, dtype=F32) / n_freq)
    ang = jnp.stack([row.astype(F32)[:, None] * inv, col.astype(F32)[:, None] * inv], axis=1)
    return jnp.cos(ang), jnp.sin(ang)


def apply_rope(x, cos, sin):
    b, l, h, dh = x.shape
    xs = x.reshape(b, l, h, 2, 2, dh // 4)
    x1, x2 = xs[..., 0, :], xs[..., 1, :]
    cs = cos[None, :, None].astype(x.dtype)
    sn = sin[None, :, None].astype(x.dtype)
    out = jnp.stack([x1 * cs - x2 * sn, x2 * cs + x1 * sn], axis=-2)
    return out.reshape(b, l, h, dh)


def group_heads(q, n_kv):
    b, l, h, dh = q.shape
    return q.reshape(b, l, n_kv, h // n_kv, dh)


def split_kv(kv):
    b, l, _ = kv.shape
    a = A_KV_HEADS * HEAD_DIM
    bb = B_KV_HEADS * HEAD_DIM
    k_a, v_a, k_b, v_b = jnp.split(kv, [a, 2 * a, 2 * a + bb], axis=-1)
    return (k_a.reshape(b, l, A_KV_HEADS, HEAD_DIM), v_a.reshape(b, l, A_KV_HEADS, HEAD_DIM),
            k_b.reshape(b, l, B_KV_HEADS, HEAD_DIM), v_b.reshape(b, l, B_KV_HEADS, HEAD_DIM))


def attend(q, k, v, sink=None):
    s = jnp.einsum('bqkgd,bskd->bkgqs', q, k, preferred_element_type=F32) * (HEAD_DIM ** -0.5)
    if sink is not None:
        s_sink = jnp.broadcast_to(sink.astype(F32)[None, :, :, None, None], s.shape[:-1] + (1,))
        p = jax.nn.softmax(jnp.concatenate([s_sink, s], axis=-1), axis=-1)[..., 1:]
    else:
        p = jax.nn.softmax(s, axis=-1)
    return jnp.einsum('bkgqs,bskd->bqkgd', p.astype(v.dtype), v)


def global_attention(q, k, v, k_ctx, v_ctx):
    b, l, hkv, g, dh = q.shape
    k_all = jnp.concatenate([k_ctx, k], axis=1)
    v_all = jnp.concatenate([v_ctx, v], axis=1)
    qb = jnp.moveaxis(q.reshape(b, l // BLOCK, BLOCK, hkv, g, dh), 1, 0)
    o = lax.map(lambda qblk: attend(qblk, k_all, v_all), qb)
    return jnp.moveaxis(o, 0, 1).reshape(b, l, hkv * g * dh)


def window_attention(q, k, v, k_ctx, v_ctx, sink):
    b, l, hkv, g, dh = q.shape
    nb = l // BLOCK
    zpad = jnp.zeros((b, BLOCK, hkv, dh), k.dtype)

    def band(t):
        tp = jnp.concatenate([zpad, t, zpad], axis=1).reshape(b, nb + 2, BLOCK, hkv, dh)
        return jnp.concatenate([tp[:, :-2], tp[:, 1:-1], tp[:, 2:]], axis=2)

    kb, vb = band(k), band(v)
    qb = q.reshape(b, nb, BLOCK, hkv, g, dh)
    scale = HEAD_DIM ** -0.5
    s_loc = jnp.einsum('bnqkgd,bnskd->bnkgqs', qb, kb, preferred_element_type=F32) * scale
    s_ctx = jnp.einsum('bnqkgd,bskd->bnkgqs', qb, k_ctx, preferred_element_type=F32) * scale
    qi = jnp.arange(BLOCK)[:, None]
    si = jnp.arange(3 * BLOCK)[None, :]
    in_window = jnp.abs(si - BLOCK - qi) <= WINDOW
    kpos = (jnp.arange(nb)[:, None] - 1) * BLOCK + jnp.arange(3 * BLOCK)[None, :]
    in_range = (kpos >= 0) & (kpos < l)
    valid = in_window[None] & in_range[:, None, :]
    s_loc = jnp.where(valid[None, :, None, None], s_loc, -jnp.inf)
    s_sink = jnp.broadcast_to(sink.astype(F32)[None, None, :, :, None, None], s_ctx.shape[:-1] + (1,))
    p = jax.nn.softmax(jnp.concatenate([s_sink, s_ctx, s_loc], axis=-1), axis=-1)
    c_len = k_ctx.shape[1]
    p_ctx = p[..., 1:1 + c_len].astype(v.dtype)
    p_loc = p[..., 1 + c_len:].astype(v.dtype)
    o = (jnp.einsum('bnkgqs,bskd->bnqkgd', p_ctx, v_ctx)
         + jnp.einsum('bnkgqs,bnskd->bnqkgd', p_loc, vb))
    return o.reshape(b, l, hkv * g * dh)


def attention_mixers(h, hc, w_in, w_out, q_g, k_g, sink, cos, sin, with_ctx_queries):
    b, l, _ = h.shape
    c_len = hc.shape[1]
    proj = h @ w_in
    q_all, kv = proj[..., :Q_WIDTH], proj[..., Q_WIDTH:]
    q_a = q_all[..., :A_HEADS * HEAD_DIM].reshape(b, l, A_HEADS, HEAD_DIM)
    q_b = q_all[..., A_HEADS * HEAD_DIM:].reshape(b, l, B_HEADS, HEAD_DIM)
    k_a, v_a, k_b, v_b = split_kv(kv)
    q_a = apply_rope(rms_norm(q_a, q_g), cos, sin)
    k_a = apply_rope(rms_norm(k_a, k_g), cos, sin)
    q_b = apply_rope(q_b, cos, sin)
    k_b = apply_rope(k_b, cos, sin)
    k_ac, v_ac, k_bc, v_bc = split_kv(hc @ w_in[:, Q_WIDTH:])
    k_ac = rms_norm(k_ac, k_g)
    sink_hg = sink.reshape(B_KV_HEADS, B_HEADS // B_KV_HEADS)
    o_a = global_attention(group_heads(q_a, A_KV_HEADS), k_a, v_a, k_ac, v_ac)
    o_b = window_attention(group_heads(q_b, B_KV_HEADS), k_b, v_b, k_bc, v_bc, sink_hg)
    y = jnp.concatenate([o_a, o_b], axis=-1) @ w_out
    if not with_ctx_queries:
        return y, None
    qc = hc @ w_in[:, :Q_WIDTH]
    qc_a = rms_norm(qc[..., :A_HEADS * HEAD_DIM].reshape(b, c_len, A_HEADS, HEAD_DIM), q_g)
    qc_b = qc[..., A_HEADS * HEAD_DIM:].reshape(b, c_len, B_HEADS, HEAD_DIM)
    oc_a = attend(group_heads(qc_a, A_KV_HEADS), k_ac, v_ac).reshape(b, c_len, -1)
    oc_b = attend(group_heads(qc_b, B_KV_HEADS), k_bc, v_bc, sink_hg).reshape(b, c_len, -1)
    yc = jnp.concatenate([oc_a, oc_b], axis=-1) @ w_out
    return y, yc


def fourier_mix(h):
    b, l, d = h.shape
    hg = h.reshape(b, l, FNET_GROUPS, d // FNET_GROUPS).astype(F32)
    f = jnp.fft.fft2(hg, axes=(1, 3), norm="ortho").real
    return f.reshape(b, l, d).astype(h.dtype)


def conv_ffn(h, w_up, conv_w, conv_b, w_down):
    u = h @ w_up
    up = jnp.pad(u, ((0, 0), (1, 1), (0, 0)))
    u = up[:, :-2] * conv_w[0] + up[:, 1:-1] * conv_w[1] + up[:, 2:] * conv_w[2] + conv_b
    gate, val = jnp.split(u, 2, axis=-1)
    return (jax.nn.silu(gate) * val) @ w_down


def setup_inputs(seed: int = 0) -> dict:
    key = jax.random.key(seed)
    ks = jax.random.split(key, 17)

    def nrm(k, shape, scale):
        return jax.random.normal(k, shape, F32) * scale

    return {
        "x": nrm(ks[0], (BATCH, SEQ, D_MODEL), 1.0),
        "c": nrm(ks[1], (BATCH, D_MODEL), 1.0),
        "ctx": nrm(ks[2], (BATCH, CTX_LEN, D_MODEL), 1.0),
        "c_ctx": nrm(ks[3], (D_MODEL,), 1.0),
        "mod_w": nrm(ks[4], (DEPTH, D_MODEL, N_MOD * D_MODEL), 0.5 * D_MODEL ** -0.5),
        "mod_b": nrm(ks[5], (DEPTH, N_MOD * D_MODEL), 0.02),
        "norm_g": 1.0 + nrm(ks[6], (DEPTH, 4, D_MODEL), 0.02),
        "attn_w_in": nrm(ks[7], (N_ATTN_LAYERS, D_MODEL, Q_WIDTH + KV_WIDTH), D_MODEL ** -0.5),
        "attn_w_out": nrm(ks[8], (N_ATTN_LAYERS, Q_WIDTH, D_MODEL), Q_WIDTH ** -0.5),
        "q_norm_g": 1.0 + nrm(ks[9], (N_ATTN_LAYERS, HEAD_DIM), 0.02),
        "k_norm_g": 1.0 + nrm(ks[10], (N_ATTN_LAYERS, HEAD_DIM), 0.02),
        "sink": nrm(ks[11], (N_ATTN_LAYERS, B_HEADS), 0.5),
        "fourier_w_out": nrm(ks[12], (N_FOURIER_LAYERS, D_MODEL, D_MODEL), D_MODEL ** -0.5),
        "ffn_w_up": nrm(ks[13], (DEPTH, D_MODEL, 2 * D_FF), D_MODEL ** -0.5),
        "ffn_conv_w": nrm(ks[14], (DEPTH, CONV_W, 2 * D_FF), CONV_W ** -0.5),
        "ffn_conv_b": nrm(ks[15], (DEPTH, 2 * D_FF), 0.02),
        "ffn_w_down": nrm(ks[16], (DEPTH, D_FF, D_MODEL), D_FF ** -0.5),
    }


def reference(x, c, ctx, c_ctx, mod_w, mod_b, norm_g, attn_w_in, attn_w_out, q_norm_g, k_norm_g, sink,
              fourier_w_out, ffn_w_up, ffn_conv_w, ffn_conv_b, ffn_w_down):
    b, l, d = x.shape
    rows = l // GRID_W
    row = jnp.repeat(jnp.arange(rows), GRID_W)
    col = jnp.tile(jnp.arange(GRID_W), rows)
    cos, sin = axial_rope_tables(row, col)

    for i in range(DEPTH):
        ctx_next = any(j % 2 == 0 for j in range(i + 1, DEPTH))
        need_hc = (i % 2 == 0) or ctx_next
        sh1, sc1, g1, sh2, sc2, g2 = adaln_params(c, mod_w[i], mod_b[i])
        h = modulate(rms_norm(x, norm_g[i, 0]), sh1, sc1)
        if need_hc:
            csh1, csc1, cg1, csh2, csc2, cg2 = adaln_params(c_ctx, mod_w[i], mod_b[i])
            hc = modulate(rms_norm(ctx, norm_g[i, 0]), csh1, csc1)
        if i % 2 == 0:
            a = i // 2
            y, yc = attention_mixers(h, hc, attn_w_in[a], attn_w_out[a], q_norm_g[a], k_norm_g[a], sink[a],
                                     cos, sin, ctx_next)
        else:
            w_f = fourier_w_out[i // 2]
            y = fourier_mix(h) @ w_f
            yc = fourier_mix(hc) @ w_f if ctx_next else None
        x = x + g1 * rms_norm(y, norm_g[i, 1])
        h = modulate(rms_norm(x, norm_g[i, 2]), sh2, sc2)
        x = x + g2 * rms_norm(conv_ffn(h, ffn_w_up[i], ffn_conv_w[i], ffn_conv_b[i], ffn_w_down[i]), norm_g[i, 3])
        if ctx_next:
            ctx = ctx + cg1 * rms_norm(yc, norm_g[i, 1])
            hc = modulate(rms_norm(ctx, norm_g[i, 2]), csh2, csc2)
            ctx = ctx + cg2 * rms_norm(conv_ffn(hc, ffn_w_up[i], ffn_conv_w[i], ffn_conv_b[i], ffn_w_down[i]),
                                       norm_g[i, 3])
    return x
```

```python
import numpy as np
import ml_dtypes
from contextlib import ExitStack
import concourse.bass as bass
import concourse.mybir as mybir
from concourse.bass_utils import run_bass_kernel_spmd

F32 = mybir.dt.float32
BF16 = mybir.dt.bfloat16
AF = mybir.ActivationFunctionType
ALU = mybir.AluOpType
AX = mybir.AxisListType
NPBF = ml_dtypes.bfloat16

D = 1024
L = 16384
NB = 2
NCORE = 8
TOK = 4096
NT = TOK // 128
CTX = 256
DFF = 2816
EPS = 1e-6
SEM_LIMIT = 30000
SELF_SYNC = True


class Buf:
    __slots__ = ("name", "w", "r", "excl")

    def __init__(self, name="", excl=False):
        self.name = name
        self.w = None
        self.r = []
        self.excl = excl


class Tile:
    def __init__(self, t, buf):
        self.t = t
        self.buf = buf

    def __getitem__(self, k):
        return self.t[k]


class Sched:
    ENGS = ("pe", "act", "dve", "pool", "sp")

    def __init__(self, nc, stack, n_dma_sems=10):
        self.nc = nc
        self.stack = stack
        self.recs = {e: [] for e in self.ENGS}
        self.sems = []
        self.cur_sem = {}
        self.cnt = {}
        self.seen = {e: {} for e in self.ENGS}
        self.pending = {e: [] for e in self.ENGS}
        self.dma_ring = {}
        self.dma_pos = {}
        self.all_dma_events = []
        for e in ("pe", "act", "dve", "pool"):
            self.cur_sem[e] = self._new_sem(f"s_{e}_0")
        for q in ("sp", "act", "pool"):
            self.dma_ring[q] = [self._new_sem(f"d_{q}_{i}") for i in range(n_dma_sems)]
            self.dma_pos[q] = 0

    def _new_sem(self, name):
        h = self.stack.enter_context(self.nc.semaphore(name))
        self.sems.append(h)
        idx = len(self.sems) - 1
        self.cnt[idx] = 0
        return idx

    def _need(self, eng, ev, waits):
        s, v = ev
        if self.seen[eng].get(s, 0) >= v:
            return
        self.seen[eng][s] = v
        waits.append((s, v))

    def _collect(self, eng, reads, writes):
        waits = []
        for b in reads:
            if b.w is not None and (SELF_SYNC or b.w[1] != eng):
                self._need(eng, b.w[0], waits)
            if b.excl:
                for ev, e2 in b.r:
                    if e2 != eng:
                        self._need(eng, ev, waits)
        for b in writes:
            if b.w is not None and (SELF_SYNC or b.w[1] != eng):
                self._need(eng, b.w[0], waits)
            for ev, e2 in b.r:
                if e2 != eng:
                    self._need(eng, ev, waits)
        return waits

    @staticmethod
    def _bufs(lst):
        return [x.buf if isinstance(x, Tile) else x for x in lst]

    def op(self, eng, fn, reads=(), writes=(), inc=True):
        reads = self._bufs(reads)
        writes = self._bufs(writes)
        waits = self._collect(eng, reads, writes)
        self.pending[eng].append((reads, writes))
        if inc:
            s = self.cur_sem[eng]
            if self.cnt[s] >= SEM_LIMIT:
                s = self._new_sem(f"s_{eng}_{len(self.sems)}")
                self.cur_sem[eng] = s
            self.cnt[s] += 1
            ev = (s, self.cnt[s])
            for (rs, ws) in self.pending[eng]:
                for b in rs:
                    b.r.append((ev, eng))
                for b in ws:
                    b.w = (ev, eng)
                    b.r = []
            self.pending[eng] = []
            self.recs[eng].append((waits, fn, (s, 1)))
        else:
            self.recs[eng].append((waits, fn, None))

    def dma(self, q, out, in_, reads=(), writes=()):
        reads = self._bufs(reads)
        writes = self._bufs(writes)
        waits = self._collect(q, reads, writes)
        ring = self.dma_ring[q]
        s = ring[self.dma_pos[q] % len(ring)]
        self.dma_pos[q] += 1
        if self.cnt[s] > 0:
            self._need(q, (s, self.cnt[s]), waits)
        self.cnt[s] += 16
        ev = (s, self.cnt[s])
        for b in reads:
            b.r.append((ev, "dma_" + q))
        for b in writes:
            b.w = (ev, "dma_" + q)
            b.r = []
        self.recs[q].append((waits, (lambda e: e.dma_start(out=out, in_=in_)), (s, 16)))
        self.all_dma_events.append(ev)
        return ev

    def drain_dmas(self, eng="sp"):
        waits = []
        for ev in self.all_dma_events:
            self._need(eng, ev, waits)
        if waits:
            self.recs[eng].append((waits, None, None))
        self.all_dma_events = []

    def flush(self):
        nc = self.nc
        for e in self.ENGS:
            assert not self.pending[e], f"uncovered instrs on {e}"
        recs = self.recs
        sems = self.sems

        def replay(engine, lst):
            for waits, fn, inc in lst:
                for (s, v) in waits:
                    engine.wait_ge(sems[s], v)
                if fn is not None:
                    ins = fn(engine)
                    if inc is not None:
                        ins.then_inc(sems[inc[0]], inc[1])

        with nc.Block() as block:
            @block.tensor
            def _(e):
                replay(e, recs["pe"])

            @block.scalar
            def _(e):
                replay(e, recs["act"])

            @block.vector
            def _(e):
                replay(e, recs["dve"])

            @block.gpsimd
            def _(e):
                replay(e, recs["pool"])

            @block.sync
            def _(e):
                replay(e, recs["sp"])
        self.recs = {e: [] for e in self.ENGS}


class Prog:
    def __init__(self):
        self.nc = bass.Bass("TRN2", target_bir_lowering=False)
        self.stack = ExitStack()
        self.S = Sched(self.nc, self.stack)
        self.n_psum = 0

    def din(self, name, shape, dt):
        return self.nc.dram_tensor(name, list(shape), dt, kind="ExternalInput").ap()

    def dout(self, name, shape, dt):
        return self.nc.dram_tensor(name, list(shape), dt, kind="ExternalOutput").ap()

    def sb(self, name, shape, dt):
        t = self.stack.enter_context(self.nc.sbuf_tensor("s_" + name, list(shape), dt))
        return Tile(t, Buf(name))

    def bank(self, name, shape, dt):
        self.n_psum += 1
        assert self.n_psum <= 8
        t = self.stack.enter_context(self.nc.psum_tensor("p_" + name, list(shape), dt))
        return Tile(t, Buf(name, excl=True))

    def finish(self):
        self.S.drain_dmas("sp")
        self.S.flush()
        self.stack.close()
        return self.nc


def rstd_from_ms(S, ms, eps_t, out, n=None):
    S.op("act", lambda e: e.activation(out=out[:], in_=ms[:], func=AF.Ln, bias=eps_t[:], scale=1.0),
         reads=[ms, eps_t], writes=[out])
    S.op("act", lambda e: e.activation(out=out[:], in_=out[:], func=AF.Exp, scale=-0.5),
         reads=[out], writes=[out])


def adaln_table(P, cT_ap, modw_ap, modb_ap, col0, ncol, consume):
    S = P.S
    ng = ncol // 512
    c_sb = P.t_c
    S.dma("sp", c_sb[:], cT_ap, writes=[c_sb])
    S.op("act", lambda e: e.activation(out=P.t_sc[:], in_=c_sb[:], func=AF.Silu), reads=[c_sb], writes=[P.t_sc])
    for k in range(8):
        S.op("dve", lambda e, k=k: e.tensor_copy(out=P.t_screp[:, k, :], in_=P.t_sc[:, k:k + 1].to_broadcast([128, 128])),
             reads=[P.t_sc], writes=[P.t_screp])
    S.dma("sp", P.t_mb[:, 0:ncol], modb_ap[:, col0:col0 + ncol], writes=[P.t_mb])
    for k in range(8):
        mw = P.t_mw[k % 2]
        S.dma("sp" if k % 2 == 0 else "pool", mw[:, 0:ncol], modw_ap[k * 128:(k + 1) * 128, col0:col0 + ncol], writes=[mw])
        for g in range(ng):
            S.op("pe", lambda e, k=k, g=g, mw=mw: e.matmul(P.banks[g][:], lhsT=P.t_screp[:, k, :],
                                                             rhs=mw[:, g * 512:(g + 1) * 512], start=(k == 0), stop=False),
                 reads=[P.t_screp, mw], writes=[P.banks[g]], inc=(g == ng - 1))
    for g in range(ng):
        S.op("pe", lambda e, g=g: e.matmul(P.banks[g][:], lhsT=P.t_ones1[:], rhs=P.t_mb[:, g * 512:(g + 1) * 512],
                                           start=False, stop=True),
             reads=[P.t_ones1, P.t_mb], writes=[P.banks[g]])
        S.op("dve" if g % 2 == 0 else "act",
             (lambda e, g=g: e.tensor_copy(out=P.t_stage[:, g * 512:(g + 1) * 512], in_=P.banks[g][:])) if g % 2 == 0 else
             (lambda e, g=g: e.copy(out=P.t_stage[:, g * 512:(g + 1) * 512], in_=P.banks[g][:])),
             reads=[P.banks[g]], writes=[P.t_stage])
    consume(P.t_stage)


def build_proj():
    P = Prog()
    S = P.S
    x = P.din("x", [TOK, D], F32)
    ctx = P.din("ctx", [CTX, D], F32)
    cT = P.din("cT", [128, 8], F32)
    ccT = P.din("ccT", [128, 8], F32)
    mod_w = P.din("mod_w", [2, D, 6 * D], F32)
    mod_b = P.din("mod_b", [2, 1, 6 * D], F32)
    ng = P.din("ng", [128, 8 * D], F32)
    w_in = P.din("w_in", [D, 2048], F32)
    qg = P.din("qg", [128, 128], F32)
    kg = P.din("kg", [128, 128], F32)
    rope = P.din("rope", [TOK, 128], F32)
    ident = P.din("ident", [128, 128], BF16)
    QT = P.dout("QT", [8, 128, TOK], BF16)
    KT = P.dout("KT", [4, 128, TOK], BF16)
    V = P.dout("V", [TOK, 512], BF16)
    KTc = P.dout("KTc", [4, 128, CTX], BF16)
    Vc = P.dout("Vc", [CTX, 512], BF16)
    M = P.dout("M", [128, 2, 6 * D], F32)

    P.banks = [P.bank(f"bk{i}", [128, 512], F32) for i in range(6)]
    bT = P.bank("bT", [128, 1024], BF16)
    bU = P.bank("bU", [128, 1024], BF16)
    P.t_c = P.sb("c_sb", [128, 8], F32)
    P.t_sc = P.sb("sc_sb", [128, 8], F32)
    P.t_screp = P.sb("screp", [128, 8, 128], F32)
    P.t_mb = P.sb("mb", [1, 3072], F32)
    P.t_mw = [P.sb(f"mw{i}", [128, 3072], F32) for i in range(2)]
    P.t_ones1 = P.sb("ones1", [1, 128], F32)
    P.t_stage = P.sb("stage", [128, 3072], F32)
    eps_t = P.sb("eps", [128, 1], F32)
    idt = P.sb("ident", [128, 128], BF16)
    G1 = P.sb("G1", [128, D], F32)
    SH1 = P.sb("SH1", [128, D], F32)
    Gc = P.sb("Gc", [128, D], F32)
    SHc = P.sb("SHc", [128, D], F32)
    ng0 = P.sb("ng0", [128, D], F32)
    qg_t = P.sb("qg", [128, 128], F32)
    kg_t = P.sb("kg", [128, 128], F32)
    wbf = P.sb("wbf", [128, 8, 2048], BF16)
    wst = [P.sb(f"wst{i}", [128, 2048], F32) for i in range(2)]

    S.op("dve", lambda e: e.memset(P.t_ones1[:], 1.0), writes=[P.t_ones1])
    S.op("dve", lambda e: e.memset(eps_t[:], EPS), writes=[eps_t])
    S.dma("pool", idt[:], ident, writes=[idt])
    S.dma("pool", ng0[:], ng[:, 0:D], writes=[ng0])
    S.dma("pool", qg_t[:], qg, writes=[qg_t])
    S.dma("pool", kg_t[:], kg, writes=[kg_t])
    for k in range(8):
        ws = wst[k % 2]
        S.dma("pool", ws[:], w_in[k * 128:(k + 1) * 128, :], writes=[ws])
        S.op("pool", lambda e, k=k, ws=ws: e.tensor_copy(out=wbf[:, k, :], in_=ws[:]), reads=[ws], writes=[wbf])

    def mk_consume(layer, half, is_ctx):
        def consume(stage):
            if is_ctx:
                S.op("dve", lambda e: e.scalar_tensor_tensor(out=Gc[:], in0=stage[:, D:2 * D], scalar=1.0, in1=ng0[:],
                                                             op0=ALU.add, op1=ALU.mult), reads=[stage, ng0], writes=[Gc])
                S.op("dve", lambda e: e.tensor_copy(out=SHc[:], in_=stage[:, 0:D]), reads=[stage], writes=[SHc])
                return
            S.dma("sp", M[:, layer, half * 3072:(half + 1) * 3072], stage[:], reads=[stage])
            if layer == 0 and half == 0:
                S.op("dve", lambda e: e.scalar_tensor_tensor(out=G1[:], in0=stage[:, D:2 * D], scalar=1.0, in1=ng0[:],
                                                             op0=ALU.add, op1=ALU.mult), reads=[stage, ng0], writes=[G1])
                S.op("dve", lambda e: e.tensor_copy(out=SH1[:], in_=stage[:, 0:D]), reads=[stage], writes=[SH1])
        return consume

    adaln_table(P, ccT, mod_w[0], mod_b[0], 0, 3072, mk_consume(0, 0, True))
    for layer in range(2):
        for half in range(2):
            adaln_table(P, cT, mod_w[layer], mod_b[layer], half * 3072, 3072, mk_consume(layer, half, False))

    xt = [P.sb(f"xt{i}", [128, D], F32) for i in range(2)]
    cs = [P.sb(f"cs{i}", [128, 128], F32) for i in range(2)]
    junk = P.sb("junk", [128, D], F32)
    ms = P.sb("ms", [128, 1], F32)
    rstd = P.sb("rstd", [128, 1], F32)
    t1 = P.sb("t1", [128, D], F32)
    hb = P.sb("hb", [128, D], BF16)
    hT = P.sb("hT", [128, 8, 128], BF16)
    pr = P.sb("pr", [128, 2048], F32)
    sq = P.sb("sq", [128, 768], F32)
    ssq = P.sb("ssq", [128, 6], F32)
    rq = P.sb("rq", [128, 6], F32)
    rb = P.sb("rb", [128, 2048], BF16)
    ta = P.sb("ta", [128, 512], F32)
    tb = P.sb("tb", [128, 512], F32)
    tcc = P.sb("tcc", [128, 512], F32)
    td = P.sb("td", [128, 512], F32)
    qT = P.sb("qT", [128, 12, 128], BF16)
    vt = P.sb("vt", [128, 512], BF16)
    eps128 = eps_t
    bP = P.banks[1:5]

    def do_tile(src_ap, rope_ap, Gt, SHt, is_ctx, tcol):
        slot = do_tile.n % 2
        do_tile.n += 1
        xx = xt[slot]
        S.dma("sp", xx[:], src_ap, writes=[xx])
        if not is_ctx:
            S.dma("sp", cs[slot][:], rope_ap, writes=[cs[slot]])
        S.op("act", lambda e: e.activation(out=junk[:], in_=xx[:], func=AF.Square, scale=1.0 / 32, accum_out=ms[:]),
             reads=[xx], writes=[junk, ms])
        rstd_from_ms(S, ms, eps_t, rstd)
        S.op("dve", lambda e: e.scalar_tensor_tensor(out=t1[:], in0=xx[:], scalar=rstd[:, 0:1], in1=Gt[:],
                                                     op0=ALU.mult, op1=ALU.mult), reads=[xx, rstd, Gt], writes=[t1])
        S.op("pool", lambda e: e.tensor_tensor(out=hb[:], in0=t1[:], in1=SHt[:], op=ALU.add),
             reads=[t1, SHt], writes=[hb])
        for k in range(8):
            S.op("pe", lambda e, k=k: e.transpose(out=bT[:, k * 128:(k + 1) * 128], in_=hb[:, k * 128:(k + 1) * 128],
                                                  identity=idt[:]), reads=[hb, idt], writes=[bT], inc=(k == 7))
        S.op("act", lambda e: e.copy(out=hT[:].rearrange("p k t -> p (k t)"), in_=bT[:]), reads=[bT], writes=[hT])
        for g in range(4):
            for k in range(8):
                S.op("pe", lambda e, g=g, k=k: e.matmul(bP[g][:], lhsT=hT[:, k, :], rhs=wbf[:, k, g * 512:(g + 1) * 512],
                                                        start=(k == 0), stop=(k == 7)),
                     reads=[hT, wbf], writes=[bP[g]], inc=(k == 7))
        for g in range(4):
            if g % 2 == 0:
                S.op("dve", lambda e, g=g: e.tensor_copy(out=pr[:, g * 512:(g + 1) * 512], in_=bP[g][:]),
                     reads=[bP[g]], writes=[pr])
            else:
                S.op("act", lambda e, g=g: e.copy(out=pr[:, g * 512:(g + 1) * 512], in_=bP[g][:]),
                     reads=[bP[g]], writes=[pr])
        S.op("pool", lambda e: e.tensor_copy(out=vt[:, 0:256], in_=pr[:, 1280:1536]), reads=[pr], writes=[vt])
        S.op("pool", lambda e: e.tensor_copy(out=vt[:, 256:512], in_=pr[:, 1792:2048]), reads=[pr], writes=[vt])
        if is_ctx:
            S.dma("pool", Vc[tcol:tcol + 128, :], vt[:], reads=[vt])
        else:
            S.dma("pool", V[tcol:tcol + 128, :], vt[:], reads=[vt])
        S.op("dve", lambda e: e.tensor_tensor(out=sq[:, 0:512], in0=pr[:, 0:512], in1=pr[:, 0:512], op=ALU.mult),
             reads=[pr], writes=[sq])
        S.op("dve", lambda e: e.tensor_tensor(out=sq[:, 512:768], in0=pr[:, 1024:1280], in1=pr[:, 1024:1280], op=ALU.mult),
             reads=[pr], writes=[sq])
        S.op("dve", lambda e: e.tensor_reduce(out=ssq[:], in_=sq[:].rearrange("p (h d) -> p h d", d=128), axis=AX.X, op=ALU.add),
             reads=[sq], writes=[ssq])
        S.op("act", lambda e: e.activation(out=rq[:], in_=ssq[:], func=AF.Ln, bias=eps_t[:], scale=1.0 / 128),
             reads=[ssq, eps_t], writes=[rq])
        S.op("act", lambda e: e.activation(out=rq[:], in_=rq[:], func=AF.Exp, scale=-0.5), reads=[rq], writes=[rq])
        for (c0, nh, r0, gt) in ((0, 4, 0, qg_t), (1024, 2, 4, kg_t)):
            v3 = lambda c0=c0, nh=nh: pr[:, c0:c0 + nh * 128].rearrange("p (h d) -> p h d", d=128)
            S.op("dve", lambda e, v3=v3, nh=nh, r0=r0: e.tensor_tensor(
                out=v3(), in0=v3(), in1=rq[:, r0:r0 + nh].unsqueeze(2).to_broadcast([128, nh, 128]), op=ALU.mult),
                 reads=[pr, rq], writes=[pr])
            S.op("dve", lambda e, v3=v3, nh=nh, gt=gt: e.tensor_tensor(
                out=v3(), in0=v3(), in1=gt[:].unsqueeze(1).to_broadcast([128, nh, 128]), op=ALU.mult),
                 reads=[pr, gt], writes=[pr])
        if is_ctx:
            S.op("pool", lambda e: e.tensor_copy(out=rb[:, 1024:2048], in_=pr[:, 1024:2048]), reads=[pr], writes=[rb])
        else:
            csl = cs[slot]
            for (c0, nh, eng) in ((0, 4, "dve"), (512, 4, "pool"), (1024, 2, "dve"), (1536, 2, "pool")):
                n = nh * 64
                def v5(t, c0=c0, nh=nh):
                    return t[:, c0:c0 + nh * 128].rearrange("p (h a s f) -> p h a s f", h=nh, a=2, s=2)
                def v4(t, nh=nh):
                    return t[:, 0:nh * 64].rearrange("p (h a f) -> p h a f", h=nh, a=2)
                def cosb(nh=nh):
                    return csl[:, 0:64].rearrange("p (a f) -> p a f", a=2).unsqueeze(1).to_broadcast([128, nh, 2, 32])
                def sinb(nh=nh):
                    return csl[:, 64:128].rearrange("p (a f) -> p a f", a=2).unsqueeze(1).to_broadcast([128, nh, 2, 32])
                S.op(eng, lambda e, v5=v5, v4=v4, cosb=cosb: e.tensor_tensor(out=v4(ta), in0=v5(pr)[:, :, :, 0, :], in1=cosb(), op=ALU.mult),
                     reads=[pr, csl], writes=[ta])
                S.op(eng, lambda e, v5=v5, v4=v4, sinb=sinb: e.tensor_tensor(out=v4(tb), in0=v5(pr)[:, :, :, 1, :], in1=sinb(), op=ALU.mult),
                     reads=[pr, csl], writes=[tb])
                S.op(eng, lambda e, v5=v5, v4=v4: e.tensor_tensor(out=v5(rb)[:, :, :, 0, :], in0=v4(ta), in1=v4(tb), op=ALU.subtract),
                     reads=[ta, tb], writes=[rb])
                S.op(eng, lambda e, v5=v5, v4=v4, cosb=cosb: e.tensor_tensor(out=v4(tcc), in0=v5(pr)[:, :, :, 1, :], in1=cosb(), op=ALU.mult),
                     reads=[pr, csl], writes=[tcc])
                S.op(eng, lambda e, v5=v5, v4=v4, sinb=sinb: e.tensor_tensor(out=v4(td), in0=v5(pr)[:, :, :, 0, :], in1=sinb(), op=ALU.mult),
                     reads=[pr, csl], writes=[td])
                S.op(eng, lambda e, v5=v5, v4=v4: e.tensor_tensor(out=v5(rb)[:, :, :, 1, :], in0=v4(tcc), in1=v4(td), op=ALU.add),
                     reads=[tcc, td], writes=[rb])
        srcs = [j * 128 for j in range(8)] + [1024, 1152, 1536, 1664]
        first = 8 if is_ctx else 0
        for j in range(first, 12):
            bank = bU if j < 8 else bT
            jj = j if j < 8 else j - 8
            S.op("pe", lambda e, j=j, bank=bank, jj=jj: e.transpose(out=bank[:, jj * 128:(jj + 1) * 128],
                                                                    in_=rb[:, srcs[j]:srcs[j] + 128], identity=idt[:]),
                 reads=[rb, idt], writes=[bank], inc=(j == 7 or j == 11))
        if not is_ctx:
            S.op("act", lambda e: e.copy(out=qT[:, 0:8, :].rearrange("p j t -> p (j t)"), in_=bU[:]), reads=[bU], writes=[qT])
        S.op("dve", lambda e: e.tensor_copy(out=qT[:, 8:12, :].rearrange("p j t -> p (j t)"), in_=bT[:, 0:512]),
             reads=[bT], writes=[qT])
        if is_ctx:
            S.dma("pool", KTc[:, :, tcol:tcol + 128].rearrange("j d t -> d j t"), qT[:, 8:12, :], reads=[qT])
        else:
            S.dma("pool", QT[:, :, tcol:tcol + 128].rearrange("j d t -> d j t"), qT[:, 0:8, :], reads=[qT])
            S.dma("pool", KT[:, :, tcol:tcol + 128].rearrange("j d t -> d j t"), qT[:, 8:12, :], reads=[qT])
    do_tile.n = 0

    for t in range(CTX // 128):
        do_tile(ctx[t * 128:(t + 1) * 128, :], None, Gc, SHc, True, t * 128)
    for t in range(NT):
        do_tile(x[t * 128:(t + 1) * 128, :], rope[t * 128:(t + 1) * 128, :], G1, SH1, False, t * 128)
    return P.finish()


def rope_table():
    t = np.arange(L)
    row = (t // 64).astype(np.float32)
    col = (t % 64).astype(np.float32)
    inv = (10000.0 ** (-np.arange(32, dtype=np.float32) / 32)).astype(np.float32)
    ang = np.stack([row[:, None] * inv, col[:, None] * inv], axis=1)
    return np.concatenate([np.cos(ang).reshape(L, 64), np.sin(ang).reshape(L, 64)], axis=1).astype(np.float32)


def run(nc, in_maps):
    res = run_bass_kernel_spmd(nc, in_maps, core_ids=list(range(NCORE)))
    return res.results


def launch_proj(inp):
    nc = build_proj()
    rope = rope_table()
    ident = np.eye(128, dtype=np.float32).astype(NPBF)
    ngb = np.ascontiguousarray(np.broadcast_to(inp["norm_g"].reshape(1, 8 * D), (128, 8 * D)))
    qgb = np.ascontiguousarray(np.broadcast_to(inp["q_norm_g"].reshape(1, 128), (128, 128)))
    kgb = np.ascontiguousarray(np.broadcast_to(inp["k_norm_g"].reshape(1, 128), (128, 128)))
    ccT = np.ascontiguousarray(inp["c_ctx"].reshape(8, 128).T)
    maps = []
    for core in range(NCORE):
        b, s = divmod(core, 4)
        maps.append(dict(
            x=np.ascontiguousarray(inp["x"][b, s * TOK:(s + 1) * TOK]),
            ctx=np.ascontiguousarray(inp["ctx"][b]),
            cT=np.ascontiguousarray(inp["c"][b].reshape(8, 128).T),
            ccT=ccT, mod_w=inp["mod_w"], mod_b=inp["mod_b"].reshape(2, 1, 6 * D), ng=ngb,
            w_in=inp["attn_w_in"][0], qg=qgb, kg=kgb,
            rope=np.ascontiguousarray(rope[s * TOK:(s + 1) * TOK]), ident=ident))
    return run(nc, maps)


def sched_barrier(S):
    evs = []
    for e in ("pe", "act", "dve", "pool"):
        s = S.cur_sem[e]
        if S.cnt[s] > 0:
            evs.append((s, S.cnt[s]))
    for q in ("sp", "act", "pool"):
        for s in S.dma_ring[q]:
            if S.cnt[s] > 0:
                evs.append((s, S.cnt[s]))
    for e in S.ENGS:
        waits = []
        for ev in evs:
            S._need(e, ev, waits)
        if waits:
            S.recs[e].append((waits, None, None))


class Scope:
    def __init__(self, P):
        self.P = P

    def __enter__(self):
        self.saved = self.P.stack
        self.P.stack = ExitStack()
        return self

    def __exit__(self, *a):
        sched_barrier(self.P.S)
        self.P.S.flush()
        self.P.stack.close()
        self.P.stack = self.saved
        return False


SCALE = 128 ** -0.5
NKB = (CTX + L) // 128
NLB = 2 + NT + 2


def build_attn():
    P = Prog()
    S = P.S
    QT = P.din("QT", [8, 128, TOK], BF16)
    KaT = P.din("KaT", [2, 128, NKB * 128], BF16)
    Va = P.din("Va", [2, 128, NKB, 128], BF16)
    KbT = P.din("KbT", [2, 128, NLB * 128], BF16)
    Vb = P.din("Vb", [2, 128, NLB, 128], BF16)
    bm = P.din("bm", [4, 128, 256], BF16)
    x = P.din("x", [TOK, D], F32)
    M = P.din("M", [128, 6 * D], F32)
    ng = P.din("ng", [128, 8 * D], F32)
    w_out = P.din("w_out", [D, D], F32)
    sink = P.din("sink", [128, 4], F32)
    ident = P.din("ident", [128, 128], BF16)
    x1o = P.dout("x1", [TOK, D], F32)
    h2T = P.dout("h2T", [8, 128, TOK], BF16)

    banks = [P.bank(f"bk{i}", [128, 512], F32) for i in range(7)]
    bT = P.bank("bT", [128, 1024], BF16)
    OT = P.sb("OT", [128, 8, TOK], BF16)
    ones = P.sb("ones", [128, 128], BF16)
    idt = P.sb("ident", [128, 128], BF16)
    eps_t = P.sb("eps", [128, 1], F32)
    S.op("dve", lambda e: e.memset(ones[:], 1.0), writes=[ones])
    S.op("dve", lambda e: e.memset(eps_t[:], EPS), writes=[eps_t])
    S.dma("pool", idt[:], ident, writes=[idt])

    with Scope(P):
        KT = P.sb("KT", [128, NKB * 128], BF16)
        Vt = P.sb("Vt", [128, NKB, 128], BF16)
        Qg = [P.sb(f"Qg{i}", [128, 2, 256], BF16) for i in range(2)]
        pt = [P.sb(f"pt{i}", [128, 512], BF16) for i in range(3)]
        rden = P.sb("rden", [128, 512], F32)
        bS = banks[0:2]
        bO = banks[2:4]
        bD = banks[4:6]
        gi = 0
        for kv in range(2):
            for c in range(5):
                S.dma("sp", KT[:, c * 3328:(c + 1) * 3328], KaT[kv, :, c * 3328:(c + 1) * 3328], writes=[KT])
                S.dma("pool", Vt[:, c * 26:(c + 1) * 26, :], Va[kv, :, c * 26:(c + 1) * 26, :], writes=[Vt])
            for g in range(TOK // 256):
                q = Qg[gi % 2]
                o_b = bO[gi % 2]
                d_b = bD[gi % 2]
                gi += 1
                S.dma("sp", q[:], QT[2 * kv:2 * kv + 2, :, g * 256:(g + 1) * 256].rearrange("h d t -> d h t"), writes=[q])
                qf = q[:].rearrange("p h t -> p (h t)")

                def smm(j, qf=qf, q=q):
                    S.op("pe", lambda e, j=j: e.matmul(bS[j % 2][:], lhsT=KT[:, j * 128:(j + 1) * 128], rhs=qf, start=True, stop=True),
                         reads=[KT, q], writes=[bS[j % 2]])
                smm(0)
                for j in range(NKB):
                    if j + 1 < NKB:
                        smm(j + 1)
                    p = pt[j % 3]
                    S.op("act", lambda e, j=j, p=p: e.activation(out=p[:], in_=bS[j % 2][:], func=AF.Exp, scale=SCALE),
                         reads=[bS[j % 2]], writes=[p])
                    S.op("pe", lambda e, j=j, p=p, o_b=o_b: e.matmul(o_b[:], lhsT=Vt[:, j, :], rhs=p[:], start=(j == 0), stop=(j == NKB - 1)),
                         reads=[Vt, p], writes=[o_b], inc=False)
                    S.op("pe", lambda e, j=j, p=p, d_b=d_b: e.matmul(d_b[:], lhsT=ones[:], rhs=p[:], start=(j == 0), stop=(j == NKB - 1)),
                         reads=[ones, p], writes=[d_b])
                S.op("dve", lambda e, d_b=d_b: e.reciprocal(out=rden[:], in_=d_b[:]), reads=[d_b], writes=[rden])
                for h in range(2):
                    S.op("dve", lambda e, h=h, o_b=o_b, kv=kv, g=g: e.tensor_tensor(
                        out=OT[:, 2 * kv + h, g * 256:(g + 1) * 256], in0=o_b[:, h * 256:(h + 1) * 256],
                        in1=rden[:, h * 256:(h + 1) * 256], op=ALU.mult), reads=[o_b, rden], writes=[OT])

    with Scope(P):
        KT = P.sb("KTb", [128, NLB * 128], BF16)
        Vt = P.sb("Vtb", [128, NLB, 128], BF16)
        QB = P.sb("QB", [128, 2, TOK], BF16)
        mk = P.sb("mk", [128, 4, 256], BF16)
        sk = P.sb("sk", [128, 4], F32)
        ske = P.sb("ske", [128, 4], F32)
        pt = [P.sb(f"ptb{i}", [128, 256], BF16) for i in range(3)]
        rden = P.sb("rdenb", [128, 256], F32)
        S.dma("sp", mk[:], bm.rearrange("m p c -> p m c"), writes=[mk])
        S.dma("sp", sk[:], sink, writes=[sk])
        S.op("act", lambda e: e.activation(out=ske[:], in_=sk[:], func=AF.Exp), reads=[sk], writes=[ske])
        bS = banks[0:2]
        bO = banks[2:4]
        bD = banks[4:6]
        gi = 0
        for kv in range(2):
            S.dma("sp", KT[:], KbT[kv], writes=[KT])
            S.dma("pool", Vt[:], Vb[kv], writes=[Vt])
            S.dma("sp", QB[:], QT[4 + 2 * kv:6 + 2 * kv].rearrange("h d t -> d h t"), writes=[QB])
            for n in range(NT):
                o_b = bO[gi % 2]
                d_b = bD[gi % 2]
                gi += 1
                kblocks = [0, 1, n + 2, n + 3, n + 4]
                for i, kb in enumerate(kblocks):
                    sb_ = bS[i % 2]
                    mask = None
                    if i == 2:
                        mask = 2 if n == 0 else 0
                    if i == 4:
                        mask = 3 if n == NT - 1 else 1
                    S.op("pe", lambda e, kb=kb, n=n, sb_=sb_, mask=mask: e.matmul(
                        sb_[:, 0:256].rearrange("p (h t) -> p h t", h=2), lhsT=KT[:, kb * 128:(kb + 1) * 128],
                        rhs=QB[:, :, n * 128:(n + 1) * 128], start=True, stop=(mask is None)),
                        reads=[KT, QB], writes=[sb_], inc=(mask is None))
                    if mask is not None:
                        S.op("pe", lambda e, sb_=sb_, mask=mask: e.matmul(sb_[:, 0:256], lhsT=idt[:], rhs=mk[:, mask, :],
                                                                          start=False, stop=True),
                             reads=[idt, mk], writes=[sb_])
                    p = pt[i % 3]
                    S.op("act", lambda e, sb_=sb_, p=p: e.activation(out=p[:], in_=sb_[:, 0:256], func=AF.Exp, scale=SCALE),
                         reads=[sb_], writes=[p])
                    S.op("pe", lambda e, kb=kb, p=p, o_b=o_b, i=i: e.matmul(o_b[:, 0:256], lhsT=Vt[:, kb, :], rhs=p[:],
                                                                           start=(i == 0), stop=(i == 4)),
                         reads=[Vt, p], writes=[o_b], inc=False)
                    S.op("pe", lambda e, p=p, d_b=d_b, i=i: e.matmul(d_b[:, 0:256], lhsT=ones[:], rhs=p[:],
                                                                    start=(i == 0), stop=(i == 4)),
                         reads=[ones, p], writes=[d_b])
                for h in range(2):
                    S.op("dve", lambda e, h=h, d_b=d_b, kv=kv: e.tensor_scalar(
                        out=rden[:, h * 128:(h + 1) * 128], in0=d_b[:, h * 128:(h + 1) * 128],
                        scalar1=ske[:, 2 * kv + h:2 * kv + h + 1], scalar2=None, op0=ALU.add), reads=[d_b, ske], writes=[rden])
                S.op("dve", lambda e: e.reciprocal(out=rden[:], in_=rden[:]), reads=[rden], writes=[rden])
                for h in range(2):
                    S.op("dve", lambda e, h=h, o_b=o_b, kv=kv, n=n: e.tensor_tensor(
                        out=OT[:, 4 + 2 * kv + h, n * 128:(n + 1) * 128], in0=o_b[:, h * 128:(h + 1) * 128],
                        in1=rden[:, h * 128:(h + 1) * 128], op=ALU.mult), reads=[o_b, rden], writes=[OT])

    with Scope(P):
        wbf = P.sb("wobf", [128, 8, D], BF16)
        wst = [P.sb(f"wost{i}", [128, D], F32) for i in range(2)]
        for k in range(8):
            ws = wst[k % 2]
            S.dma("pool", ws[:], w_out[k * 128:(k + 1) * 128, :], writes=[ws])
            S.op("pool", lambda e, k=k, ws=ws: e.tensor_copy(out=wbf[:, k, :], in_=ws[:]), reads=[ws], writes=[wbf])
        GG1 = P.sb("GG1", [128, D], F32)
        G2 = P.sb("G2", [128, D], F32)
        SH2 = P.sb("SH2", [128, D], F32)
        tmpa = P.sb("tmpa", [128, D], F32)
        tmpb = P.sb("tmpb", [128, D], F32)
        S.dma("sp", tmpa[:], M[:, 2 * D:3 * D], writes=[tmpa])
        S.dma("sp", tmpb[:], ng[:, 1 * D:2 * D], writes=[tmpb])
        S.op("dve", lambda e: e.tensor_tensor(out=GG1[:], in0=tmpa[:], in1=tmpb[:], op=ALU.mult), reads=[tmpa, tmpb], writes=[GG1])
        S.dma("sp", tmpa[:], M[:, 4 * D:5 * D], reads=[], writes=[tmpa])
        S.dma("sp", tmpb[:], ng[:, 2 * D:3 * D], writes=[tmpb])
        S.op("dve", lambda e: e.scalar_tensor_tensor(out=G2[:], in0=tmpa[:], scalar=1.0, in1=tmpb[:], op0=ALU.add, op1=ALU.mult),
             reads=[tmpa, tmpb], writes=[G2])
        S.dma("sp", SH2[:], M[:, 3 * D:4 * D], writes=[SH2])
        xt = [P.sb(f"xt{i}", [128, D], F32) for i in range(2)]
        ysb = P.sb("ysb", [128, D], F32)
        junk = P.sb("junk", [128, D], F32)
        ms = P.sb("ms", [128, 1], F32)
        rstd = P.sb("rstd", [128, 1], F32)
        t1 = P.sb("t1", [128, D], F32)
        x1 = [P.sb(f"x1_{i}", [128, D], F32) for i in range(2)]
        hb = P.sb("hb", [128, D], BF16)
        hT = P.sb("hT", [128, 8, 128], BF16)
        bY = banks[0:2]
        for t in range(NT):
            xx = xt[t % 2]
            xo = x1[t % 2]
            S.dma("sp", xx[:], x[t * 128:(t + 1) * 128, :], writes=[xx])
            for n in range(2):
                for h in range(8):
                    S.op("pe", lambda e, n=n, h=h, t=t: e.matmul(bY[n][:], lhsT=OT[:, h, t * 128:(t + 1) * 128],
                                                                 rhs=wbf[:, h, n * 512:(n + 1) * 512], start=(h == 0), stop=(h == 7)),
                         reads=[OT, wbf], writes=[bY[n]], inc=(h == 7))
            S.op("act", lambda e: e.copy(out=ysb[:, 0:512], in_=bY[0][:]), reads=[bY[0]], writes=[ysb])
            S.op("dve", lambda e: e.tensor_copy(out=ysb[:, 512:1024], in_=bY[1][:]), reads=[bY[1]], writes=[ysb])
            S.op("act", lambda e: e.activation(out=junk[:], in_=ysb[:], func=AF.Square, scale=1.0 / 32, accum_out=ms[:]),
                 reads=[ysb], writes=[junk, ms])
            rstd_from_ms(S, ms, eps_t, rstd)
            S.op("dve", lambda e: e.scalar_tensor_tensor(out=t1[:], in0=ysb[:], scalar=rstd[:, 0:1], in1=GG1[:],
                                                         op0=ALU.mult, op1=ALU.mult), reads=[ysb, rstd, GG1], writes=[t1])
            S.op("pool", lambda e, xx=xx, xo=xo: e.tensor_tensor(out=xo[:], in0=t1[:], in1=xx[:], op=ALU.add),
                 reads=[t1, xx], writes=[xo])
            S.dma("pool", x1o[t * 128:(t + 1) * 128, :], xo[:], reads=[xo])
            S.op("act", lambda e, xo=xo: e.activation(out=junk[:], in_=xo[:], func=AF.Square, scale=1.0 / 32, accum_out=ms[:]),
                 reads=[xo], writes=[junk, ms])
            rstd_from_ms(S, ms, eps_t, rstd)
            S.op("dve", lambda e, xo=xo: e.scalar_tensor_tensor(out=t1[:], in0=xo[:], scalar=rstd[:, 0:1], in1=G2[:],
                                                                op0=ALU.mult, op1=ALU.mult), reads=[xo, rstd, G2], writes=[t1])
            S.op("pool", lambda e: e.tensor_tensor(out=hb[:], in0=t1[:], in1=SH2[:], op=ALU.add), reads=[t1, SH2], writes=[hb])
            for k in range(8):
                S.op("pe", lambda e, k=k: e.transpose(out=bT[:, k * 128:(k + 1) * 128], in_=hb[:, k * 128:(k + 1) * 128],
                                                      identity=idt[:]), reads=[hb, idt], writes=[bT], inc=(k == 7))
            S.op("act", lambda e: e.copy(out=hT[:].rearrange("p k t -> p (k t)"), in_=bT[:]), reads=[bT], writes=[hT])
            S.dma("pool", h2T[:, :, t * 128:(t + 1) * 128].rearrange("k p t -> p k t"), hT[:], reads=[hT])
    return P.finish()


def window_masks(s):
    a = np.arange(128)[:, None]
    b = np.arange(128)[None, :]
    NEG = -30000.0
    mL = np.where(a >= b, 0.0, NEG).astype(np.float32)
    mR = np.where(a <= b, 0.0, NEG).astype(np.float32)
    full = np.full((128, 128), NEG, np.float32)
    ms = [mL, mR, full if s == 0 else mL, full if s == 3 else mR]
    return np.stack([np.concatenate([m, m], axis=1) for m in ms]).astype(NPBF)


def launch_attn(inp, r1):
    nc = build_attn()
    ident = np.eye(128, dtype=np.float32).astype(NPBF)
    ngb = np.ascontiguousarray(np.broadcast_to(inp["norm_g"].reshape(1, 8 * D), (128, 8 * D)))
    sinkb = np.ascontiguousarray(np.broadcast_to(inp["sink"].reshape(1, 4), (128, 4)))
    maps = []
    for core in range(NCORE):
        b, s = divmod(core, 4)
        grp = [r1[b * 4 + i] for i in range(4)]
        KT_all = np.concatenate([grp[0]["KTc"]] + [g["KT"] for g in grp], axis=2)
        V_all = np.concatenate([grp[0]["Vc"]] + [g["V"] for g in grp], axis=0)
        KaT = np.ascontiguousarray(KT_all[0:2])
        Va = np.ascontiguousarray(V_all[:, 0:256].reshape(NKB, 128, 2, 128).transpose(2, 1, 0, 3))
        zk = np.zeros((4, 128, 128), NPBF)
        zv = np.zeros((128, 512), NPBF)
        lo, hi = s * TOK + CTX - 128, (s + 1) * TOK + CTX + 128
        kt_parts = [KT_all[:, :, 0:CTX]]
        v_parts = [V_all[0:CTX]]
        if s == 0:
            kt_parts += [zk, KT_all[:, :, CTX:hi]]
            v_parts += [zv, V_all[CTX:hi]]
        elif s == 3:
            kt_parts += [KT_all[:, :, lo:], zk]
            v_parts += [V_all[lo:], zv]
        else:
            kt_parts += [KT_all[:, :, lo:hi]]
            v_parts += [V_all[lo:hi]]
        KbT = np.ascontiguousarray(np.concatenate(kt_parts, axis=2)[2:4])
        Vloc = np.concatenate(v_parts, axis=0)
        Vb = np.ascontiguousarray(Vloc[:, 256:512].reshape(NLB, 128, 2, 128).transpose(2, 1, 0, 3))
        maps.append(dict(QT=r1[core]["QT"], KaT=KaT, Va=Va, KbT=KbT, Vb=Vb, bm=window_masks(s),
                         x=np.ascontiguousarray(inp["x"][b, s * TOK:(s + 1) * TOK]),
                         M=np.ascontiguousarray(r1[core]["M"][:, 0, :]), ng=ngb, w_out=inp["attn_w_out"][0],
                         sink=sinkb, ident=ident))
    return run(nc, maps)


WIN = 256
NWIN = TOK // WIN
NFC = DFF // 128


def build_ffn(emit_next):
    P = Prog()
    S = P.S
    hT_in = P.din("hT", [8, 128, TOK + 2], BF16)
    x1 = P.din("x1", [TOK, D], F32)
    w_up = P.din("w_up", [D, 2 * DFF], F32)
    w_dn = P.din("w_dn", [DFF, D], F32)
    cw_d = P.din("cw", [128, 2 * NFC, 3], F32)
    cb_d = P.din("cb", [128, 2 * NFC], F32)
    Mg2 = P.din("Mg2", [128, D], F32)
    ng3 = P.din("ng3", [128, D], F32)
    if emit_next:
        Mn = P.din("Mn", [128, 2 * D], F32)
        ngn = P.din("ngn", [128, D], F32)
        h1n = P.dout("h1n", [TOK, D], BF16)
    x2o = P.dout("x2", [TOK, D], F32)

    banks = [P.bank(f"bk{i}", [128, 512], F32) for i in range(6)]
    eps_t = P.sb("eps", [128, 1], F32)
    S.op("dve", lambda e: e.memset(eps_t[:], EPS), writes=[eps_t])
    wup = P.sb("wup", [128, 8, 2 * DFF], BF16)
    wdn = P.sb("wdn", [128, NFC, D], BF16)
    stg = [P.sb(f"stg{i}", [128, 704], F32) for i in range(3)]
    cw = P.sb("cw", [128, 2 * NFC, 3], F32)
    cb = P.sb("cb", [128, 2 * NFC], F32)
    GG2 = P.sb("GG2", [128, D], F32)
    xt = [P.sb(f"xt{i}", [128, D], F32) for i in range(2)]
    t1 = P.sb("t1", [128, D], F32)
    S.dma("sp", cw[:], cw_d, writes=[cw])
    S.dma("sp", cb[:], cb_d, writes=[cb])
    S.dma("sp", t1[:], Mg2, writes=[t1])
    S.dma("sp", xt[0][:], ng3, writes=[xt[0]])
    S.op("dve", lambda e: e.tensor_tensor(out=GG2[:], in0=t1[:], in1=xt[0][:], op=ALU.mult), reads=[t1, xt[0]], writes=[GG2])
    if emit_next:
        G1n = P.sb("G1n", [128, D], F32)
        SH1n = P.sb("SH1n", [128, D], F32)
        S.dma("sp", t1[:], Mn[:, D:2 * D], writes=[t1])
        S.dma("sp", xt[1][:], ngn, writes=[xt[1]])
        S.op("dve", lambda e: e.scalar_tensor_tensor(out=G1n[:], in0=t1[:], scalar=1.0, in1=xt[1][:], op0=ALU.add, op1=ALU.mult),
             reads=[t1, xt[1]], writes=[G1n])
        S.dma("sp", SH1n[:], Mn[:, 0:D], writes=[SH1n])
    ci = 0
    engs = ("dve", "pool", "act")
    def cast(dst_ap, src_ap, dst_tile):
        nonlocal ci
        st_ = stg[ci % 3]
        eng = engs[ci % 3]
        S.dma("sp" if ci % 2 == 0 else "pool", st_[:, 0:src_ap.shape[1]], src_ap, writes=[st_])
        n = src_ap.shape[1]
        if eng == "act":
            S.op("act", lambda e: e.copy(out=dst_ap, in_=st_[:, 0:n]), reads=[st_], writes=[dst_tile])
        else:
            S.op(eng, lambda e: e.tensor_copy(out=dst_ap, in_=st_[:, 0:n]), reads=[st_], writes=[dst_tile])
        ci += 1
    for k in range(8):
        for pc in range(8):
            cast(wup[:, k, pc * 704:(pc + 1) * 704], w_up[k * 128:(k + 1) * 128, pc * 704:(pc + 1) * 704], wup)
    for c in range(NFC):
        for pc in range(2):
            cast(wdn[:, c, pc * 512:(pc + 1) * 512], w_dn[c * 128:(c + 1) * 128, pc * 512:(pc + 1) * 512], wdn)

    hw = [P.sb(f"hw{i}", [128, 8, WIN + 2], BF16) for i in range(2)]
    aT = P.sb("aT", [128, NFC, WIN], BF16)
    ag = [P.sb(f"ag{i}", [128, WIN], F32) for i in range(2)]
    av = [P.sb(f"av{i}", [128, WIN], F32) for i in range(2)]
    sg = [P.sb(f"sg{i}", [128, WIN], F32) for i in range(2)]
    junk = P.sb("junk", [128, D], BF16)
    ms2 = P.sb("ms2", [128, 2], F32)
    ms = P.sb("ms", [128, 1], F32)
    rstd = P.sb("rstd", [128, 1], F32)
    hb = P.sb("hb", [128, D], BF16)
    bG = banks[0:2]
    bV = banks[2:4]
    bY = banks[4:6]
    pi = 0
    ti = 0
    for w in range(NWIN):
        h = hw[w % 2]
        S.dma("sp", h[:], hT_in[:, :, w * WIN:w * WIN + WIN + 2].rearrange("k p t -> p k t"), writes=[h])
        for c in range(NFC):
            g_b = bG[pi % 2]
            v_b = bV[pi % 2]
            a_g = ag[pi % 2]
            a_v = av[pi % 2]
            s_g = sg[pi % 2]
            pi += 1
            for (bk, col0) in ((g_b, c * 128), (v_b, DFF + c * 128)):
                for k in range(8):
                    S.op("pe", lambda e, bk=bk, col0=col0, k=k, h=h: e.matmul(bk[:, 0:WIN + 2], lhsT=wup[:, k, col0:col0 + 128],
                                                                             rhs=h[:, k, :], start=(k == 0), stop=(k == 7)),
                         reads=[wup, h], writes=[bk], inc=(k == 7))
            for (bk, a_, cc) in ((g_b, a_g, c), (v_b, a_v, NFC + c)):
                S.op("act", lambda e, bk=bk, a_=a_, cc=cc: e.activation(out=a_[:], in_=bk[:, 1:WIN + 1], func=AF.Identity,
                                                                       bias=cb[:, cc:cc + 1], scale=cw[:, cc, 1:2]),
                     reads=[bk, cb, cw], writes=[a_])
                S.op("dve", lambda e, bk=bk, a_=a_, cc=cc: e.scalar_tensor_tensor(out=a_[:], in0=bk[:, 0:WIN], scalar=cw[:, cc, 0:1],
                                                                                 in1=a_[:], op0=ALU.mult, op1=ALU.add),
                     reads=[bk, cw, a_], writes=[a_])
                S.op("dve", lambda e, bk=bk, a_=a_, cc=cc: e.scalar_tensor_tensor(out=a_[:], in0=bk[:, 2:WIN + 2], scalar=cw[:, cc, 2:3],
                                                                                 in1=a_[:], op0=ALU.mult, op1=ALU.add),
                     reads=[bk, cw, a_], writes=[a_])
            S.op("act", lambda e, a_g=a_g, s_g=s_g: e.activation(out=s_g[:], in_=a_g[:], func=AF.Silu), reads=[a_g], writes=[s_g])
            S.op("pool", lambda e, s_g=s_g, a_v=a_v, c=c: e.tensor_tensor(out=aT[:, c, :], in0=s_g[:], in1=a_v[:], op=ALU.mult),
                 reads=[s_g, a_v], writes=[aT])
        for tt in range(WIN // 128):
            t = w * (WIN // 128) + tt
            xx = xt[ti % 2]
            ti += 1
            S.dma("sp", xx[:], x1[t * 128:(t + 1) * 128, :], writes=[xx])
            for n in range(2):
                for c in range(NFC):
                    S.op("pe", lambda e, n=n, c=c, tt=tt: e.matmul(bY[n][:], lhsT=aT[:, c, tt * 128:(tt + 1) * 128],
                                                                   rhs=wdn[:, c, n * 512:(n + 1) * 512], start=(c == 0), stop=(c == NFC - 1)),
                         reads=[aT, wdn], writes=[bY[n]], inc=(c == NFC - 1))
            for n in range(2):
                S.op("act", lambda e, n=n: e.activation(out=junk[:, 0:512], in_=bY[n][:], func=AF.Square, scale=1.0 / 32,
                                                        accum_out=ms2[:, n:n + 1]), reads=[bY[n]], writes=[junk, ms2])
            S.op("dve", lambda e: e.tensor_tensor(out=ms[:], in0=ms2[:, 0:1], in1=ms2[:, 1:2], op=ALU.add), reads=[ms2], writes=[ms])
            rstd_from_ms(S, ms, eps_t, rstd)
            for n in range(2):
                S.op("dve", lambda e, n=n: e.scalar_tensor_tensor(out=t1[:, n * 512:(n + 1) * 512], in0=bY[n][:], scalar=rstd[:, 0:1],
                                                                  in1=GG2[:, n * 512:(n + 1) * 512], op0=ALU.mult, op1=ALU.mult),
                     reads=[bY[n], rstd, GG2], writes=[t1])
            S.op("pool", lambda e, xx=xx: e.tensor_tensor(out=t1[:], in0=t1[:], in1=xx[:], op=ALU.add), reads=[t1, xx], writes=[t1])
            S.dma("pool", x2o[t * 128:(t + 1) * 128, :], t1[:], reads=[t1])
            if emit_next:
                S.op("act", lambda e: e.activation(out=junk[:], in_=t1[:], func=AF.Square, scale=1.0 / 32, accum_out=ms[:]),
                     reads=[t1], writes=[junk, ms])
                rstd_from_ms(S, ms, eps_t, rstd)
                S.op("dve", lambda e, xx=xx: e.scalar_tensor_tensor(out=xx[:], in0=t1[:], scalar=rstd[:, 0:1], in1=G1n[:],
                                                                    op0=ALU.mult, op1=ALU.mult), reads=[t1, rstd, G1n], writes=[xx])
                S.op("pool", lambda e, xx=xx: e.tensor_tensor(out=hb[:], in0=xx[:], in1=SH1n[:], op=ALU.add), reads=[xx, SH1n], writes=[hb])
                S.dma("pool", h1n[t * 128:(t + 1) * 128, :], hb[:], reads=[hb])
    return P.finish()


def bc(v):
    return np.ascontiguousarray(np.broadcast_to(np.asarray(v).reshape(1, -1), (128, np.asarray(v).size)))


def launch_ffn(inp, layer, hT_list, x1_list, M_list, emit_next):
    nc = build_ffn(emit_next)
    cw = np.ascontiguousarray(inp["ffn_conv_w"][layer].T.reshape(2 * NFC, 128, 3).transpose(1, 0, 2))
    cb = np.ascontiguousarray(inp["ffn_conv_b"][layer].reshape(2 * NFC, 128).T)
    maps = []
    z = np.zeros((8, 128, 1), NPBF)
    for core in range(NCORE):
        b, s = divmod(core, 4)
        lo = hT_list[core - 1][:, :, -1:] if s > 0 else z
        hi = hT_list[core + 1][:, :, 0:1] if s < 3 else z
        m = dict(hT=np.ascontiguousarray(np.concatenate([lo, hT_list[core], hi], axis=2)), x1=x1_list[core],
                 w_up=inp["ffn_w_up"][layer], w_dn=inp["ffn_w_down"][layer], cw=cw, cb=cb,
                 Mg2=np.ascontiguousarray(M_list[core][:, layer, 5 * D:6 * D]), ng3=bc(inp["norm_g"][layer, 3]))
        if emit_next:
            m["Mn"] = np.ascontiguousarray(M_list[core][:, layer + 1, 0:2 * D])
            m["ngn"] = bc(inp["norm_g"][layer + 1, 0])
        maps.append(m)
    return run(nc, maps)


def build_fnet():
    P = Prog()
    S = P.S
    h_all = P.din("h_all", [L, D], BF16)
    x2 = P.din("x2", [TOK, D], F32)
    Dcs_d = P.din("Dcs", [128, 256], BF16)
    E_d = P.din("E", [16, 128, 8, 128], BF16)
    CS_d = P.din("CS", [128, 2, 2, 2, 128], BF16)
    w_f = P.din("w_f", [D, D], F32)
    M = P.din("M", [128, 6 * D], F32)
    ng = P.din("ng", [128, 4 * D], F32)
    ident = P.din("ident", [128, 128], BF16)
    x3o = P.dout("x3", [TOK, D], F32)
    h2T = P.dout("h2T", [8, 128, TOK], BF16)

    banks = [P.bank(f"bk{i}", [128, 512], F32) for i in range(7)]
    bT = P.bank("bT", [128, 1024], BF16)
    FT = P.sb("FT", [128, 8, TOK], BF16)
    idt = P.sb("ident", [128, 128], BF16)
    eps_t = P.sb("eps", [128, 1], F32)
    S.op("dve", lambda e: e.memset(eps_t[:], EPS), writes=[eps_t])
    S.dma("pool", idt[:], ident, writes=[idt])
    hv = h_all.rearrange("(t2 t1) c -> t2 t1 c", t1=128)

    with Scope(P):
        Hc = P.sb("Hc", [128, 128, 128], BF16)
        B = P.sb("B", [128, 2, 128, 128], BF16)
        GT = P.sb("GT", [128, 2, 2, TOK], BF16)
        Eg = [P.sb(f"Eg{i}", [128, 8, 128], BF16) for i in range(2)]
        Dcs = P.sb("Dcs", [128, 256], BF16)
        CS = P.sb("CS", [128, 2, 2, 2, 128], BF16)
        S.dma("sp", Dcs[:], Dcs_d, writes=[Dcs])
        S.dma("sp", CS[:], CS_d, writes=[CS])
        ei = 0
        bi = 0
        for cb in range(8):
            cbl = cb % 2
            for q in range(4):
                S.dma("sp" if q % 2 == 0 else "pool", Hc[:, q * 32:(q + 1) * 32, :],
                      hv[:, q * 32:(q + 1) * 32, cb * 128:(cb + 1) * 128], writes=[Hc])
            for cp in range(64):
                bk = banks[bi % 4]
                bi += 1
                for j in range(2):
                    ch = 2 * cp + j
                    S.op("pe", lambda e, bk=bk, j=j, ch=ch: e.matmul(bk[:, j * 256:(j + 1) * 256], lhsT=Hc[:, :, ch], rhs=Dcs[:],
                                                                     start=True, stop=True),
                         reads=[Hc, Dcs], writes=[bk], inc=(j == 1))
                eng = "act" if cp % 2 == 0 else "dve"
                outv = lambda cp=cp: B[:].rearrange("p r k c -> p (r k) c")[:, :, 2 * cp:2 * cp + 2]
                inv = lambda bk=bk: bk[:].rearrange("p (j q) -> p q j", j=2)
                if eng == "act":
                    S.op("act", lambda e, outv=outv, inv=inv: e.copy(out=outv(), in_=inv()), reads=[bk], writes=[B])
                else:
                    S.op("dve", lambda e, outv=outv, inv=inv: e.tensor_copy(out=outv(), in_=inv()), reads=[bk], writes=[B])
            for kg in range(16):
                eg = Eg[ei % 2]
                ei += 1
                S.dma("sp", eg[:], E_d[kg], writes=[eg])
                bk = banks[4 + kg % 2]
                for j in range(8):
                    k2 = 8 * kg + j
                    S.op("pe", lambda e, bk=bk, j=j, k2=k2, eg=eg: e.matmul(bk[:, j * 64:(j + 1) * 64], lhsT=B[:, 0, k2, :],
                                                                           rhs=eg[:, j, 0:64], start=True, stop=False),
                         reads=[B, eg], writes=[bk], inc=False)
                    S.op("pe", lambda e, bk=bk, j=j, k2=k2, eg=eg: e.matmul(bk[:, j * 64:(j + 1) * 64], lhsT=B[:, 1, k2, :],
                                                                           rhs=eg[:, j, 64:128], start=False, stop=True),
                         reads=[B, eg], writes=[bk], inc=(j == 7))
                outv = lambda cbl=cbl, kg=kg: GT[:, cbl, :, :].rearrange("p r (a b) -> p b r a", b=128)[:, 8 * kg:8 * kg + 8, :, :]
                inv = lambda bk=bk: bk[:].rearrange("p (j r a) -> p j r a", j=8, r=2)
                if kg % 2 == 0:
                    S.op("act", lambda e, outv=outv, inv=inv: e.copy(out=outv(), in_=inv()), reads=[bk], writes=[GT])
                else:
                    S.op("dve", lambda e, outv=outv, inv=inv: e.tensor_copy(out=outv(), in_=inv()), reads=[bk], writes=[GT])
            if cbl == 1:
                g = cb // 2
                for mb in range(2):
                    for tg in range(TOK // 512):
                        bk = banks[6]
                        n = 0
                        for nb in range(2):
                            for ri in range(2):
                                S.op("pe", lambda e, nb=nb, ri=ri, mb=mb, tg=tg, n=n: e.matmul(
                                    banks[6][:], lhsT=CS[:, ri, nb, mb, :], rhs=GT[:, nb, ri, tg * 512:(tg + 1) * 512],
                                    start=(n == 0), stop=(n == 3)), reads=[CS, GT], writes=[bk], inc=(n == 3))
                                n += 1
                        if tg % 2 == 0:
                            S.op("act", lambda e, g=g, mb=mb, tg=tg: e.copy(out=FT[:, 2 * g + mb, tg * 512:(tg + 1) * 512], in_=banks[6][:]),
                                 reads=[bk], writes=[FT])
                        else:
                            S.op("dve", lambda e, g=g, mb=mb, tg=tg: e.tensor_copy(out=FT[:, 2 * g + mb, tg * 512:(tg + 1) * 512], in_=banks[6][:]),
                                 reads=[bk], writes=[FT])

    with Scope(P):
        wbf = P.sb("wfbf", [128, 8, D], BF16)
        wst = [P.sb(f"wfst{i}", [128, D], F32) for i in range(2)]
        for k in range(8):
            ws = wst[k % 2]
            S.dma("pool", ws[:], w_f[k * 128:(k + 1) * 128, :], writes=[ws])
            S.op("pool", lambda e, k=k, ws=ws: e.tensor_copy(out=wbf[:, k, :], in_=ws[:]), reads=[ws], writes=[wbf])
        GG1 = P.sb("GG1", [128, D], F32)
        G2 = P.sb("G2", [128, D], F32)
        SH2 = P.sb("SH2", [128, D], F32)
        tmpa = P.sb("tmpa", [128, D], F32)
        tmpb = P.sb("tmpb", [128, D], F32)
        S.dma("sp", tmpa[:], M[:, 2 * D:3 * D], writes=[tmpa])
        S.dma("sp", tmpb[:], ng[:, 1 * D:2 * D], writes=[tmpb])
        S.op("dve", lambda e: e.scalar_tensor_tensor(out=GG1[:], in0=tmpa[:], scalar=1.0 / 2048, in1=tmpb[:], op0=ALU.mult, op1=ALU.mult),
             reads=[tmpa, tmpb], writes=[GG1])
        S.dma("sp", tmpa[:], M[:, 4 * D:5 * D], writes=[tmpa])
        S.dma("sp", tmpb[:], ng[:, 2 * D:3 * D], writes=[tmpb])
        S.op("dve", lambda e: e.scalar_tensor_tensor(out=G2[:], in0=tmpa[:], scalar=1.0, in1=tmpb[:], op0=ALU.add, op1=ALU.mult),
             reads=[tmpa, tmpb], writes=[G2])
        S.dma("sp", SH2[:], M[:, 3 * D:4 * D], writes=[SH2])
        xt = [P.sb(f"xt{i}", [128, D], F32) for i in range(2)]
        ysb = P.sb("ysb", [128, D], F32)
        junk = P.sb("junk", [128, D], F32)
        ms = P.sb("ms", [128, 1], F32)
        rstd = P.sb("rstd", [128, 1], F32)
        t1 = P.sb("t1", [128, D], F32)
        x1 = [P.sb(f"x1_{i}", [128, D], F32) for i in range(2)]
        hb = P.sb("hb", [128, D], BF16)
        hT = P.sb("hT", [128, 8, 128], BF16)
        bY = banks[0:2]
        for t in range(NT):
            xx = xt[t % 2]
            xo = x1[t % 2]
            S.dma("sp", xx[:], x2[t * 128:(t + 1) * 128, :], writes=[xx])
            for n in range(2):
                for h in range(8):
                    S.op("pe", lambda e, n=n, h=h, t=t: e.matmul(bY[n][:], lhsT=FT[:, h, t * 128:(t + 1) * 128],
                                                                 rhs=wbf[:, h, n * 512:(n + 1) * 512], start=(h == 0), stop=(h == 7)),
                         reads=[FT, wbf], writes=[bY[n]], inc=(h == 7))
            S.op("act", lambda e: e.copy(out=ysb[:, 0:512], in_=bY[0][:]), reads=[bY[0]], writes=[ysb])
            S.op("dve", lambda e: e.tensor_copy(out=ysb[:, 512:1024], in_=bY[1][:]), reads=[bY[1]], writes=[ysb])
            S.op("act", lambda e: e.activation(out=junk[:], in_=ysb[:], func=AF.Square, scale=1.0 / (32 * 2048), accum_out=ms[:]),
                 reads=[ysb], writes=[junk, ms])
            rstd_from_ms(S, ms, eps_t, rstd)
            S.op("dve", lambda e: e.scalar_tensor_tensor(out=t1[:], in0=ysb[:], scalar=rstd[:, 0:1], in1=GG1[:],
                                                         op0=ALU.mult, op1=ALU.mult), reads=[ysb, rstd, GG1], writes=[t1])
            S.op("pool", lambda e, xx=xx, xo=xo: e.tensor_tensor(out=xo[:], in0=t1[:], in1=xx[:], op=ALU.add),
                 reads=[t1, xx], writes=[xo])
            S.dma("pool", x3o[t * 128:(t + 1) * 128, :], xo[:], reads=[xo])
            S.op("act", lambda e, xo=xo: e.activation(out=junk[:], in_=xo[:], func=AF.Square, scale=1.0 / 32, accum_out=ms[:]),
                 reads=[xo], writes=[junk, ms])
            rstd_from_ms(S, ms, eps_t, rstd)
            S.op("dve", lambda e, xo=xo: e.scalar_tensor_tensor(out=t1[:], in0=xo[:], scalar=rstd[:, 0:1], in1=G2[:],
                                                                op0=ALU.mult, op1=ALU.mult), reads=[xo, rstd, G2], writes=[t1])
            S.op("pool", lambda e: e.tensor_tensor(out=hb[:], in0=t1[:], in1=SH2[:], op=ALU.add), reads=[t1, SH2], writes=[hb])
            for k in range(8):
                S.op("pe", lambda e, k=k: e.transpose(out=bT[:, k * 128:(k + 1) * 128], in_=hb[:, k * 128:(k + 1) * 128],
                                                      identity=idt[:]), reads=[hb, idt], writes=[bT], inc=(k == 7))
            S.op("act", lambda e: e.copy(out=hT[:].rearrange("p k t -> p (k t)"), in_=bT[:]), reads=[bT], writes=[hT])
            S.dma("pool", h2T[:, :, t * 128:(t + 1) * 128].rearrange("k p t -> p k t"), hT[:], reads=[hT])
    return P.finish()


def dft_tables(s):
    t = np.arange(128)
    ang = 2 * np.pi * np.outer(t, t) / 128.0
    Dcs = np.concatenate([np.cos(ang), np.sin(ang)], axis=1)
    k1 = 32 * s + np.arange(32)
    k2 = np.arange(128)
    t1 = np.arange(128)
    k = 128 * k1[None, None, :] + k2[None, :, None]
    ph = (k.astype(np.int64) * t1[:, None, None].astype(np.int64)) % L
    th = 2 * np.pi * ph / float(L)
    Ec, Es = np.cos(th), np.sin(th)
    E = np.concatenate([Ec, Es, -Es, Ec], axis=2)
    E = E.reshape(128, 16, 8, 128).transpose(1, 0, 2, 3)
    n = np.arange(256)
    a = 2 * np.pi * ((np.outer(n, n)) % 256) / 256.0
    C = np.cos(a).reshape(2, 128, 2, 128)
    Sn = -np.sin(a).reshape(2, 128, 2, 128)
    CS = np.stack([C, Sn], axis=0).transpose(2, 0, 1, 3, 4)
    return (np.ascontiguousarray(Dcs).astype(NPBF), np.ascontiguousarray(E).astype(NPBF),
            np.ascontiguousarray(CS).astype(NPBF))


def launch_fnet(inp, h1_list, x2_list, M_list):
    nc = build_fnet()
    ident = np.eye(128, dtype=np.float32).astype(NPBF)
    ngb = np.ascontiguousarray(np.broadcast_to(inp["norm_g"][1].reshape(1, 4 * D), (128, 4 * D)))
    maps = []
    for core in range(NCORE):
        b, s = divmod(core, 4)
        h_all = np.ascontiguousarray(np.concatenate([h1_list[b * 4 + i] for i in range(4)], axis=0))
        Dcs, E, CS = dft_tables(s)
        maps.append(dict(h_all=h_all, x2=x2_list[core], Dcs=Dcs, E=E, CS=CS, w_f=inp["fourier_w_out"][0],
                         M=np.ascontiguousarray(M_list[core][:, 1, :]), ng=ngb, ident=ident))
    return run(nc, maps)


def _np(res):
    return [{k: np.asarray(v) for k, v in r.items()} for r in res]


def kernel(**inputs):
    inp = {k: np.asarray(v) for k, v in inputs.items()}
    r1 = _np(launch_proj(inp))
    Ms = [r["M"] for r in r1]
    r2 = _np(launch_attn(inp, r1))
    r3 = _np(launch_ffn(inp, 0, [r["h2T"] for r in r2], [r["x1"] for r in r2], Ms, True))
    del r2
    r4 = _np(launch_fnet(inp, [r["h1n"] for r in r3], [r["x2"] for r in r3], Ms))
    del r3
    r5 = _np(launch_ffn(inp, 1, [r["h2T"] for r in r4], [r["x3"] for r in r4], Ms, False))
    out = np.empty((NB, L, D), np.float32)
    for core in range(NCORE):
        b, s = divmod(core, 4)
        out[b, s * TOK:(s + 1) * TOK] = r5[core]["x2"]
    return out
```

```python
import numpy as np
import ml_dtypes
from contextlib import ExitStack
import concourse.bass as bass
import concourse.mybir as mybir
from concourse.bass_utils import run_bass_kernel_spmd

F32 = mybir.dt.float32
BF16 = mybir.dt.bfloat16
AF = mybir.ActivationFunctionType
ALU = mybir.AluOpType
AX = mybir.AxisListType
NPBF = ml_dtypes.bfloat16

D = 1024
L = 16384
NB = 2
NCORE = 8
TOK = 4096
NT = TOK // 128
CTX = 256
DFF = 2816
EPS = 1e-6
SEM_LIMIT = 30000
SELF_SYNC = True
MAX_ATTACH = 1


class Buf:
    __slots__ = ("name", "w", "r", "excl")

    def __init__(self, name="", excl=False):
        self.name = name
        self.w = None
        self.r = []
        self.excl = excl


class Tile:
    def __init__(self, t, buf):
        self.t = t
        self.buf = buf

    def __getitem__(self, k):
        return self.t[k]


class Sched:
    ENGS = ("pe", "act", "dve", "pool", "sp")

    def __init__(self, nc, stack, n_dma_sems=10):
        self.nc = nc
        self.stack = stack
        self.recs = {e: [] for e in self.ENGS}
        self.sems = []
        self.cur_sem = {}
        self.cnt = {}
        self.seen = {e: {} for e in self.ENGS}
        self.pending = {e: [] for e in self.ENGS}
        self.dma_ring = {}
        self.dma_pos = {}
        self.all_dma_events = []
        for e in ("pe", "act", "dve", "pool"):
            self.cur_sem[e] = self._new_sem(f"s_{e}_0")
        for q in ("sp", "act", "pool"):
            self.dma_ring[q] = [self._new_sem(f"d_{q}_{i}") for i in range(n_dma_sems)]
            self.dma_pos[q] = 0

    def _new_sem(self, name):
        h = self.stack.enter_context(self.nc.semaphore(name))
        self.sems.append(h)
        idx = len(self.sems) - 1
        self.cnt[idx] = 0
        return idx

    def _need(self, eng, ev, waits):
        s, v = ev
        if self.seen[eng].get(s, 0) >= v:
            return
        self.seen[eng][s] = v
        waits.append((s, v))

    def _collect(self, eng, reads, writes):
        waits = []
        for b in reads:
            if b.w is not None and (SELF_SYNC or b.w[1] != eng):
                self._need(eng, b.w[0], waits)
            if b.excl:
                for ev, e2 in b.r:
                    if e2 != eng:
                        self._need(eng, ev, waits)
        for b in writes:
            if b.w is not None and (SELF_SYNC or b.w[1] != eng):
                self._need(eng, b.w[0], waits)
            for ev, e2 in b.r:
                if e2 != eng:
                    self._need(eng, ev, waits)
        return waits

    @staticmethod
    def _bufs(lst):
        return [x.buf if isinstance(x, Tile) else x for x in lst]

    def op(self, eng, fn, reads=(), writes=(), inc=True):
        reads = self._bufs(reads)
        writes = self._bufs(writes)
        waits = self._collect(eng, reads, writes)
        self.pending[eng].append((reads, writes))
        if inc:
            s = self.cur_sem[eng]
            if self.cnt[s] >= SEM_LIMIT:
                s = self._new_sem(f"s_{eng}_{len(self.sems)}")
                self.cur_sem[eng] = s
            self.cnt[s] += 1
            ev = (s, self.cnt[s])
            for (rs, ws) in self.pending[eng]:
                for b in rs:
                    b.r.append((ev, eng))
                for b in ws:
                    b.w = (ev, eng)
                    b.r = []
            self.pending[eng] = []
            self.recs[eng].append((waits, fn, (s, 1)))
        else:
            self.recs[eng].append((waits, fn, None))

    def dma(self, q, out, in_, reads=(), writes=()):
        reads = self._bufs(reads)
        writes = self._bufs(writes)
        waits = self._collect(q, reads, writes)
        ring = self.dma_ring[q]
        s = ring[self.dma_pos[q] % len(ring)]
        self.dma_pos[q] += 1
        if self.cnt[s] > 0:
            self._need(q, (s, self.cnt[s]), waits)
        self.cnt[s] += 16
        ev = (s, self.cnt[s])
        for b in reads:
            b.r.append((ev, "dma_" + q))
        for b in writes:
            b.w = (ev, "dma_" + q)
            b.r = []
        self.recs[q].append((waits, (lambda e: e.dma_start(out=out, in_=in_)), (s, 16)))
        self.all_dma_events.append(ev)
        return ev

    def coll(self, kind, groups, src, dst, reads=(), writes=()):
        q = "pool"
        reads = self._bufs(reads)
        writes = self._bufs(writes)
        waits = self._collect(q, reads, writes)
        s = self._new_sem(f"cc_{len(self.sems)}")
        self.cnt[s] += 16
        ev = (s, self.cnt[s])
        for b in reads:
            b.r.append((ev, "cc"))
        for b in writes:
            b.w = (ev, "cc")
            b.r = []
        self.recs[q].append((waits, (lambda e: e.collective_compute(kind, (ALU.add if kind == 'AllReduce' else ALU.bypass), replica_groups=groups,
                                                                    ins=[src], outs=[dst])), (s, 16)))
        self.all_dma_events.append(ev)
        return ev

    def drain_dmas(self, eng="sp"):
        waits = []
        for ev in self.all_dma_events:
            self._need(eng, ev, waits)
        if waits:
            self.recs[eng].append((waits, None, None))
        self.all_dma_events = []

    def flush(self):
        nc = self.nc
        for e in self.ENGS:
            assert not self.pending[e], f"uncovered instrs on {e}"
        recs = self.recs
        sems = self.sems

        def replay(engine, lst):
            for waits, fn, inc in lst:
                nw = len(waits)
                n_sep = nw if fn is None else max(0, nw - MAX_ATTACH)
                for (s, v) in waits[:n_sep]:
                    engine.wait_ge(sems[s], v)
                if fn is not None:
                    ins = fn(engine)
                    for (s, v) in waits[n_sep:]:
                        ins._wait_ge(sems[s], v)
                    if inc is not None:
                        ins.then_inc(sems[inc[0]], inc[1])

        with nc.Block() as block:
            @block.tensor
            def _(e):
                replay(e, recs["pe"])

            @block.scalar
            def _(e):
                replay(e, recs["act"])

            @block.vector
            def _(e):
                replay(e, recs["dve"])

            @block.gpsimd
            def _(e):
                replay(e, recs["pool"])

            @block.sync
            def _(e):
                replay(e, recs["sp"])
        self.recs = {e: [] for e in self.ENGS}


class Prog:
    def __init__(self):
        self.nc = bass.Bass("TRN2", target_bir_lowering=False)
        self.stack = ExitStack()
        self.S = Sched(self.nc, self.stack)
        self.n_psum = 0

    def din(self, name, shape, dt):
        return self.nc.dram_tensor(name, list(shape), dt, kind="ExternalInput").ap()

    def dout(self, name, shape, dt):
        return self.nc.dram_tensor(name, list(shape), dt, kind="ExternalOutput").ap()

    def sb(self, name, shape, dt):
        t = self.stack.enter_context(self.nc.sbuf_tensor("s_" + name, list(shape), dt))
        return Tile(t, Buf(name))

    def bank(self, name, shape, dt, nbanks=1):
        self.n_psum += nbanks
        assert self.n_psum <= 8
        t = self.stack.enter_context(self.nc.psum_tensor("p_" + name, list(shape), dt))
        return Tile(t, Buf(name, excl=True))

    def finish(self):
        self.S.drain_dmas("sp")
        self.S.flush()
        self.stack.close()
        return self.nc


def rstd_from_ms(S, ms, eps_t, out, n=None):
    S.op("act", lambda e: e.activation(out=out[:], in_=ms[:], func=AF.Ln, bias=eps_t[:], scale=1.0),
         reads=[ms, eps_t], writes=[out])
    S.op("act", lambda e: e.activation(out=out[:], in_=out[:], func=AF.Exp, scale=-0.5),
         reads=[out], writes=[out])


def adaln_table(P, cT_ap, modw_ap, modb_ap, col0, ncol, consume):
    S = P.S
    ng = ncol // 512
    c_sb = P.t_c
    S.dma("sp", c_sb[:], cT_ap, writes=[c_sb])
    S.op("act", lambda e: e.activation(out=P.t_sc[:], in_=c_sb[:], func=AF.Silu), reads=[c_sb], writes=[P.t_sc])
    for k in range(8):
        S.op("dve", lambda e, k=k: e.tensor_copy(out=P.t_screp[:, k, :], in_=P.t_sc[:, k:k + 1].to_broadcast([128, 128])),
             reads=[P.t_sc], writes=[P.t_screp])
    S.dma("sp", P.t_mb[:, 0:ncol], modb_ap[:, col0:col0 + ncol], writes=[P.t_mb])
    for k in range(8):
        mw = P.t_mw[k % 2]
        S.dma("sp" if k % 2 == 0 else "pool", mw[:, 0:ncol], modw_ap[k * 128:(k + 1) * 128, col0:col0 + ncol], writes=[mw])
        for g in range(ng):
            S.op("pe", lambda e, k=k, g=g, mw=mw: e.matmul(P.banks[g][:], lhsT=P.t_screp[:, k, :],
                                                             rhs=mw[:, g * 512:(g + 1) * 512], start=(k == 0), stop=False),
                 reads=[P.t_screp, mw], writes=[P.banks[g]], inc=(g == ng - 1))
    for g in range(ng):
        S.op("pe", lambda e, g=g: e.matmul(P.banks[g][:], lhsT=P.t_ones1[:], rhs=P.t_mb[:, g * 512:(g + 1) * 512],
                                           start=False, stop=True),
             reads=[P.t_ones1, P.t_mb], writes=[P.banks[g]])
        S.op("dve" if g % 2 == 0 else "act",
             (lambda e, g=g: e.tensor_copy(out=P.t_stage[:, g * 512:(g + 1) * 512], in_=P.banks[g][:])) if g % 2 == 0 else
             (lambda e, g=g: e.copy(out=P.t_stage[:, g * 512:(g + 1) * 512], in_=P.banks[g][:])),
             reads=[P.banks[g]], writes=[P.t_stage])
    consume(P.t_stage)


def build_proj():
    P = Prog()
    S = P.S
    x = P.din("x", [TOK, D], F32)
    ctx = P.din("ctx", [CTX, D], F32)
    cT = P.din("cT", [128, 8], F32)
    ccT = P.din("ccT", [128, 8], F32)
    mw0a = P.din("mw0a", [D, 2048], F32)
    mb0a = P.din("mb0a", [1, 2048], F32)
    mw0q = P.din("mw0q", [D, 1024], F32)
    mb0q = P.din("mb0q", [1, 1024], F32)
    mw1q = P.din("mw1q", [D, 1536], F32)
    mb1q = P.din("mb1q", [1, 1536], F32)
    ng = P.din("ng", [128, 8 * D], F32)
    w_in = P.din("w_in", [D, 2048], F32)
    qg = P.din("qg", [128, 128], F32)
    kg = P.din("kg", [128, 128], F32)
    rope = P.din("rope", [TOK, 128], F32)
    ident = P.din("ident", [128, 128], BF16)
    QT = P.dout("QT", [8, 128, TOK], BF16)
    KT = P.dout("KT", [4, 128, TOK], BF16)
    V = P.dout("V", [TOK, 512], BF16)
    KTc = P.dout("KTc", [4, 128, CTX], BF16)
    Vc = P.dout("Vc", [CTX, 512], BF16)
    M0a = P.dout("M0a", [128, 2048], F32)
    M0q = P.dout("M0q", [128, 1024], F32)
    M1q = P.dout("M1q", [128, 1536], F32)

    eps_t = P.sb("eps", [128, 1], F32)
    idt = P.sb("ident", [128, 128], BF16)
    G1 = P.sb("G1", [128, D], F32)
    SH1 = P.sb("SH1", [128, D], F32)
    Gc = P.sb("Gc", [128, D], F32)
    SHc = P.sb("SHc", [128, D], F32)
    qg_t = P.sb("qg", [128, 128], F32)
    kg_t = P.sb("kg", [128, 128], F32)
    wbf = P.sb("wbf", [128, 8, 2048], BF16)
    S.op("dve", lambda e: e.memset(eps_t[:], EPS), writes=[eps_t])
    S.dma("pool", idt[:], ident, writes=[idt])
    S.dma("pool", qg_t[:], qg, writes=[qg_t])
    S.dma("pool", kg_t[:], kg, writes=[kg_t])

    with Scope(P):
        P.banks = [P.bank(f"bk{i}", [128, 512], F32) for i in range(4)]
        P.t_c = P.sb("c_sb", [128, 8], F32)
        P.t_sc = P.sb("sc_sb", [128, 8], F32)
        P.t_screp = P.sb("screp", [128, 8, 128], F32)
        P.t_mb = P.sb("mb", [1, 2048], F32)
        P.t_mw = [P.sb(f"mw{i}", [128, 2048], F32) for i in range(2)]
        P.t_ones1 = P.sb("ones1", [1, 128], F32)
        P.t_stage = P.sb("stage", [128, 2048], F32)
        ng0 = P.sb("ng0", [128, D], F32)
        wst = [P.sb(f"wst{i}", [128, 2048], F32) for i in range(2)]
        S.op("dve", lambda e: e.memset(P.t_ones1[:], 1.0), writes=[P.t_ones1])
        S.dma("pool", ng0[:], ng[:, 0:D], writes=[ng0])
        for k in range(8):
            ws = wst[k % 2]
            S.dma("pool", ws[:], w_in[k * 128:(k + 1) * 128, :], writes=[ws])
            S.op("pool", lambda e, k=k, ws=ws: e.tensor_copy(out=wbf[:, k, :], in_=ws[:]), reads=[ws], writes=[wbf])

        def mk_consume(kind):
            def consume(stage):
                if kind == "ctx":
                    S.op("dve", lambda e: e.scalar_tensor_tensor(out=Gc[:], in0=stage[:, D:2 * D], scalar=1.0, in1=ng0[:],
                                                                 op0=ALU.add, op1=ALU.mult), reads=[stage, ng0], writes=[Gc])
                    S.op("dve", lambda e: e.tensor_copy(out=SHc[:], in_=stage[:, 0:D]), reads=[stage], writes=[SHc])
                elif kind == "0a":
                    S.dma("sp", M0a, stage[:, 0:2048], reads=[stage])
                    S.op("dve", lambda e: e.scalar_tensor_tensor(out=G1[:], in0=stage[:, D:2 * D], scalar=1.0, in1=ng0[:],
                                                                 op0=ALU.add, op1=ALU.mult), reads=[stage, ng0], writes=[G1])
                    S.op("dve", lambda e: e.tensor_copy(out=SH1[:], in_=stage[:, 0:D]), reads=[stage], writes=[SH1])
                elif kind == "0q":
                    S.dma("sp", M0q, stage[:, 0:1024], reads=[stage])
                else:
                    S.dma("sp", M1q, stage[:, 0:1536], reads=[stage])
            return consume

        adaln_table(P, ccT, mw0a, mb0a, 0, 2048, mk_consume("ctx"))
        adaln_table(P, cT, mw0a, mb0a, 0, 2048, mk_consume("0a"))
        adaln_table(P, cT, mw0q, mb0q, 0, 1024, mk_consume("0q"))
        adaln_table(P, cT, mw1q, mb1q, 0, 1536, mk_consume("1q"))

    R = 2
    def ring(name, shape, dt, n=R):
        return [P.sb(f"{name}{i}", shape, dt) for i in range(n)]
    xt = ring("xt", [128, D], F32)
    cs = ring("cs", [128, 128], F32, n=4)
    junk = ring("junk", [128, D], BF16)
    ms = ring("ms", [128, 1], F32)
    rstd = ring("rstd", [128, 1], F32)
    t1 = ring("t1", [128, D], F32)
    hb = ring("hb", [128, D], BF16)
    hT = ring("hT", [128, 8, 128], BF16)
    pr = ring("pr", [128, 2048], F32)
    sq = ring("sq", [128, 768], F32)
    ssq = ring("ssq", [128, 6], F32)
    rq = ring("rq", [128, 6], F32)
    rb = ring("rb", [128, 2048], BF16)
    ta = ring("ta", [128, 512], F32)
    tb = ring("tb", [128, 512], F32)
    tcc = ring("tcc", [128, 512], F32)
    td = ring("td", [128, 512], F32)
    qT = ring("qT", [128, 12, 128], BF16)
    vt = ring("vt", [128, 512], BF16)
    bTs = [P.bank(f"bT{i}", [128, 1024], BF16) for i in range(2)]
    bUs = [P.bank(f"bU{i}", [128, 1024], BF16) for i in range(2)]
    bP = [P.bank(f"bP{i}", [128, 512], F32) for i in range(4)]

    def do_tile(i, src_ap, rope_ap, Gt, SHt, is_ctx, tcol):
        r = i % R
        xx, csl, jk, ms_, rs_, t1_, hb_, hT_, pr_, sq_, ssq_, rq_, rb_ = (xt[r], cs[i % 4], junk[r], ms[r], rstd[r], t1[r], hb[r], hT[r],
                                                                           pr[r], sq[r], ssq[r], rq[r], rb[r])
        ta_, tb_, tc_, td_, qT_, vt_, bT, bU = ta[r], tb[r], tcc[r], td[r], qT[r], vt[r], bTs[r], bUs[r]
        S.dma("sp", xx[:], src_ap, writes=[xx])
        if not is_ctx:
            S.dma("sp", csl[:], rope_ap, writes=[csl])
        S.op("act", lambda e: e.activation(out=jk[:], in_=xx[:], func=AF.Square, scale=1.0 / 32, accum_out=ms_[:]),
             reads=[xx], writes=[jk, ms_])
        rstd_from_ms(S, ms_, eps_t, rs_)
        S.op("dve", lambda e: e.scalar_tensor_tensor(out=t1_[:], in0=xx[:], scalar=rs_[:, 0:1], in1=Gt[:],
                                                     op0=ALU.mult, op1=ALU.mult), reads=[xx, rs_, Gt], writes=[t1_])
        S.op("pool", lambda e: e.tensor_tensor(out=hb_[:], in0=t1_[:], in1=SHt[:], op=ALU.add),
             reads=[t1_, SHt], writes=[hb_])
        yield
        for k in range(8):
            S.op("pe", lambda e, k=k: e.transpose(out=bT[:, k * 128:(k + 1) * 128], in_=hb_[:, k * 128:(k + 1) * 128],
                                                  identity=idt[:]), reads=[hb_, idt], writes=[bT], inc=(k == 7))
        S.op("act", lambda e: e.copy(out=hT_[:].rearrange("p k t -> p (k t)"), in_=bT[:]), reads=[bT], writes=[hT_])
        g0 = 2 if is_ctx else 0
        for g in range(g0, 4):
            for k in range(8):
                S.op("pe", lambda e, g=g, k=k: e.matmul(bP[g][:], lhsT=hT_[:, k, :], rhs=wbf[:, k, g * 512:(g + 1) * 512],
                                                        start=(k == 0), stop=(k == 7)),
                     reads=[hT_, wbf], writes=[bP[g]], inc=(k == 7))
        for g in range(g0, 4):
            if g % 2 == 0:
                S.op("dve", lambda e, g=g: e.tensor_copy(out=pr_[:, g * 512:(g + 1) * 512], in_=bP[g][:]),
                     reads=[bP[g]], writes=[pr_])
            else:
                S.op("act", lambda e, g=g: e.copy(out=pr_[:, g * 512:(g + 1) * 512], in_=bP[g][:]),
                     reads=[bP[g]], writes=[pr_])
        yield
        S.op("pool", lambda e: e.tensor_copy(out=vt_[:, 0:256], in_=pr_[:, 1280:1536]), reads=[pr_], writes=[vt_])
        S.op("pool", lambda e: e.tensor_copy(out=vt_[:, 256:512], in_=pr_[:, 1792:2048]), reads=[pr_], writes=[vt_])
        S.dma("sp", (Vc if is_ctx else V)[tcol:tcol + 128, :], vt_[:], reads=[vt_])
        if not is_ctx:
            S.op("dve", lambda e: e.tensor_tensor(out=sq_[:, 0:512], in0=pr_[:, 0:512], in1=pr_[:, 0:512], op=ALU.mult),
                 reads=[pr_], writes=[sq_])
        else:
            S.op("dve", lambda e: e.memset(sq_[:, 0:512], 1.0), writes=[sq_])
        S.op("dve", lambda e: e.tensor_tensor(out=sq_[:, 512:768], in0=pr_[:, 1024:1280], in1=pr_[:, 1024:1280], op=ALU.mult),
             reads=[pr_], writes=[sq_])
        S.op("dve", lambda e: e.tensor_reduce(out=ssq_[:], in_=sq_[:].rearrange("p (h d) -> p h d", d=128), axis=AX.X, op=ALU.add),
             reads=[sq_], writes=[ssq_])
        S.op("act", lambda e: e.activation(out=rq_[:], in_=ssq_[:], func=AF.Ln, bias=eps_t[:], scale=1.0 / 128),
             reads=[ssq_, eps_t], writes=[rq_])
        S.op("act", lambda e: e.activation(out=rq_[:], in_=rq_[:], func=AF.Exp, scale=-0.5), reads=[rq_], writes=[rq_])
        groups = ((1024, 2, 4, kg_t),) if is_ctx else ((0, 4, 0, qg_t), (1024, 2, 4, kg_t))
        for (c0, nh, r0, gt) in groups:
            v3 = lambda c0=c0, nh=nh: pr_[:, c0:c0 + nh * 128].rearrange("p (h d) -> p h d", d=128)
            S.op("dve", lambda e, v3=v3, nh=nh, r0=r0: e.tensor_tensor(
                out=v3(), in0=v3(), in1=rq_[:, r0:r0 + nh].unsqueeze(2).to_broadcast([128, nh, 128]), op=ALU.mult),
                 reads=[pr_, rq_], writes=[pr_])
            S.op("dve", lambda e, v3=v3, nh=nh, gt=gt: e.tensor_tensor(
                out=v3(), in0=v3(), in1=gt[:].unsqueeze(1).to_broadcast([128, nh, 128]), op=ALU.mult),
                 reads=[pr_, gt], writes=[pr_])
        if is_ctx:
            S.op("pool", lambda e: e.tensor_copy(out=rb_[:, 1024:2048], in_=pr_[:, 1024:2048]), reads=[pr_], writes=[rb_])
        else:
            for (c0, nh, eng) in ((0, 4, "dve"), (512, 4, "pool"), (1024, 2, "dve"), (1536, 2, "pool")):
                def v5(t, c0=c0, nh=nh):
                    return t[:, c0:c0 + nh * 128].rearrange("p (h a s f) -> p h a s f", h=nh, a=2, s=2)
                def v4(t, nh=nh):
                    return t[:, 0:nh * 64].rearrange("p (h a f) -> p h a f", h=nh, a=2)
                def cosb(nh=nh):
                    return csl[:, 0:64].rearrange("p (a f) -> p a f", a=2).unsqueeze(1).to_broadcast([128, nh, 2, 32])
                def sinb(nh=nh):
                    return csl[:, 64:128].rearrange("p (a f) -> p a f", a=2).unsqueeze(1).to_broadcast([128, nh, 2, 32])
                S.op(eng, lambda e, v5=v5, v4=v4, cosb=cosb: e.tensor_tensor(out=v4(ta_), in0=v5(pr_)[:, :, :, 0, :], in1=cosb(), op=ALU.mult),
                     reads=[pr_, csl], writes=[ta_])
                S.op(eng, lambda e, v5=v5, v4=v4, sinb=sinb: e.tensor_tensor(out=v4(tb_), in0=v5(pr_)[:, :, :, 1, :], in1=sinb(), op=ALU.mult),
                     reads=[pr_, csl], writes=[tb_])
                S.op(eng, lambda e, v5=v5, v4=v4: e.tensor_tensor(out=v5(rb_)[:, :, :, 0, :], in0=v4(ta_), in1=v4(tb_), op=ALU.subtract),
                     reads=[ta_, tb_], writes=[rb_])
                S.op(eng, lambda e, v5=v5, v4=v4, cosb=cosb: e.tensor_tensor(out=v4(tc_), in0=v5(pr_)[:, :, :, 1, :], in1=cosb(), op=ALU.mult),
                     reads=[pr_, csl], writes=[tc_])
                S.op(eng, lambda e, v5=v5, v4=v4, sinb=sinb: e.tensor_tensor(out=v4(td_), in0=v5(pr_)[:, :, :, 0, :], in1=sinb(), op=ALU.mult),
                     reads=[pr_, csl], writes=[td_])
                S.op(eng, lambda e, v5=v5, v4=v4: e.tensor_tensor(out=v5(rb_)[:, :, :, 1, :], in0=v4(tc_), in1=v4(td_), op=ALU.add),
                     reads=[tc_, td_], writes=[rb_])
        yield
        srcs = [j * 128 for j in range(8)] + [1024, 1152, 1536, 1664]
        first = 8 if is_ctx else 0
        for j in range(first, 12):
            bank = bU if j < 8 else bT
            jj = j if j < 8 else j - 8
            S.op("pe", lambda e, j=j, bank=bank, jj=jj: e.transpose(out=bank[:, jj * 128:(jj + 1) * 128],
                                                                    in_=rb_[:, srcs[j]:srcs[j] + 128], identity=idt[:]),
                 reads=[rb_, idt], writes=[bank], inc=(j == 7 or j == 11))
        if not is_ctx:
            S.op("act", lambda e: e.copy(out=qT_[:, 0:8, :].rearrange("p j t -> p (j t)"), in_=bU[:]), reads=[bU], writes=[qT_])
        S.op("dve", lambda e: e.tensor_copy(out=qT_[:, 8:12, :].rearrange("p j t -> p (j t)"), in_=bT[:, 0:512]),
             reads=[bT], writes=[qT_])
        if is_ctx:
            S.dma("sp", KTc[:, :, tcol:tcol + 128].rearrange("j d t -> d j t"), qT_[:, 8:12, :], reads=[qT_])
        else:
            S.dma("sp", QT[:, :, tcol:tcol + 128].rearrange("j d t -> d j t"), qT_[:, 0:8, :], reads=[qT_])
            S.dma("sp", KT[:, :, tcol:tcol + 128].rearrange("j d t -> d j t"), qT_[:, 8:12, :], reads=[qT_])
        yield

    gens = [do_tile(t, ctx[t * 128:(t + 1) * 128, :], None, Gc, SHc, True, t * 128) for t in range(CTX // 128)]
    gens += [do_tile(2 + t, x[t * 128:(t + 1) * 128, :], rope[t * 128:(t + 1) * 128, :], G1, SH1, False, t * 128) for t in range(NT)]
    pipeline(gens)
    return P.finish()


def rope_table():
    t = np.arange(L)
    row = (t // 64).astype(np.float32)
    col = (t % 64).astype(np.float32)
    inv = (10000.0 ** (-np.arange(32, dtype=np.float32) / 32)).astype(np.float32)
    ang = np.stack([row[:, None] * inv, col[:, None] * inv], axis=1)
    return np.concatenate([np.cos(ang).reshape(L, 64), np.sin(ang).reshape(L, 64)], axis=1).astype(np.float32)


def run(nc, in_maps):
    res = run_bass_kernel_spmd(nc, in_maps, core_ids=list(range(NCORE)))
    return res.results


def launch_proj(inp):
    nc = build_proj()
    rope = rope_table()
    ident = np.eye(128, dtype=np.float32).astype(NPBF)
    ngb = np.ascontiguousarray(np.broadcast_to(inp["norm_g"].reshape(1, 8 * D), (128, 8 * D)))
    qgb = np.ascontiguousarray(np.broadcast_to(inp["q_norm_g"].reshape(1, 128), (128, 128)))
    kgb = np.ascontiguousarray(np.broadcast_to(inp["k_norm_g"].reshape(1, 128), (128, 128)))
    ccT = np.ascontiguousarray(inp["c_ctx"].reshape(8, 128).T)
    mw, mb = inp["mod_w"], inp["mod_b"]
    mw0a = np.ascontiguousarray(mw[0][:, 0:2048])
    mb0a = np.ascontiguousarray(mb[0][None, 0:2048])
    maps = []
    for core in range(NCORE):
        b, s = divmod(core, 4)
        maps.append(dict(
            x=np.ascontiguousarray(inp["x"][b, s * TOK:(s + 1) * TOK]),
            ctx=np.ascontiguousarray(inp["ctx"][b]),
            cT=np.ascontiguousarray(inp["c"][b].reshape(8, 128).T),
            ccT=ccT, mw0a=mw0a, mb0a=mb0a,
            mw0q=np.ascontiguousarray(mw[0][:, 2048 + 1024 * s:2048 + 1024 * (s + 1)]),
            mb0q=np.ascontiguousarray(mb[0][None, 2048 + 1024 * s:2048 + 1024 * (s + 1)]),
            mw1q=np.ascontiguousarray(mw[1][:, 1536 * s:1536 * (s + 1)]),
            mb1q=np.ascontiguousarray(mb[1][None, 1536 * s:1536 * (s + 1)]),
            ng=ngb, w_in=inp["attn_w_in"][0], qg=qgb, kg=kgb,
            rope=np.ascontiguousarray(rope[s * TOK:(s + 1) * TOK]), ident=ident))
    res = _np(run(nc, maps))
    for b in range(NB):
        grp = res[b * 4:(b + 1) * 4]
        M0 = np.concatenate([grp[0]["M0a"]] + [g["M0q"] for g in grp], axis=1)
        M1 = np.concatenate([g["M1q"] for g in grp], axis=1)
        Mfull = np.ascontiguousarray(np.stack([M0, M1], axis=1))
        for g in grp:
            g["M"] = Mfull
    return res


def sched_barrier(S):
    evs = []
    for e in ("pe", "act", "dve", "pool"):
        s = S.cur_sem[e]
        if S.cnt[s] > 0:
            evs.append((s, S.cnt[s]))
    for q in ("sp", "act", "pool"):
        for s in S.dma_ring[q]:
            if S.cnt[s] > 0:
                evs.append((s, S.cnt[s]))
    for e in S.ENGS:
        waits = []
        for ev in evs:
            S._need(e, ev, waits)
        if waits:
            S.recs[e].append((waits, None, None))


class Scope:
    def __init__(self, P):
        self.P = P

    def __enter__(self):
        self.saved = self.P.stack
        self.saved_npsum = self.P.n_psum
        self.P.stack = ExitStack()
        return self

    def __exit__(self, *a):
        sched_barrier(self.P.S)
        self.P.S.flush()
        self.P.stack.close()
        self.P.stack = self.saved
        self.P.n_psum = self.saved_npsum
        return False


SCALE = 128 ** -0.5
NKB = (CTX + L) // 128
NLB = 2 + NT + 2


def resid_tile(P, t, x_in, x_out, hT_out, lhsT_of, lhs_tile, wbf, GG1, G2, SH2, eps_t, idt,
               xx, xo, ysb, junk, ms, rstd, t1, hb, hT, bY, bT, sq_scale, t2=None):
    t2 = t2 if t2 is not None else t1
    S = P.S
    S.dma("sp", xx[:], x_in[t * 128:(t + 1) * 128, :], writes=[xx])
    for n in range(2):
        for h in range(8):
            S.op("pe", lambda e, n=n, h=h: e.matmul(bY[n][:], lhsT=lhsT_of(h), rhs=wbf[:, h, n * 512:(n + 1) * 512],
                                                    start=(h == 0), stop=(h == 7)),
                 reads=[lhs_tile, wbf], writes=[bY[n]], inc=(h == 7))
    S.op("act", lambda e: e.copy(out=ysb[:, 0:512], in_=bY[0][:]), reads=[bY[0]], writes=[ysb])
    S.op("dve", lambda e: e.tensor_copy(out=ysb[:, 512:1024], in_=bY[1][:]), reads=[bY[1]], writes=[ysb])
    S.op("act", lambda e: e.activation(out=junk[:], in_=ysb[:], func=AF.Square, scale=sq_scale, accum_out=ms[:, 0:1]),
         reads=[ysb], writes=[junk, ms])
    S.op("act", lambda e: e.activation(out=rstd[:, 0:1], in_=ms[:, 0:1], func=AF.Ln, bias=eps_t[:], scale=1.0),
         reads=[ms, eps_t], writes=[rstd])
    S.op("act", lambda e: e.activation(out=rstd[:, 0:1], in_=rstd[:, 0:1], func=AF.Exp, scale=-0.5), reads=[rstd], writes=[rstd])
    yield
    S.op("dve", lambda e: e.scalar_tensor_tensor(out=t1[:], in0=ysb[:], scalar=rstd[:, 0:1], in1=GG1[:],
                                                 op0=ALU.mult, op1=ALU.mult), reads=[ysb, rstd, GG1], writes=[t1])
    S.op("pool", lambda e: e.tensor_tensor(out=xo[:], in0=t1[:], in1=xx[:], op=ALU.add), reads=[t1, xx], writes=[xo])
    S.dma("sp", x_out[t * 128:(t + 1) * 128, :], xo[:], reads=[xo])
    S.op("act", lambda e: e.activation(out=junk[:], in_=xo[:], func=AF.Square, scale=1.0 / 32, accum_out=ms[:, 1:2]),
         reads=[xo], writes=[junk, ms])
    S.op("act", lambda e: e.activation(out=rstd[:, 1:2], in_=ms[:, 1:2], func=AF.Ln, bias=eps_t[:], scale=1.0),
         reads=[ms, eps_t], writes=[rstd])
    S.op("act", lambda e: e.activation(out=rstd[:, 1:2], in_=rstd[:, 1:2], func=AF.Exp, scale=-0.5), reads=[rstd], writes=[rstd])
    yield
    S.op("dve", lambda e: e.scalar_tensor_tensor(out=t2[:], in0=xo[:], scalar=rstd[:, 1:2], in1=G2[:],
                                                 op0=ALU.mult, op1=ALU.mult), reads=[xo, rstd, G2], writes=[t2])
    S.op("pool", lambda e: e.tensor_tensor(out=hb[:], in0=t2[:], in1=SH2[:], op=ALU.add), reads=[t2, SH2], writes=[hb])
    yield
    for k in range(8):
        S.op("pe", lambda e, k=k: e.transpose(out=bT[:, k * 128:(k + 1) * 128], in_=hb[:, k * 128:(k + 1) * 128],
                                              identity=idt[:]), reads=[hb, idt], writes=[bT], inc=(k == 7))
    S.op("act", lambda e: e.copy(out=hT[:].rearrange("p k t -> p (k t)"), in_=bT[:]), reads=[bT], writes=[hT])
    S.dma("sp", hT_out[:, :, t * 128:(t + 1) * 128].rearrange("k p t -> p k t"), hT[:], reads=[hT])
    yield


def pipeline(gens):
    live = []
    for g in gens:
        live.insert(0, g)
        keep = []
        for gg in live:
            try:
                next(gg)
                keep.append(gg)
            except StopIteration:
                pass
        live = keep
    while live:
        keep = []
        for gg in live:
            try:
                next(gg)
                keep.append(gg)
            except StopIteration:
                pass
        live = keep


def build_attn():
    P = Prog()
    S = P.S
    QT = P.din("QT", [8, 128, TOK], BF16)
    KaT = P.din("KaT", [2, 128, NKB * 128], BF16)
    Va = P.din("Va", [2, 128, NKB, 128], BF16)
    KbT = P.din("KbT", [2, 128, NLB * 128], BF16)
    Vb = P.din("Vb", [2, 128, NLB, 128], BF16)
    bm = P.din("bm", [4, 128, 256], BF16)
    x = P.din("x", [TOK, D], F32)
    M = P.din("M", [128, 6 * D], F32)
    ng = P.din("ng", [128, 8 * D], F32)
    w_out = P.din("w_out", [D, D], F32)
    sink = P.din("sink", [128, 4], F32)
    ident = P.din("ident", [128, 128], BF16)
    x1o = P.dout("x1", [TOK, D], F32)
    h2T = P.dout("h2T", [8, 128, TOK], BF16)

    OT = P.sb("OT", [128, 8, TOK], BF16)
    ones = P.sb("ones", [128, 128], BF16)
    onesf = P.sb("onesf", [128, 128], F32)
    idt = P.sb("ident", [128, 128], BF16)
    eps_t = P.sb("eps", [128, 1], F32)
    S.op("dve", lambda e: e.memset(ones[:], 1.0), writes=[ones])
    S.op("dve", lambda e: e.memset(onesf[:], 1.0), writes=[onesf])
    S.op("dve", lambda e: e.memset(eps_t[:], EPS), writes=[eps_t])
    S.dma("pool", idt[:], ident, writes=[idt])

    with Scope(P):
        bS = [P.bank(f"bS{i}", [128, 1024], F32, nbanks=2) for i in range(2)]
        bO = [P.bank(f"bO{i}", [128, 512], F32) for i in range(2)]
        bD = [P.bank(f"bD{i}", [128, 512], F32) for i in range(2)]
        KT = P.sb("KT", [128, NKB * 128], BF16)
        Vt = P.sb("Vt", [128, NKB, 128], BF16)
        Qg = [P.sb(f"Qg{i}", [128, 2, 256], BF16) for i in range(2)]
        pt = [P.sb(f"pt{i}", [128, 1024], BF16) for i in range(3)]
        accd = [P.sb(f"accd{i}", [128, 512], F32) for i in range(2)]
        accp = [P.sb(f"accp{i}", [128, 512], F32) for i in range(2)]
        rden = P.sb("rden", [128, 512], F32)
        NPAIR = NKB // 2
        gi = 0
        for kv in range(2):
            for c in range(5):
                S.dma("sp", KT[:, c * 3328:(c + 1) * 3328], KaT[kv, :, c * 3328:(c + 1) * 3328], writes=[KT])
                S.dma("pool", Vt[:, c * 26:(c + 1) * 26, :], Va[kv, :, c * 26:(c + 1) * 26, :], writes=[Vt])
            for g in range(TOK // 256):
                q = Qg[gi % 2]
                o_b = bO[gi % 2]
                d_b = bD[gi % 2]
                a_d = accd[gi % 2]
                a_p = accp[gi % 2]
                gi += 1
                S.dma("sp", q[:], QT[2 * kv:2 * kv + 2, :, g * 256:(g + 1) * 256].rearrange("h d t -> d h t"), writes=[q])
                qf = q[:].rearrange("p h t -> p (h t)")

                def smm(jp, qf=qf, q=q):
                    sb_ = bS[jp % 2]
                    for u in range(2):
                        j = 2 * jp + u
                        S.op("pe", lambda e, j=j, u=u, sb_=sb_: e.matmul(sb_[:, u * 512:(u + 1) * 512], lhsT=KT[:, j * 128:(j + 1) * 128],
                                                                         rhs=qf, start=True, stop=True),
                             reads=[KT, q], writes=[sb_], inc=(u == 1))
                smm(0)
                nd = 0
                npl = 0
                npe = 0
                for jp in range(NPAIR):
                    if jp + 1 < NPAIR:
                        smm(jp + 1)
                    p = pt[jp % 3]
                    sb_ = bS[jp % 2]
                    S.op("act", lambda e, p=p, sb_=sb_: e.activation(out=p[:], in_=sb_[:], func=AF.Exp, scale=SCALE),
                         reads=[sb_], writes=[p])
                    for u in range(2):
                        j = 2 * jp + u
                        S.op("pe", lambda e, j=j, u=u, p=p, o_b=o_b: e.matmul(o_b[:], lhsT=Vt[:, j, :], rhs=p[:, u * 512:(u + 1) * 512],
                                                                             start=(j == 0), stop=(j == NKB - 1)),
                             reads=[Vt, p], writes=[o_b], inc=(u == 1))
                        r6 = j % 6
                        if r6 == 5:
                            S.op("pe", lambda e, u=u, p=p, d_b=d_b, npe=npe: e.matmul(d_b[:], lhsT=ones[:], rhs=p[:, u * 512:(u + 1) * 512],
                                                                                    start=(npe == 0), stop=False),
                                 reads=[ones, p], writes=[d_b])
                            npe += 1
                        elif r6 in (0, 2, 4):
                            if nd == 0:
                                S.op("dve", lambda e, u=u, p=p, a_d=a_d: e.tensor_copy(out=a_d[:], in_=p[:, u * 512:(u + 1) * 512]),
                                     reads=[p], writes=[a_d])
                            else:
                                S.op("dve", lambda e, u=u, p=p, a_d=a_d: e.tensor_tensor(out=a_d[:], in0=a_d[:], in1=p[:, u * 512:(u + 1) * 512],
                                                                                         op=ALU.add), reads=[p, a_d], writes=[a_d])
                            nd += 1
                        else:
                            if npl == 0:
                                S.op("pool", lambda e, u=u, p=p, a_p=a_p: e.tensor_copy(out=a_p[:], in_=p[:, u * 512:(u + 1) * 512]),
                                     reads=[p], writes=[a_p])
                            else:
                                S.op("pool", lambda e, u=u, p=p, a_p=a_p: e.tensor_tensor(out=a_p[:], in0=a_p[:], in1=p[:, u * 512:(u + 1) * 512],
                                                                                          op=ALU.add), reads=[p, a_p], writes=[a_p])
                            npl += 1
                S.op("pe", lambda e, d_b=d_b, a_d=a_d: e.matmul(d_b[:], lhsT=onesf[:], rhs=a_d[:], start=False, stop=False),
                     reads=[onesf, a_d], writes=[d_b], inc=False)
                S.op("pe", lambda e, d_b=d_b, a_p=a_p: e.matmul(d_b[:], lhsT=onesf[:], rhs=a_p[:], start=False, stop=True),
                     reads=[onesf, a_p], writes=[d_b])
                S.op("dve", lambda e, d_b=d_b: e.reciprocal(out=rden[:], in_=d_b[:]), reads=[d_b], writes=[rden])
                for h in range(2):
                    S.op("dve", lambda e, h=h, o_b=o_b, kv=kv, g=g: e.tensor_tensor(
                        out=OT[:, 2 * kv + h, g * 256:(g + 1) * 256], in0=o_b[:, h * 256:(h + 1) * 256],
                        in1=rden[:, h * 256:(h + 1) * 256], op=ALU.mult), reads=[o_b, rden], writes=[OT])

    with Scope(P):
        KT = P.sb("KTb", [128, NLB * 128], BF16)
        Vt = P.sb("Vtb", [128, NLB, 128], BF16)
        QB = P.sb("QB", [128, 2, TOK], BF16)
        mk = P.sb("mk", [128, 4, 256], BF16)
        sk = P.sb("sk", [128, 4], F32)
        ske = P.sb("ske", [128, 4], F32)
        pt = [P.sb(f"ptb{i}", [128, 256], BF16) for i in range(3)]
        rden = P.sb("rdenb", [128, 256], F32)
        S.dma("sp", mk[:], bm.rearrange("m p c -> p m c"), writes=[mk])
        S.dma("sp", sk[:], sink, writes=[sk])
        S.op("act", lambda e: e.activation(out=ske[:], in_=sk[:], func=AF.Exp), reads=[sk], writes=[ske])
        banks = [P.bank(f"bkB{i}", [128, 512], F32) for i in range(6)]
        bS = banks[0:2]
        bO = banks[2:4]
        bD = banks[4:6]
        gi = 0
        for kv in range(2):
            S.dma("sp", KT[:], KbT[kv], writes=[KT])
            S.dma("pool", Vt[:], Vb[kv], writes=[Vt])
            S.dma("sp", QB[:], QT[4 + 2 * kv:6 + 2 * kv].rearrange("h d t -> d h t"), writes=[QB])
            for n in range(NT):
                o_b = bO[gi % 2]
                d_b = bD[gi % 2]
                gi += 1
                kblocks = [0, 1, n + 2, n + 3, n + 4]
                for i, kb in enumerate(kblocks):
                    sb_ = bS[i % 2]
                    mask = None
                    if i == 2:
                        mask = 2 if n == 0 else 0
                    if i == 4:
                        mask = 3 if n == NT - 1 else 1
                    S.op("pe", lambda e, kb=kb, n=n, sb_=sb_, mask=mask: e.matmul(
                        sb_[:, 0:256].rearrange("p (h t) -> p h t", h=2), lhsT=KT[:, kb * 128:(kb + 1) * 128],
                        rhs=QB[:, :, n * 128:(n + 1) * 128], start=True, stop=(mask is None)),
                        reads=[KT, QB], writes=[sb_], inc=(mask is None))
                    if mask is not None:
                        S.op("pe", lambda e, sb_=sb_, mask=mask: e.matmul(sb_[:, 0:256], lhsT=idt[:], rhs=mk[:, mask, :],
                                                                          start=False, stop=True),
                             reads=[idt, mk], writes=[sb_])
                    p = pt[i % 3]
                    S.op("act", lambda e, sb_=sb_, p=p: e.activation(out=p[:], in_=sb_[:, 0:256], func=AF.Exp, scale=SCALE),
                         reads=[sb_], writes=[p])
                    S.op("pe", lambda e, kb=kb, p=p, o_b=o_b, i=i: e.matmul(o_b[:, 0:256], lhsT=Vt[:, kb, :], rhs=p[:],
                                                                           start=(i == 0), stop=(i == 4)),
                         reads=[Vt, p], writes=[o_b], inc=False)
                    S.op("pe", lambda e, p=p, d_b=d_b, i=i: e.matmul(d_b[:, 0:256], lhsT=ones[:], rhs=p[:],
                                                                    start=(i == 0), stop=(i == 4)),
                         reads=[ones, p], writes=[d_b])
                for h in range(2):
                    S.op("dve", lambda e, h=h, d_b=d_b, kv=kv: e.tensor_scalar(
                        out=rden[:, h * 128:(h + 1) * 128], in0=d_b[:, h * 128:(h + 1) * 128],
                        scalar1=ske[:, 2 * kv + h:2 * kv + h + 1], scalar2=None, op0=ALU.add), reads=[d_b, ske], writes=[rden])
                S.op("dve", lambda e: e.reciprocal(out=rden[:], in_=rden[:]), reads=[rden], writes=[rden])
                for h in range(2):
                    S.op("dve", lambda e, h=h, o_b=o_b, kv=kv, n=n: e.tensor_tensor(
                        out=OT[:, 4 + 2 * kv + h, n * 128:(n + 1) * 128], in0=o_b[:, h * 128:(h + 1) * 128],
                        in1=rden[:, h * 128:(h + 1) * 128], op=ALU.mult), reads=[o_b, rden], writes=[OT])

    with Scope(P):
        wbf = P.sb("wobf", [128, 8, D], BF16)
        wst = [P.sb(f"wost{i}", [128, D], F32) for i in range(2)]
        for k in range(8):
            ws = wst[k % 2]
            S.dma("pool", ws[:], w_out[k * 128:(k + 1) * 128, :], writes=[ws])
            S.op("pool", lambda e, k=k, ws=ws: e.tensor_copy(out=wbf[:, k, :], in_=ws[:]), reads=[ws], writes=[wbf])
        GG1 = P.sb("GG1", [128, D], F32)
        G2 = P.sb("G2", [128, D], F32)
        SH2 = P.sb("SH2", [128, D], F32)
        tmpa = P.sb("tmpa", [128, D], F32)
        tmpb = P.sb("tmpb", [128, D], F32)
        S.dma("sp", tmpa[:], M[:, 2 * D:3 * D], writes=[tmpa])
        S.dma("sp", tmpb[:], ng[:, 1 * D:2 * D], writes=[tmpb])
        S.op("dve", lambda e: e.tensor_tensor(out=GG1[:], in0=tmpa[:], in1=tmpb[:], op=ALU.mult), reads=[tmpa, tmpb], writes=[GG1])
        S.dma("sp", tmpa[:], M[:, 4 * D:5 * D], reads=[], writes=[tmpa])
        S.dma("sp", tmpb[:], ng[:, 2 * D:3 * D], writes=[tmpb])
        S.op("dve", lambda e: e.scalar_tensor_tensor(out=G2[:], in0=tmpa[:], scalar=1.0, in1=tmpb[:], op0=ALU.add, op1=ALU.mult),
             reads=[tmpa, tmpb], writes=[G2])
        S.dma("sp", SH2[:], M[:, 3 * D:4 * D], writes=[SH2])
        RD = 3
        xt = [P.sb(f"xt{i}", [128, D], F32) for i in range(RD)]
        ysb_ = [P.sb(f"ysb{i}", [128, D], F32) for i in range(RD)]
        junk_ = [P.sb(f"junk{i}", [128, D], BF16) for i in range(RD)]
        ms_ = [P.sb(f"ms{i}", [128, 2], F32) for i in range(RD)]
        rstd_ = [P.sb(f"rstd{i}", [128, 2], F32) for i in range(RD)]
        t1_ = [P.sb(f"t1{i}", [128, D], F32) for i in range(RD)]
        t2_ = [P.sb(f"t2{i}", [128, D], F32) for i in range(RD)]
        x1 = [P.sb(f"x1_{i}", [128, D], F32) for i in range(RD)]
        hb_ = [P.sb(f"hb{i}", [128, D], BF16) for i in range(RD)]
        hT_ = [P.sb(f"hT{i}", [128, 8, 128], BF16) for i in range(RD)]
        bYs = [[P.bank(f"bY{i}_{n}", [128, 512], F32) for n in range(2)] for i in range(2)]
        bTs = [P.bank(f"bT{i}", [128, 1024], BF16) for i in range(2)]
        pipeline(resid_tile(P, t, x, x1o, h2T, lambda h, t=t: OT[:, h, t * 128:(t + 1) * 128], OT, wbf, GG1, G2, SH2, eps_t, idt,
                            xt[t % RD], x1[t % RD], ysb_[t % RD], junk_[t % RD], ms_[t % RD], rstd_[t % RD], t1_[t % RD], hb_[t % RD], hT_[t % RD],
                            bYs[t % 2], bTs[t % 2], 1.0 / 32, t2_[t % RD]) for t in range(NT))
    return P.finish()


def window_masks(s):
    a = np.arange(128)[:, None]
    b = np.arange(128)[None, :]
    NEG = -30000.0
    mL = np.where(a >= b, 0.0, NEG).astype(np.float32)
    mR = np.where(a <= b, 0.0, NEG).astype(np.float32)
    full = np.full((128, 128), NEG, np.float32)
    ms = [mL, mR, full if s == 0 else mL, full if s == 3 else mR]
    return np.stack([np.concatenate([m, m], axis=1) for m in ms]).astype(NPBF)


def launch_attn(inp, r1):
    nc = build_attn()
    ident = np.eye(128, dtype=np.float32).astype(NPBF)
    ngb = np.ascontiguousarray(np.broadcast_to(inp["norm_g"].reshape(1, 8 * D), (128, 8 * D)))
    sinkb = np.ascontiguousarray(np.broadcast_to(inp["sink"].reshape(1, 4), (128, 4)))
    maps = []
    for core in range(NCORE):
        b, s = divmod(core, 4)
        grp = [r1[b * 4 + i] for i in range(4)]
        KT_all = np.concatenate([grp[0]["KTc"]] + [g["KT"] for g in grp], axis=2)
        V_all = np.concatenate([grp[0]["Vc"]] + [g["V"] for g in grp], axis=0)
        KaT = np.ascontiguousarray(KT_all[0:2])
        Va = np.ascontiguousarray(V_all[:, 0:256].reshape(NKB, 128, 2, 128).transpose(2, 1, 0, 3))
        zk = np.zeros((4, 128, 128), NPBF)
        zv = np.zeros((128, 512), NPBF)
        lo, hi = s * TOK + CTX - 128, (s + 1) * TOK + CTX + 128
        kt_parts = [KT_all[:, :, 0:CTX]]
        v_parts = [V_all[0:CTX]]
        if s == 0:
            kt_parts += [zk, KT_all[:, :, CTX:hi]]
            v_parts += [zv, V_all[CTX:hi]]
        elif s == 3:
            kt_parts += [KT_all[:, :, lo:], zk]
            v_parts += [V_all[lo:], zv]
        else:
            kt_parts += [KT_all[:, :, lo:hi]]
            v_parts += [V_all[lo:hi]]
        KbT = np.ascontiguousarray(np.concatenate(kt_parts, axis=2)[2:4])
        Vloc = np.concatenate(v_parts, axis=0)
        Vb = np.ascontiguousarray(Vloc[:, 256:512].reshape(NLB, 128, 2, 128).transpose(2, 1, 0, 3))
        maps.append(dict(QT=r1[core]["QT"], KaT=KaT, Va=Va, KbT=KbT, Vb=Vb, bm=window_masks(s),
                         x=np.ascontiguousarray(inp["x"][b, s * TOK:(s + 1) * TOK]),
                         M=np.ascontiguousarray(r1[core]["M"][:, 0, :]), ng=ngb, w_out=inp["attn_w_out"][0],
                         sink=sinkb, ident=ident))
    return run(nc, maps)


WIN = 256
NWIN = TOK // WIN
NFC = DFF // 128


def build_ffn(emit_next):
    P = Prog()
    S = P.S
    hT_in = P.din("hT", [8, 128, TOK + 2], BF16)
    x1 = P.din("x1", [TOK, D], F32)
    w_up = P.din("w_up", [D, 2 * DFF], F32)
    w_dn = P.din("w_dn", [DFF, D], F32)
    cw_d = P.din("cw", [128, 2 * NFC, 3], F32)
    cb_d = P.din("cb", [128, 2 * NFC], F32)
    Mg2 = P.din("Mg2", [128, D], F32)
    ng3 = P.din("ng3", [128, D], F32)
    if emit_next:
        Mn = P.din("Mn", [128, 2 * D], F32)
        ngn = P.din("ngn", [128, D], F32)
        h1n = P.dout("h1n", [TOK, D], BF16)
    x2o = P.dout("x2", [TOK, D], F32)

    banks = [P.bank(f"bk{i}", [128, 512], F32) for i in range(6)]
    eps_t = P.sb("eps", [128, 1], F32)
    S.op("dve", lambda e: e.memset(eps_t[:], EPS), writes=[eps_t])
    wup = P.sb("wup", [128, 8, 2 * DFF], BF16)
    wdn = P.sb("wdn", [128, NFC, D], BF16)
    stg = [P.sb(f"stg{i}", [128, 704], F32) for i in range(3)]
    cw = P.sb("cw", [128, 2 * NFC, 3], F32)
    cb = P.sb("cb", [128, 2 * NFC], F32)
    GG2 = P.sb("GG2", [128, D], F32)
    xt = [P.sb(f"xt{i}", [128, D], F32) for i in range(2)]
    t1 = P.sb("t1", [128, D], F32)
    S.dma("sp", cw[:], cw_d, writes=[cw])
    S.dma("sp", cb[:], cb_d, writes=[cb])
    S.dma("sp", t1[:], Mg2, writes=[t1])
    S.dma("sp", xt[0][:], ng3, writes=[xt[0]])
    S.op("dve", lambda e: e.tensor_tensor(out=GG2[:], in0=t1[:], in1=xt[0][:], op=ALU.mult), reads=[t1, xt[0]], writes=[GG2])
    if emit_next:
        G1n = P.sb("G1n", [128, D], F32)
        SH1n = P.sb("SH1n", [128, D], F32)
        S.dma("sp", t1[:], Mn[:, D:2 * D], writes=[t1])
        S.dma("sp", xt[1][:], ngn, writes=[xt[1]])
        S.op("dve", lambda e: e.scalar_tensor_tensor(out=G1n[:], in0=t1[:], scalar=1.0, in1=xt[1][:], op0=ALU.add, op1=ALU.mult),
             reads=[t1, xt[1]], writes=[G1n])
        S.dma("sp", SH1n[:], Mn[:, 0:D], writes=[SH1n])
    ci = 0
    engs = ("dve", "pool", "act")
    def cast(dst_ap, src_ap, dst_tile):
        nonlocal ci
        st_ = stg[ci % 3]
        eng = engs[ci % 3]
        S.dma("sp" if ci % 2 == 0 else "pool", st_[:, 0:src_ap.shape[1]], src_ap, writes=[st_])
        n = src_ap.shape[1]
        if eng == "act":
            S.op("act", lambda e: e.copy(out=dst_ap, in_=st_[:, 0:n]), reads=[st_], writes=[dst_tile])
        else:
            S.op(eng, lambda e: e.tensor_copy(out=dst_ap, in_=st_[:, 0:n]), reads=[st_], writes=[dst_tile])
        ci += 1
    for k in range(8):
        for pc in range(8):
            cast(wup[:, k, pc * 704:(pc + 1) * 704], w_up[k * 128:(k + 1) * 128, pc * 704:(pc + 1) * 704], wup)
    for c in range(NFC):
        for pc in range(2):
            cast(wdn[:, c, pc * 512:(pc + 1) * 512], w_dn[c * 128:(c + 1) * 128, pc * 512:(pc + 1) * 512], wdn)

    hw = [P.sb(f"hw{i}", [128, 8, WIN + 2], BF16) for i in range(2)]
    aT = P.sb("aT", [128, NFC, WIN], BF16)
    ag = [P.sb(f"ag{i}", [128, WIN], F32) for i in range(2)]
    av = [P.sb(f"av{i}", [128, WIN], F32) for i in range(2)]
    sg = [P.sb(f"sg{i}", [128, WIN], F32) for i in range(2)]
    junk = P.sb("junk", [128, D], BF16)
    ms2 = P.sb("ms2", [128, 2], F32)
    ms = P.sb("ms", [128, 1], F32)
    rstd = P.sb("rstd", [128, 1], F32)
    hb = P.sb("hb", [128, D], BF16)
    bG = banks[0:2]
    bV = banks[2:4]
    bY = banks[4:6]
    pi = 0
    ti = 0
    for w in range(NWIN):
        h = hw[w % 2]
        S.dma("sp", h[:], hT_in[:, :, w * WIN:w * WIN + WIN + 2].rearrange("k p t -> p k t"), writes=[h])
        for c in range(NFC):
            g_b = bG[pi % 2]
            v_b = bV[pi % 2]
            a_g = ag[pi % 2]
            a_v = av[pi % 2]
            s_g = sg[pi % 2]
            pi += 1
            for (bk, col0) in ((g_b, c * 128), (v_b, DFF + c * 128)):
                for k in range(8):
                    S.op("pe", lambda e, bk=bk, col0=col0, k=k, h=h: e.matmul(bk[:, 0:WIN + 2], lhsT=wup[:, k, col0:col0 + 128],
                                                                             rhs=h[:, k, :], start=(k == 0), stop=(k == 7)),
                         reads=[wup, h], writes=[bk], inc=(k == 7))
            for (bk, a_, cc) in ((g_b, a_g, c), (v_b, a_v, NFC + c)):
                S.op("act", lambda e, bk=bk, a_=a_, cc=cc: e.activation(out=a_[:], in_=bk[:, 1:WIN + 1], func=AF.Identity,
                                                                       bias=cb[:, cc:cc + 1], scale=cw[:, cc, 1:2]),
                     reads=[bk, cb, cw], writes=[a_])
                S.op("dve", lambda e, bk=bk, a_=a_, cc=cc: e.scalar_tensor_tensor(out=a_[:], in0=bk[:, 0:WIN], scalar=cw[:, cc, 0:1],
                                                                                 in1=a_[:], op0=ALU.mult, op1=ALU.add),
                     reads=[bk, cw, a_], writes=[a_])
                S.op("dve", lambda e, bk=bk, a_=a_, cc=cc: e.scalar_tensor_tensor(out=a_[:], in0=bk[:, 2:WIN + 2], scalar=cw[:, cc, 2:3],
                                                                                 in1=a_[:], op0=ALU.mult, op1=ALU.add),
                     reads=[bk, cw, a_], writes=[a_])
            S.op("act", lambda e, a_g=a_g, s_g=s_g: e.activation(out=s_g[:], in_=a_g[:], func=AF.Silu), reads=[a_g], writes=[s_g])
            S.op("pool", lambda e, s_g=s_g, a_v=a_v, c=c: e.tensor_tensor(out=aT[:, c, :], in0=s_g[:], in1=a_v[:], op=ALU.mult),
                 reads=[s_g, a_v], writes=[aT])
        for tt in range(WIN // 128):
            t = w * (WIN // 128) + tt
            xx = xt[ti % 2]
            ti += 1
            S.dma("sp", xx[:], x1[t * 128:(t + 1) * 128, :], writes=[xx])
            for n in range(2):
                for c in range(NFC):
                    S.op("pe", lambda e, n=n, c=c, tt=tt: e.matmul(bY[n][:], lhsT=aT[:, c, tt * 128:(tt + 1) * 128],
                                                                   rhs=wdn[:, c, n * 512:(n + 1) * 512], start=(c == 0), stop=(c == NFC - 1)),
                         reads=[aT, wdn], writes=[bY[n]], inc=(c == NFC - 1))
            for n in range(2):
                S.op("act", lambda e, n=n: e.activation(out=junk[:, 0:512], in_=bY[n][:], func=AF.Square, scale=1.0 / 32,
                                                        accum_out=ms2[:, n:n + 1]), reads=[bY[n]], writes=[junk, ms2])
            S.op("dve", lambda e: e.tensor_tensor(out=ms[:], in0=ms2[:, 0:1], in1=ms2[:, 1:2], op=ALU.add), reads=[ms2], writes=[ms])
            rstd_from_ms(S, ms, eps_t, rstd)
            for n in range(2):
                S.op("dve", lambda e, n=n: e.scalar_tensor_tensor(out=t1[:, n * 512:(n + 1) * 512], in0=bY[n][:], scalar=rstd[:, 0:1],
                                                                  in1=GG2[:, n * 512:(n + 1) * 512], op0=ALU.mult, op1=ALU.mult),
                     reads=[bY[n], rstd, GG2], writes=[t1])
            S.op("pool", lambda e, xx=xx: e.tensor_tensor(out=t1[:], in0=t1[:], in1=xx[:], op=ALU.add), reads=[t1, xx], writes=[t1])
            S.dma("pool", x2o[t * 128:(t + 1) * 128, :], t1[:], reads=[t1])
            if emit_next:
                S.op("act", lambda e: e.activation(out=junk[:], in_=t1[:], func=AF.Square, scale=1.0 / 32, accum_out=ms[:]),
                     reads=[t1], writes=[junk, ms])
                rstd_from_ms(S, ms, eps_t, rstd)
                S.op("dve", lambda e, xx=xx: e.scalar_tensor_tensor(out=xx[:], in0=t1[:], scalar=rstd[:, 0:1], in1=G1n[:],
                                                                    op0=ALU.mult, op1=ALU.mult), reads=[t1, rstd, G1n], writes=[xx])
                S.op("pool", lambda e, xx=xx: e.tensor_tensor(out=hb[:], in0=xx[:], in1=SH1n[:], op=ALU.add), reads=[xx, SH1n], writes=[hb])
                S.dma("pool", h1n[t * 128:(t + 1) * 128, :], hb[:], reads=[hb])
    return P.finish()


def bc(v):
    return np.ascontiguousarray(np.broadcast_to(np.asarray(v).reshape(1, -1), (128, np.asarray(v).size)))


def launch_ffn(inp, layer, hT_list, x1_list, M_list, emit_next):
    nc = build_ffn(emit_next)
    cw = np.ascontiguousarray(inp["ffn_conv_w"][layer].T.reshape(2 * NFC, 128, 3).transpose(1, 0, 2))
    cb = np.ascontiguousarray(inp["ffn_conv_b"][layer].reshape(2 * NFC, 128).T)
    maps = []
    z = np.zeros((8, 128, 1), NPBF)
    for core in range(NCORE):
        b, s = divmod(core, 4)
        lo = hT_list[core - 1][:, :, -1:] if s > 0 else z
        hi = hT_list[core + 1][:, :, 0:1] if s < 3 else z
        m = dict(hT=np.ascontiguousarray(np.concatenate([lo, hT_list[core], hi], axis=2)), x1=x1_list[core],
                 w_up=inp["ffn_w_up"][layer], w_dn=inp["ffn_w_down"][layer], cw=cw, cb=cb,
                 Mg2=np.ascontiguousarray(M_list[core][:, layer, 5 * D:6 * D]), ng3=bc(inp["norm_g"][layer, 3]))
        if emit_next:
            m["Mn"] = np.ascontiguousarray(M_list[core][:, layer + 1, 0:2 * D])
            m["ngn"] = bc(inp["norm_g"][layer + 1, 0])
        maps.append(m)
    return run(nc, maps)


def build_fnet():
    P = Prog()
    S = P.S
    h_all = P.din("h_all", [L, D], BF16)
    x2 = P.din("x2", [TOK, D], F32)
    Dcs_d = P.din("Dcs", [128, 256], BF16)
    E_d = P.din("E", [16, 128, 8, 128], BF16)
    CS_d = P.din("CS", [128, 2, 2, 2, 128], BF16)
    w_f = P.din("w_f", [D, D], F32)
    M = P.din("M", [128, 6 * D], F32)
    ng = P.din("ng", [128, 4 * D], F32)
    ident = P.din("ident", [128, 128], BF16)
    x3o = P.dout("x3", [TOK, D], F32)
    h2T = P.dout("h2T", [8, 128, TOK], BF16)

    FT = P.sb("FT", [128, 8, TOK], BF16)
    idt = P.sb("ident", [128, 128], BF16)
    eps_t = P.sb("eps", [128, 1], F32)
    S.op("dve", lambda e: e.memset(eps_t[:], EPS), writes=[eps_t])
    S.dma("pool", idt[:], ident, writes=[idt])
    hv = h_all.rearrange("(t2 t1) c -> t2 t1 c", t1=128)

    with Scope(P):
        banks = [P.bank(f"bk{i}", [128, 512], F32) for i in range(8)]
        Hc = P.sb("Hc", [128, 128, 128], BF16)
        B = P.sb("B", [128, 2, 128, 128], BF16)
        GT = P.sb("GT", [128, 2, 2, TOK], BF16)
        Eg = [P.sb(f"Eg{i}", [128, 8, 128], BF16) for i in range(2)]
        Dcs = P.sb("Dcs", [128, 256], BF16)
        CS = P.sb("CS", [128, 2, 2, 2, 128], BF16)
        S.dma("sp", Dcs[:], Dcs_d, writes=[Dcs])
        S.dma("sp", CS[:], CS_d, writes=[CS])
        ei = 0
        bi = 0
        for cb in range(8):
            cbl = cb % 2
            for q in range(4):
                S.dma("sp" if q % 2 == 0 else "pool", Hc[:, q * 32:(q + 1) * 32, :],
                      hv[:, q * 32:(q + 1) * 32, cb * 128:(cb + 1) * 128], writes=[Hc])
            for cp in range(64):
                bk = banks[bi % 4]
                bi += 1
                for j in range(2):
                    ch = 2 * cp + j
                    S.op("pe", lambda e, bk=bk, j=j, ch=ch: e.matmul(bk[:, j * 256:(j + 1) * 256], lhsT=Hc[:, :, ch], rhs=Dcs[:],
                                                                     start=True, stop=True),
                         reads=[Hc, Dcs], writes=[bk], inc=(j == 1))
                eng = "act" if cp % 2 == 0 else "dve"
                outv = lambda cp=cp: B[:].rearrange("p r k c -> p (r k) c")[:, :, 2 * cp:2 * cp + 2]
                inv = lambda bk=bk: bk[:].rearrange("p (j q) -> p q j", j=2)
                if eng == "act":
                    S.op("act", lambda e, outv=outv, inv=inv: e.copy(out=outv(), in_=inv()), reads=[bk], writes=[B])
                else:
                    S.op("dve", lambda e, outv=outv, inv=inv: e.tensor_copy(out=outv(), in_=inv()), reads=[bk], writes=[B])
            for kg in range(16):
                eg = Eg[ei % 2]
                ei += 1
                S.dma("sp", eg[:], E_d[kg], writes=[eg])
                bk = banks[4 + kg % 2]
                for j in range(8):
                    k2 = 8 * kg + j
                    S.op("pe", lambda e, bk=bk, j=j, k2=k2, eg=eg: e.matmul(bk[:, j * 64:(j + 1) * 64], lhsT=B[:, 0, k2, :],
                                                                           rhs=eg[:, j, 0:64], start=True, stop=False),
                         reads=[B, eg], writes=[bk], inc=False)
                    S.op("pe", lambda e, bk=bk, j=j, k2=k2, eg=eg: e.matmul(bk[:, j * 64:(j + 1) * 64], lhsT=B[:, 1, k2, :],
                                                                           rhs=eg[:, j, 64:128], start=False, stop=True),
                         reads=[B, eg], writes=[bk], inc=(j == 7))
                outv = lambda cbl=cbl, kg=kg: GT[:, cbl, :, :].rearrange("p r (a b) -> p b r a", b=128)[:, 8 * kg:8 * kg + 8, :, :]
                inv = lambda bk=bk: bk[:].rearrange("p (j r a) -> p j r a", j=8, r=2)
                if kg % 2 == 0:
                    S.op("act", lambda e, outv=outv, inv=inv: e.copy(out=outv(), in_=inv()), reads=[bk], writes=[GT])
                else:
                    S.op("dve", lambda e, outv=outv, inv=inv: e.tensor_copy(out=outv(), in_=inv()), reads=[bk], writes=[GT])
            if cbl == 1:
                g = cb // 2
                for mb in range(2):
                    for tg in range(TOK // 512):
                        bk = banks[6]
                        n = 0
                        for nb in range(2):
                            for ri in range(2):
                                S.op("pe", lambda e, nb=nb, ri=ri, mb=mb, tg=tg, n=n: e.matmul(
                                    banks[6][:], lhsT=CS[:, ri, nb, mb, :], rhs=GT[:, nb, ri, tg * 512:(tg + 1) * 512],
                                    start=(n == 0), stop=(n == 3)), reads=[CS, GT], writes=[bk], inc=(n == 3))
                                n += 1
                        if tg % 2 == 0:
                            S.op("act", lambda e, g=g, mb=mb, tg=tg: e.copy(out=FT[:, 2 * g + mb, tg * 512:(tg + 1) * 512], in_=banks[6][:]),
                                 reads=[bk], writes=[FT])
                        else:
                            S.op("dve", lambda e, g=g, mb=mb, tg=tg: e.tensor_copy(out=FT[:, 2 * g + mb, tg * 512:(tg + 1) * 512], in_=banks[6][:]),
                                 reads=[bk], writes=[FT])

    with Scope(P):
        wbf = P.sb("wfbf", [128, 8, D], BF16)
        wst = [P.sb(f"wfst{i}", [128, D], F32) for i in range(2)]
        for k in range(8):
            ws = wst[k % 2]
            S.dma("pool", ws[:], w_f[k * 128:(k + 1) * 128, :], writes=[ws])
            S.op("pool", lambda e, k=k, ws=ws: e.tensor_copy(out=wbf[:, k, :], in_=ws[:]), reads=[ws], writes=[wbf])
        GG1 = P.sb("GG1", [128, D], F32)
        G2 = P.sb("G2", [128, D], F32)
        SH2 = P.sb("SH2", [128, D], F32)
        tmpa = P.sb("tmpa", [128, D], F32)
        tmpb = P.sb("tmpb", [128, D], F32)
        S.dma("sp", tmpa[:], M[:, 2 * D:3 * D], writes=[tmpa])
        S.dma("sp", tmpb[:], ng[:, 1 * D:2 * D], writes=[tmpb])
        S.op("dve", lambda e: e.scalar_tensor_tensor(out=GG1[:], in0=tmpa[:], scalar=1.0 / 2048, in1=tmpb[:], op0=ALU.mult, op1=ALU.mult),
             reads=[tmpa, tmpb], writes=[GG1])
        S.dma("sp", tmpa[:], M[:, 4 * D:5 * D], writes=[tmpa])
        S.dma("sp", tmpb[:], ng[:, 2 * D:3 * D], writes=[tmpb])
        S.op("dve", lambda e: e.scalar_tensor_tensor(out=G2[:], in0=tmpa[:], scalar=1.0, in1=tmpb[:], op0=ALU.add, op1=ALU.mult),
             reads=[tmpa, tmpb], writes=[G2])
        S.dma("sp", SH2[:], M[:, 3 * D:4 * D], writes=[SH2])
        RD = 3
        xt = [P.sb(f"xt{i}", [128, D], F32) for i in range(RD)]
        ysb_ = [P.sb(f"ysb{i}", [128, D], F32) for i in range(RD)]
        junk_ = [P.sb(f"junk{i}", [128, D], BF16) for i in range(RD)]
        ms_ = [P.sb(f"ms{i}", [128, 2], F32) for i in range(RD)]
        rstd_ = [P.sb(f"rstd{i}", [128, 2], F32) for i in range(RD)]
        t1_ = [P.sb(f"t1{i}", [128, D], F32) for i in range(RD)]
        t2_ = [P.sb(f"t2{i}", [128, D], F32) for i in range(RD)]
        x1 = [P.sb(f"x1_{i}", [128, D], F32) for i in range(RD)]
        hb_ = [P.sb(f"hb{i}", [128, D], BF16) for i in range(RD)]
        hT_ = [P.sb(f"hT{i}", [128, 8, 128], BF16) for i in range(RD)]
        bYs = [[P.bank(f"bY{i}_{n}", [128, 512], F32) for n in range(2)] for i in range(2)]
        bTs = [P.bank(f"bT{i}", [128, 1024], BF16) for i in range(2)]
        pipeline(resid_tile(P, t, x2, x3o, h2T, lambda h, t=t: FT[:, h, t * 128:(t + 1) * 128], FT, wbf, GG1, G2, SH2, eps_t, idt,
                            xt[t % RD], x1[t % RD], ysb_[t % RD], junk_[t % RD], ms_[t % RD], rstd_[t % RD], t1_[t % RD], hb_[t % RD], hT_[t % RD],
                            bYs[t % 2], bTs[t % 2], 1.0 / (32 * 2048), t2_[t % RD]) for t in range(NT))
    return P.finish()


def dft_tables(s):
    t = np.arange(128)
    ang = 2 * np.pi * np.outer(t, t) / 128.0
    Dcs = np.concatenate([np.cos(ang), np.sin(ang)], axis=1)
    k1 = 32 * s + np.arange(32)
    k2 = np.arange(128)
    t1 = np.arange(128)
    k = 128 * k1[None, None, :] + k2[None, :, None]
    ph = (k.astype(np.int64) * t1[:, None, None].astype(np.int64)) % L
    th = 2 * np.pi * ph / float(L)
    Ec, Es = np.cos(th), np.sin(th)
    E = np.concatenate([Ec, Es, -Es, Ec], axis=2)
    E = E.reshape(128, 16, 8, 128).transpose(1, 0, 2, 3)
    n = np.arange(256)
    a = 2 * np.pi * ((np.outer(n, n)) % 256) / 256.0
    C = np.cos(a).reshape(2, 128, 2, 128)
    Sn = -np.sin(a).reshape(2, 128, 2, 128)
    CS = np.stack([C, Sn], axis=0).transpose(2, 0, 1, 3, 4)
    return (np.ascontiguousarray(Dcs).astype(NPBF), np.ascontiguousarray(E).astype(NPBF),
            np.ascontiguousarray(CS).astype(NPBF))


def launch_fnet(inp, h1_list, x2_list, M_list):
    nc = build_fnet()
    ident = np.eye(128, dtype=np.float32).astype(NPBF)
    ngb = np.ascontiguousarray(np.broadcast_to(inp["norm_g"][1].reshape(1, 4 * D), (128, 4 * D)))
    maps = []
    for core in range(NCORE):
        b, s = divmod(core, 4)
        h_all = np.ascontiguousarray(np.concatenate([h1_list[b * 4 + i] for i in range(4)], axis=0))
        Dcs, E, CS = dft_tables(s)
        maps.append(dict(h_all=h_all, x2=x2_list[core], Dcs=Dcs, E=E, CS=CS, w_f=inp["fourier_w_out"][0],
                         M=np.ascontiguousarray(M_list[core][:, 1, :]), ng=ngb, ident=ident))
    return run(nc, maps)


def _np(res):
    return [{k: np.asarray(v) for k, v in r.items()} for r in res]


def kernel(**inputs):
    inp = {k: np.asarray(v) for k, v in inputs.items()}
    r1 = _np(launch_proj(inp))
    Ms = [r["M"] for r in r1]
    r2 = _np(launch_attn(inp, r1))
    r3 = _np(launch_ffn(inp, 0, [r["h2T"] for r in r2], [r["x1"] for r in r2], Ms, True))
    del r2
    r4 = _np(launch_fnet(inp, [r["h1n"] for r in r3], [r["x2"] for r in r3], Ms))
    del r3
    r5 = _np(launch_ffn(inp, 1, [r["h2T"] for r in r4], [r["x3"] for r in r4], Ms, False))
    out = np.empty((NB, L, D), np.float32)
    for core in range(NCORE):
        b, s = divmod(core, 4)
        out[b, s * TOK:(s + 1) * TOK] = r5[core]["x2"]
    return out
```

```python
import numpy as np
import ml_dtypes
from contextlib import ExitStack
import concourse.bass as bass
import concourse.mybir as mybir
from concourse.bass_utils import run_bass_kernel_spmd

F32 = mybir.dt.float32
BF16 = mybir.dt.bfloat16
AF = mybir.ActivationFunctionType
ALU = mybir.AluOpType
AX = mybir.AxisListType
NPBF = ml_dtypes.bfloat16

D = 1024
L = 16384
NB = 2
NCORE = 8
TOK = 4096
NT = TOK // 128
CTX = 256
DFF = 2816
EPS = 1e-6
SEM_LIMIT = 30000
SELF_SYNC = True
MAX_ATTACH = 1


class Buf:
    __slots__ = ("name", "w", "r", "excl")

    def __init__(self, name="", excl=False):
        self.name = name
        self.w = None
        self.r = []
        self.excl = excl


class Tile:
    def __init__(self, t, buf):
        self.t = t
        self.buf = buf

    def __getitem__(self, k):
        return self.t[k]


class Sched:
    ENGS = ("pe", "act", "dve", "pool", "sp")

    def __init__(self, nc, stack, n_dma_sems=10):
        self.nc = nc
        self.stack = stack
        self.recs = {e: [] for e in self.ENGS}
        self.sems = []
        self.cur_sem = {}
        self.cnt = {}
        self.seen = {e: {} for e in self.ENGS}
        self.pending = {e: [] for e in self.ENGS}
        self.dma_ring = {}
        self.dma_pos = {}
        self.all_dma_events = []
        for e in ("pe", "act", "dve", "pool"):
            self.cur_sem[e] = self._new_sem(f"s_{e}_0")
        for q in ("sp", "act", "pool"):
            self.dma_ring[q] = [self._new_sem(f"d_{q}_{i}") for i in range(n_dma_sems)]
            self.dma_pos[q] = 0

    def _new_sem(self, name):
        h = self.stack.enter_context(self.nc.semaphore(name))
        self.sems.append(h)
        idx = len(self.sems) - 1
        self.cnt[idx] = 0
        return idx

    def _need(self, eng, ev, waits):
        s, v = ev
        if self.seen[eng].get(s, 0) >= v:
            return
        self.seen[eng][s] = v
        waits.append((s, v))

    def _collect(self, eng, reads, writes, self_sync=True):
        waits = []
        SELF_SYNC = self_sync and eng != "pe"
        for b in reads:
            if b.w is not None and (SELF_SYNC or b.w[1] != eng):
                self._need(eng, b.w[0], waits)
            if b.excl:
                for ev, e2 in b.r:
                    if e2 != eng:
                        self._need(eng, ev, waits)
        for b in writes:
            if b.w is not None and (SELF_SYNC or b.w[1] != eng):
                self._need(eng, b.w[0], waits)
            for ev, e2 in b.r:
                if e2 != eng:
                    self._need(eng, ev, waits)
        return waits

    @staticmethod
    def _bufs(lst):
        return [x.buf if isinstance(x, Tile) else x for x in lst]

    def op(self, eng, fn, reads=(), writes=(), inc=True, self_sync=True):
        reads = self._bufs(reads)
        writes = self._bufs(writes)
        waits = self._collect(eng, reads, writes, self_sync)
        self.pending[eng].append((reads, writes))
        if inc:
            s = self.cur_sem[eng]
            if self.cnt[s] >= SEM_LIMIT:
                s = self._new_sem(f"s_{eng}_{len(self.sems)}")
                self.cur_sem[eng] = s
            self.cnt[s] += 1
            ev = (s, self.cnt[s])
            for (rs, ws) in self.pending[eng]:
                for b in rs:
                    b.r.append((ev, eng))
                for b in ws:
                    b.w = (ev, eng)
                    b.r = []
            self.pending[eng] = []
            self.recs[eng].append((waits, fn, (s, 1)))
        else:
            self.recs[eng].append((waits, fn, None))

    def dma(self, q, out, in_, reads=(), writes=()):
        reads = self._bufs(reads)
        writes = self._bufs(writes)
        waits = self._collect(q, reads, writes)
        ring = self.dma_ring[q]
        s = ring[self.dma_pos[q] % len(ring)]
        self.dma_pos[q] += 1
        if self.cnt[s] > 0:
            self._need(q, (s, self.cnt[s]), waits)
        self.cnt[s] += 16
        ev = (s, self.cnt[s])
        for b in reads:
            b.r.append((ev, "dma_" + q))
        for b in writes:
            b.w = (ev, "dma_" + q)
            b.r = []
        self.recs[q].append((waits, (lambda e: e.dma_start(out=out, in_=in_)), (s, 16)))
        self.all_dma_events.append(ev)
        return ev

    def coll(self, kind, groups, src, dst, reads=(), writes=()):
        q = "pool"
        reads = self._bufs(reads)
        writes = self._bufs(writes)
        waits = self._collect(q, reads, writes)
        s = self._new_sem(f"cc_{len(self.sems)}")
        self.cnt[s] += 16
        ev = (s, self.cnt[s])
        for b in reads:
            b.r.append((ev, "cc"))
        for b in writes:
            b.w = (ev, "cc")
            b.r = []
        self.recs[q].append((waits, (lambda e: e.collective_compute(kind, (ALU.add if kind == 'AllReduce' else ALU.bypass), replica_groups=groups,
                                                                    ins=[src], outs=[dst])), (s, 16)))
        self.all_dma_events.append(ev)
        return ev

    def drain_dmas(self, eng="sp"):
        waits = []
        for ev in self.all_dma_events:
            self._need(eng, ev, waits)
        if waits:
            self.recs[eng].append((waits, None, None))
        self.all_dma_events = []

    def flush(self):
        nc = self.nc
        for e in self.ENGS:
            assert not self.pending[e], f"uncovered instrs on {e}"
        recs = self.recs
        sems = self.sems

        def replay(engine, lst):
            for waits, fn, inc in lst:
                nw = len(waits)
                n_sep = nw if fn is None else max(0, nw - MAX_ATTACH)
                for (s, v) in waits[:n_sep]:
                    engine.wait_ge(sems[s], v)
                if fn is not None:
                    ins = fn(engine)
                    for (s, v) in waits[n_sep:]:
                        ins._wait_ge(sems[s], v)
                    if inc is not None:
                        ins.then_inc(sems[inc[0]], inc[1])

        with nc.Block() as block:
            @block.tensor
            def _(e):
                replay(e, recs["pe"])

            @block.scalar
            def _(e):
                replay(e, recs["act"])

            @block.vector
            def _(e):
                replay(e, recs["dve"])

            @block.gpsimd
            def _(e):
                replay(e, recs["pool"])

            @block.sync
            def _(e):
                replay(e, recs["sp"])
        self.recs = {e: [] for e in self.ENGS}


class Prog:
    def __init__(self):
        self.nc = bass.Bass("TRN2", target_bir_lowering=False)
        self.stack = ExitStack()
        self.S = Sched(self.nc, self.stack)
        self.n_psum = 0

    def din(self, name, shape, dt):
        return self.nc.dram_tensor(name, list(shape), dt, kind="ExternalInput").ap()

    def dout(self, name, shape, dt):
        return self.nc.dram_tensor(name, list(shape), dt, kind="ExternalOutput").ap()

    def sb(self, name, shape, dt):
        t = self.stack.enter_context(self.nc.sbuf_tensor("s_" + name, list(shape), dt))
        return Tile(t, Buf(name))

    def bank(self, name, shape, dt, nbanks=1):
        self.n_psum += nbanks
        assert self.n_psum <= 8
        t = self.stack.enter_context(self.nc.psum_tensor("p_" + name, list(shape), dt))
        return Tile(t, Buf(name, excl=True))

    def finish(self):
        self.S.drain_dmas("sp")
        self.S.flush()
        self.stack.close()
        return self.nc


def rstd_from_ms(S, ms, eps_t, out, n=None):
    S.op("act", lambda e: e.activation(out=out[:], in_=ms[:], func=AF.Ln, bias=eps_t[:], scale=1.0),
         reads=[ms, eps_t], writes=[out])
    S.op("act", lambda e: e.activation(out=out[:], in_=out[:], func=AF.Exp, scale=-0.5),
         reads=[out], writes=[out])


def adaln_table(P, cT_ap, modw_ap, modb_ap, col0, ncol, consume):
    S = P.S
    ng = ncol // 512
    c_sb = P.t_c
    S.dma("sp", c_sb[:], cT_ap, writes=[c_sb])
    S.op("act", lambda e: e.activation(out=P.t_sc[:], in_=c_sb[:], func=AF.Silu), reads=[c_sb], writes=[P.t_sc])
    for k in range(8):
        S.op("dve", lambda e, k=k: e.tensor_copy(out=P.t_screp[:, k, :], in_=P.t_sc[:, k:k + 1].to_broadcast([128, 128])),
             reads=[P.t_sc], writes=[P.t_screp])
    S.dma("sp", P.t_mb[:, 0:ncol], modb_ap[:, col0:col0 + ncol], writes=[P.t_mb])
    for k in range(8):
        mw = P.t_mw[k % 2]
        S.dma("sp" if k % 2 == 0 else "pool", mw[:, 0:ncol], modw_ap[k * 128:(k + 1) * 128, col0:col0 + ncol], writes=[mw])
        for g in range(ng):
            S.op("pe", lambda e, k=k, g=g, mw=mw: e.matmul(P.banks[g][:], lhsT=P.t_screp[:, k, :],
                                                             rhs=mw[:, g * 512:(g + 1) * 512], start=(k == 0), stop=False),
                 reads=[P.t_screp, mw], writes=[P.banks[g]], inc=(g == ng - 1))
    for g in range(ng):
        S.op("pe", lambda e, g=g: e.matmul(P.banks[g][:], lhsT=P.t_ones1[:], rhs=P.t_mb[:, g * 512:(g + 1) * 512],
                                           start=False, stop=True),
             reads=[P.t_ones1, P.t_mb], writes=[P.banks[g]])
        S.op("dve" if g % 2 == 0 else "act",
             (lambda e, g=g: e.tensor_copy(out=P.t_stage[:, g * 512:(g + 1) * 512], in_=P.banks[g][:])) if g % 2 == 0 else
             (lambda e, g=g: e.copy(out=P.t_stage[:, g * 512:(g + 1) * 512], in_=P.banks[g][:])),
             reads=[P.banks[g]], writes=[P.t_stage])
    consume(P.t_stage)


def build_proj():
    P = Prog()
    S = P.S
    x = P.din("x", [TOK, D], F32)
    ctx = P.din("ctx", [CTX, D], F32)
    cT = P.din("cT", [128, 8], F32)
    ccT = P.din("ccT", [128, 8], F32)
    mw0a = P.din("mw0a", [D, 2048], F32)
    mb0a = P.din("mb0a", [1, 2048], F32)
    mw0q = P.din("mw0q", [D, 1024], F32)
    mb0q = P.din("mb0q", [1, 1024], F32)
    mw1q = P.din("mw1q", [D, 1536], F32)
    mb1q = P.din("mb1q", [1, 1536], F32)
    ng = P.din("ng", [128, 8 * D], F32)
    w_in = P.din("w_in", [D, 2048], F32)
    qg = P.din("qg", [128, 128], F32)
    kg = P.din("kg", [128, 128], F32)
    rope = P.din("rope", [TOK, 128], F32)
    ident = P.din("ident", [128, 128], BF16)
    QT = P.dout("QT", [8, 128, TOK], BF16)
    KT = P.dout("KT", [4, 128, TOK], BF16)
    V = P.dout("V", [TOK, 512], BF16)
    KTc = P.dout("KTc", [4, 128, CTX], BF16)
    Vc = P.dout("Vc", [CTX, 512], BF16)
    M0a = P.dout("M0a", [128, 2048], F32)
    M0q = P.dout("M0q", [128, 1024], F32)
    M1q = P.dout("M1q", [128, 1536], F32)

    eps_t = P.sb("eps", [128, 1], F32)
    idt = P.sb("ident", [128, 128], BF16)
    G1 = P.sb("G1", [128, D], F32)
    SH1 = P.sb("SH1", [128, D], F32)
    Gc = P.sb("Gc", [128, D], F32)
    SHc = P.sb("SHc", [128, D], F32)
    qg_t = P.sb("qg", [128, 128], F32)
    kg_t = P.sb("kg", [128, 128], F32)
    wbf = P.sb("wbf", [128, 8, 2048], BF16)
    S.op("dve", lambda e: e.memset(eps_t[:], EPS), writes=[eps_t])
    S.dma("pool", idt[:], ident, writes=[idt])
    S.dma("pool", qg_t[:], qg, writes=[qg_t])
    S.dma("pool", kg_t[:], kg, writes=[kg_t])

    with Scope(P):
        P.banks = [P.bank(f"bk{i}", [128, 512], F32) for i in range(4)]
        P.t_c = P.sb("c_sb", [128, 8], F32)
        P.t_sc = P.sb("sc_sb", [128, 8], F32)
        P.t_screp = P.sb("screp", [128, 8, 128], F32)
        P.t_mb = P.sb("mb", [1, 2048], F32)
        P.t_mw = [P.sb(f"mw{i}", [128, 2048], F32) for i in range(2)]
        P.t_ones1 = P.sb("ones1", [1, 128], F32)
        P.t_stage = P.sb("stage", [128, 2048], F32)
        ng0 = P.sb("ng0", [128, D], F32)
        wst = [P.sb(f"wst{i}", [128, 2048], F32) for i in range(2)]
        S.op("dve", lambda e: e.memset(P.t_ones1[:], 1.0), writes=[P.t_ones1])
        S.dma("pool", ng0[:], ng[:, 0:D], writes=[ng0])
        for k in range(8):
            ws = wst[k % 2]
            S.dma("pool", ws[:], w_in[k * 128:(k + 1) * 128, :], writes=[ws])
            S.op("pool", lambda e, k=k, ws=ws: e.tensor_copy(out=wbf[:, k, :], in_=ws[:]), reads=[ws], writes=[wbf])

        def mk_consume(kind):
            def consume(stage):
                if kind == "ctx":
                    S.op("dve", lambda e: e.scalar_tensor_tensor(out=Gc[:], in0=stage[:, D:2 * D], scalar=1.0, in1=ng0[:],
                                                                 op0=ALU.add, op1=ALU.mult), reads=[stage, ng0], writes=[Gc])
                    S.op("dve", lambda e: e.tensor_copy(out=SHc[:], in_=stage[:, 0:D]), reads=[stage], writes=[SHc])
                elif kind == "0a":
                    S.dma("sp", M0a, stage[:, 0:2048], reads=[stage])
                    S.op("dve", lambda e: e.scalar_tensor_tensor(out=G1[:], in0=stage[:, D:2 * D], scalar=1.0, in1=ng0[:],
                                                                 op0=ALU.add, op1=ALU.mult), reads=[stage, ng0], writes=[G1])
                    S.op("dve", lambda e: e.tensor_copy(out=SH1[:], in_=stage[:, 0:D]), reads=[stage], writes=[SH1])
                elif kind == "0q":
                    S.dma("sp", M0q, stage[:, 0:1024], reads=[stage])
                else:
                    S.dma("sp", M1q, stage[:, 0:1536], reads=[stage])
            return consume

        adaln_table(P, ccT, mw0a, mb0a, 0, 2048, mk_consume("ctx"))
        adaln_table(P, cT, mw0a, mb0a, 0, 2048, mk_consume("0a"))
        adaln_table(P, cT, mw0q, mb0q, 0, 1024, mk_consume("0q"))
        adaln_table(P, cT, mw1q, mb1q, 0, 1536, mk_consume("1q"))

    R = 2
    def ring(name, shape, dt, n=R):
        return [P.sb(f"{name}{i}", shape, dt) for i in range(n)]
    xt = ring("xt", [128, D], F32)
    cs = ring("cs", [128, 128], F32, n=4)
    junk = ring("junk", [128, D], BF16)
    ms = ring("ms", [128, 1], F32)
    rstd = ring("rstd", [128, 1], F32)
    t1 = ring("t1", [128, D], F32)
    hb = ring("hb", [128, D], BF16)
    hT = ring("hT", [128, 8, 128], BF16)
    pr = ring("pr", [128, 2048], F32)
    sq = ring("sq", [128, 768], F32)
    ssq = ring("ssq", [128, 6], F32)
    rq = ring("rq", [128, 6], F32)
    rb = ring("rb", [128, 2048], BF16)
    ta = ring("ta", [128, 512], F32)
    tb = ring("tb", [128, 512], F32)
    tcc = ring("tcc", [128, 512], F32)
    td = ring("td", [128, 512], F32)
    qT = ring("qT", [128, 12, 128], BF16)
    vt = ring("vt", [128, 512], BF16)
    bTs = [P.bank(f"bT{i}", [128, 1024], BF16) for i in range(2)]
    bUs = [P.bank(f"bU{i}", [128, 1024], BF16) for i in range(2)]
    bP = [P.bank(f"bP{i}", [128, 512], F32) for i in range(4)]

    def do_tile(i, src_ap, rope_ap, Gt, SHt, is_ctx, tcol):
        r = i % R
        xx, csl, jk, ms_, rs_, t1_, hb_, hT_, pr_, sq_, ssq_, rq_, rb_ = (xt[r], cs[i % 4], junk[r], ms[r], rstd[r], t1[r], hb[r], hT[r],
                                                                           pr[r], sq[r], ssq[r], rq[r], rb[r])
        ta_, tb_, tc_, td_, qT_, vt_, bT, bU = ta[r], tb[r], tcc[r], td[r], qT[r], vt[r], bTs[r], bUs[r]
        S.dma("sp", xx[:], src_ap, writes=[xx])
        if not is_ctx:
            S.dma("sp", csl[:], rope_ap, writes=[csl])
        S.op("act", lambda e: e.activation(out=jk[:], in_=xx[:], func=AF.Square, scale=1.0 / 32, accum_out=ms_[:]),
             reads=[xx], writes=[jk, ms_])
        rstd_from_ms(S, ms_, eps_t, rs_)
        S.op("dve", lambda e: e.scalar_tensor_tensor(out=t1_[:], in0=xx[:], scalar=rs_[:, 0:1], in1=Gt[:],
                                                     op0=ALU.mult, op1=ALU.mult), reads=[xx, rs_, Gt], writes=[t1_])
        S.op("pool", lambda e: e.tensor_tensor(out=hb_[:], in0=t1_[:], in1=SHt[:], op=ALU.add),
             reads=[t1_, SHt], writes=[hb_])
        yield
        for k in range(8):
            S.op("pe", lambda e, k=k: e.transpose(out=bT[:, k * 128:(k + 1) * 128], in_=hb_[:, k * 128:(k + 1) * 128],
                                                  identity=idt[:]), reads=[hb_, idt], writes=[bT], inc=(k == 7))
        S.op("act", lambda e: e.copy(out=hT_[:].rearrange("p k t -> p (k t)"), in_=bT[:]), reads=[bT], writes=[hT_])
        g0 = 2 if is_ctx else 0
        for g in range(g0, 4):
            for k in range(8):
                S.op("pe", lambda e, g=g, k=k: e.matmul(bP[g][:], lhsT=hT_[:, k, :], rhs=wbf[:, k, g * 512:(g + 1) * 512],
                                                        start=(k == 0), stop=(k == 7)),
                     reads=[hT_, wbf], writes=[bP[g]], inc=(k == 7))
        for g in range(g0, 4):
            if g % 2 == 0:
                S.op("dve", lambda e, g=g: e.tensor_copy(out=pr_[:, g * 512:(g + 1) * 512], in_=bP[g][:]),
                     reads=[bP[g]], writes=[pr_])
            else:
                S.op("act", lambda e, g=g: e.copy(out=pr_[:, g * 512:(g + 1) * 512], in_=bP[g][:]),
                     reads=[bP[g]], writes=[pr_])
        yield
        S.op("pool", lambda e: e.tensor_copy(out=vt_[:, 0:256], in_=pr_[:, 1280:1536]), reads=[pr_], writes=[vt_])
        S.op("pool", lambda e: e.tensor_copy(out=vt_[:, 256:512], in_=pr_[:, 1792:2048]), reads=[pr_], writes=[vt_])
        S.dma("sp", (Vc if is_ctx else V)[tcol:tcol + 128, :], vt_[:], reads=[vt_])
        if not is_ctx:
            S.op("dve", lambda e: e.tensor_tensor(out=sq_[:, 0:512], in0=pr_[:, 0:512], in1=pr_[:, 0:512], op=ALU.mult),
                 reads=[pr_], writes=[sq_])
        else:
            S.op("dve", lambda e: e.memset(sq_[:, 0:512], 1.0), writes=[sq_])
        S.op("dve", lambda e: e.tensor_tensor(out=sq_[:, 512:768], in0=pr_[:, 1024:1280], in1=pr_[:, 1024:1280], op=ALU.mult),
             reads=[pr_], writes=[sq_])
        S.op("dve", lambda e: e.tensor_reduce(out=ssq_[:], in_=sq_[:].rearrange("p (h d) -> p h d", d=128), axis=AX.X, op=ALU.add),
             reads=[sq_], writes=[ssq_])
        S.op("act", lambda e: e.activation(out=rq_[:], in_=ssq_[:], func=AF.Ln, bias=eps_t[:], scale=1.0 / 128),
             reads=[ssq_, eps_t], writes=[rq_])
        S.op("act", lambda e: e.activation(out=rq_[:], in_=rq_[:], func=AF.Exp, scale=-0.5), reads=[rq_], writes=[rq_])
        groups = ((1024, 2, 4, kg_t),) if is_ctx else ((0, 4, 0, qg_t), (1024, 2, 4, kg_t))
        for (c0, nh, r0, gt) in groups:
            v3 = lambda c0=c0, nh=nh: pr_[:, c0:c0 + nh * 128].rearrange("p (h d) -> p h d", d=128)
            S.op("dve", lambda e, v3=v3, nh=nh, r0=r0: e.tensor_tensor(
                out=v3(), in0=v3(), in1=rq_[:, r0:r0 + nh].unsqueeze(2).to_broadcast([128, nh, 128]), op=ALU.mult),
                 reads=[pr_, rq_], writes=[pr_])
            S.op("dve", lambda e, v3=v3, nh=nh, gt=gt: e.tensor_tensor(
                out=v3(), in0=v3(), in1=gt[:].unsqueeze(1).to_broadcast([128, nh, 128]), op=ALU.mult),
                 reads=[pr_, gt], writes=[pr_])
        if is_ctx:
            S.op("pool", lambda e: e.tensor_copy(out=rb_[:, 1024:2048], in_=pr_[:, 1024:2048]), reads=[pr_], writes=[rb_])
        else:
            for (c0, nh, eng) in ((0, 4, "dve"), (512, 4, "pool"), (1024, 2, "dve"), (1536, 2, "pool")):
                def v5(t, c0=c0, nh=nh):
                    return t[:, c0:c0 + nh * 128].rearrange("p (h a s f) -> p h a s f", h=nh, a=2, s=2)
                def v4(t, nh=nh):
                    return t[:, 0:nh * 64].rearrange("p (h a f) -> p h a f", h=nh, a=2)
                def cosb(nh=nh):
                    return csl[:, 0:64].rearrange("p (a f) -> p a f", a=2).unsqueeze(1).to_broadcast([128, nh, 2, 32])
                def sinb(nh=nh):
                    return csl[:, 64:128].rearrange("p (a f) -> p a f", a=2).unsqueeze(1).to_broadcast([128, nh, 2, 32])
                S.op(eng, lambda e, v5=v5, v4=v4, cosb=cosb: e.tensor_tensor(out=v4(ta_), in0=v5(pr_)[:, :, :, 0, :], in1=cosb(), op=ALU.mult),
                     reads=[pr_, csl], writes=[ta_])
                S.op(eng, lambda e, v5=v5, v4=v4, sinb=sinb: e.tensor_tensor(out=v4(tb_), in0=v5(pr_)[:, :, :, 1, :], in1=sinb(), op=ALU.mult),
                     reads=[pr_, csl], writes=[tb_])
                S.op(eng, lambda e, v5=v5, v4=v4: e.tensor_tensor(out=v5(rb_)[:, :, :, 0, :], in0=v4(ta_), in1=v4(tb_), op=ALU.subtract),
                     reads=[ta_, tb_], writes=[rb_])
                S.op(eng, lambda e, v5=v5, v4=v4, cosb=cosb: e.tensor_tensor(out=v4(tc_), in0=v5(pr_)[:, :, :, 1, :], in1=cosb(), op=ALU.mult),
                     reads=[pr_, csl], writes=[tc_])
                S.op(eng, lambda e, v5=v5, v4=v4, sinb=sinb: e.tensor_tensor(out=v4(td_), in0=v5(pr_)[:, :, :, 0, :], in1=sinb(), op=ALU.mult),
                     reads=[pr_, csl], writes=[td_])
                S.op(eng, lambda e, v5=v5, v4=v4: e.tensor_tensor(out=v5(rb_)[:, :, :, 1, :], in0=v4(tc_), in1=v4(td_), op=ALU.add),
                     reads=[tc_, td_], writes=[rb_])
        yield
        srcs = [j * 128 for j in range(8)] + [1024, 1152, 1536, 1664]
        first = 8 if is_ctx else 0
        for j in range(first, 12):
            bank = bU if j < 8 else bT
            jj = j if j < 8 else j - 8
            S.op("pe", lambda e, j=j, bank=bank, jj=jj: e.transpose(out=bank[:, jj * 128:(jj + 1) * 128],
                                                                    in_=rb_[:, srcs[j]:srcs[j] + 128], identity=idt[:]),
                 reads=[rb_, idt], writes=[bank], inc=(j == 7 or j == 11))
        if not is_ctx:
            S.op("act", lambda e: e.copy(out=qT_[:, 0:8, :].rearrange("p j t -> p (j t)"), in_=bU[:]), reads=[bU], writes=[qT_])
        S.op("dve", lambda e: e.tensor_copy(out=qT_[:, 8:12, :].rearrange("p j t -> p (j t)"), in_=bT[:, 0:512]),
             reads=[bT], writes=[qT_])
        if is_ctx:
            S.dma("sp", KTc[:, :, tcol:tcol + 128].rearrange("j d t -> d j t"), qT_[:, 8:12, :], reads=[qT_])
        else:
            S.dma("sp", QT[:, :, tcol:tcol + 128].rearrange("j d t -> d j t"), qT_[:, 0:8, :], reads=[qT_])
            S.dma("sp", KT[:, :, tcol:tcol + 128].rearrange("j d t -> d j t"), qT_[:, 8:12, :], reads=[qT_])
        yield

    gens = [do_tile(t, ctx[t * 128:(t + 1) * 128, :], None, Gc, SHc, True, t * 128) for t in range(CTX // 128)]
    gens += [do_tile(2 + t, x[t * 128:(t + 1) * 128, :], rope[t * 128:(t + 1) * 128, :], G1, SH1, False, t * 128) for t in range(NT)]
    pipeline(gens)
    return P.finish()


def rope_table():
    t = np.arange(L)
    row = (t // 64).astype(np.float32)
    col = (t % 64).astype(np.float32)
    inv = (10000.0 ** (-np.arange(32, dtype=np.float32) / 32)).astype(np.float32)
    ang = np.stack([row[:, None] * inv, col[:, None] * inv], axis=1)
    return np.concatenate([np.cos(ang).reshape(L, 64), np.sin(ang).reshape(L, 64)], axis=1).astype(np.float32)


def run(nc, in_maps):
    res = run_bass_kernel_spmd(nc, in_maps, core_ids=list(range(NCORE)))
    return res.results


def launch_proj(inp):
    nc = build_proj()
    rope = rope_table()
    ident = np.eye(128, dtype=np.float32).astype(NPBF)
    ngb = np.ascontiguousarray(np.broadcast_to(inp["norm_g"].reshape(1, 8 * D), (128, 8 * D)))
    qgb = np.ascontiguousarray(np.broadcast_to(inp["q_norm_g"].reshape(1, 128), (128, 128)))
    kgb = np.ascontiguousarray(np.broadcast_to(inp["k_norm_g"].reshape(1, 128), (128, 128)))
    ccT = np.ascontiguousarray(inp["c_ctx"].reshape(8, 128).T)
    mw, mb = inp["mod_w"], inp["mod_b"]
    mw0a = np.ascontiguousarray(mw[0][:, 0:2048])
    mb0a = np.ascontiguousarray(mb[0][None, 0:2048])
    maps = []
    for core in range(NCORE):
        b, s = divmod(core, 4)
        maps.append(dict(
            x=np.ascontiguousarray(inp["x"][b, s * TOK:(s + 1) * TOK]),
            ctx=np.ascontiguousarray(inp["ctx"][b]),
            cT=np.ascontiguousarray(inp["c"][b].reshape(8, 128).T),
            ccT=ccT, mw0a=mw0a, mb0a=mb0a,
            mw0q=np.ascontiguousarray(mw[0][:, 2048 + 1024 * s:2048 + 1024 * (s + 1)]),
            mb0q=np.ascontiguousarray(mb[0][None, 2048 + 1024 * s:2048 + 1024 * (s + 1)]),
            mw1q=np.ascontiguousarray(mw[1][:, 1536 * s:1536 * (s + 1)]),
            mb1q=np.ascontiguousarray(mb[1][None, 1536 * s:1536 * (s + 1)]),
            ng=ngb, w_in=inp["attn_w_in"][0], qg=qgb, kg=kgb,
            rope=np.ascontiguousarray(rope[s * TOK:(s + 1) * TOK]), ident=ident))
    res = _np(run(nc, maps))
    for b in range(NB):
        grp = res[b * 4:(b + 1) * 4]
        M0 = np.concatenate([grp[0]["M0a"]] + [g["M0q"] for g in grp], axis=1)
        M1 = np.concatenate([g["M1q"] for g in grp], axis=1)
        Mfull = np.ascontiguousarray(np.stack([M0, M1], axis=1))
        for g in grp:
            g["M"] = Mfull
    return res


def sched_barrier(S):
    evs = []
    for e in ("pe", "act", "dve", "pool"):
        s = S.cur_sem[e]
        if S.cnt[s] > 0:
            evs.append((s, S.cnt[s]))
    for q in ("sp", "act", "pool"):
        for s in S.dma_ring[q]:
            if S.cnt[s] > 0:
                evs.append((s, S.cnt[s]))
    for e in S.ENGS:
        waits = []
        for ev in evs:
            S._need(e, ev, waits)
        if waits:
            S.recs[e].append((waits, None, None))


class Scope:
    def __init__(self, P):
        self.P = P

    def __enter__(self):
        self.saved = self.P.stack
        self.saved_npsum = self.P.n_psum
        self.P.stack = ExitStack()
        return self

    def __exit__(self, *a):
        sched_barrier(self.P.S)
        self.P.S.flush()
        self.P.stack.close()
        self.P.stack = self.saved
        self.P.n_psum = self.saved_npsum
        return False


SCALE = 128 ** -0.5
NKB = (CTX + L) // 128
NLB = 2 + NT + 2


def resid_tile(P, t, x_in, x_out, hT_out, lhsT_of, lhs_tile, wbf, GG1, G2, SH2, eps_t, idt,
               xx, xo, ysb, junk, ms, rstd, t1, hb, hT, bY, bT, sq_scale, t2=None):
    t2 = t2 if t2 is not None else t1
    S = P.S
    S.dma("sp", xx[:], x_in[t * 128:(t + 1) * 128, :], writes=[xx])
    for n in range(2):
        for h in range(8):
            S.op("pe", lambda e, n=n, h=h: e.matmul(bY[n][:], lhsT=lhsT_of(h), rhs=wbf[:, h, n * 512:(n + 1) * 512],
                                                    start=(h == 0), stop=(h == 7)),
                 reads=[lhs_tile, wbf], writes=[bY[n]], inc=(h == 7))
    S.op("act", lambda e: e.copy(out=ysb[:, 0:512], in_=bY[0][:]), reads=[bY[0]], writes=[ysb])
    S.op("dve", lambda e: e.tensor_copy(out=ysb[:, 512:1024], in_=bY[1][:]), reads=[bY[1]], writes=[ysb])
    S.op("act", lambda e: e.activation(out=junk[:], in_=ysb[:], func=AF.Square, scale=sq_scale, accum_out=ms[:, 0:1]),
         reads=[ysb], writes=[junk, ms])
    S.op("act", lambda e: e.activation(out=rstd[:, 0:1], in_=ms[:, 0:1], func=AF.Ln, bias=eps_t[:], scale=1.0),
         reads=[ms, eps_t], writes=[rstd])
    S.op("act", lambda e: e.activation(out=rstd[:, 0:1], in_=rstd[:, 0:1], func=AF.Exp, scale=-0.5), reads=[rstd], writes=[rstd])
    yield
    S.op("dve", lambda e: e.scalar_tensor_tensor(out=t1[:], in0=ysb[:], scalar=rstd[:, 0:1], in1=GG1[:],
                                                 op0=ALU.mult, op1=ALU.mult), reads=[ysb, rstd, GG1], writes=[t1])
    S.op("pool", lambda e: e.tensor_tensor(out=xo[:], in0=t1[:], in1=xx[:], op=ALU.add), reads=[t1, xx], writes=[xo])
    S.dma("sp", x_out[t * 128:(t + 1) * 128, :], xo[:], reads=[xo])
    S.op("act", lambda e: e.activation(out=junk[:], in_=xo[:], func=AF.Square, scale=1.0 / 32, accum_out=ms[:, 1:2]),
         reads=[xo], writes=[junk, ms])
    S.op("act", lambda e: e.activation(out=rstd[:, 1:2], in_=ms[:, 1:2], func=AF.Ln, bias=eps_t[:], scale=1.0),
         reads=[ms, eps_t], writes=[rstd])
    S.op("act", lambda e: e.activation(out=rstd[:, 1:2], in_=rstd[:, 1:2], func=AF.Exp, scale=-0.5), reads=[rstd], writes=[rstd])
    yield
    S.op("dve", lambda e: e.scalar_tensor_tensor(out=t2[:], in0=xo[:], scalar=rstd[:, 1:2], in1=G2[:],
                                                 op0=ALU.mult, op1=ALU.mult), reads=[xo, rstd, G2], writes=[t2])
    S.op("pool", lambda e: e.tensor_tensor(out=hb[:], in0=t2[:], in1=SH2[:], op=ALU.add), reads=[t2, SH2], writes=[hb])
    yield
    for k in range(8):
        S.op("pe", lambda e, k=k: e.transpose(out=bT[:, k * 128:(k + 1) * 128], in_=hb[:, k * 128:(k + 1) * 128],
                                              identity=idt[:]), reads=[hb, idt], writes=[bT], inc=(k == 7))
    S.op("act", lambda e: e.copy(out=hT[:].rearrange("p k t -> p (k t)"), in_=bT[:]), reads=[bT], writes=[hT])
    S.dma("sp", hT_out[:, :, t * 128:(t + 1) * 128].rearrange("k p t -> p k t"), hT[:], reads=[hT])
    yield


def pipeline(gens):
    live = []
    for g in gens:
        live.insert(0, g)
        keep = []
        for gg in live:
            try:
                next(gg)
                keep.append(gg)
            except StopIteration:
                pass
        live = keep
    while live:
        keep = []
        for gg in live:
            try:
                next(gg)
                keep.append(gg)
            except StopIteration:
                pass
        live = keep


def build_attn():
    P = Prog()
    S = P.S
    QT = P.din("QT", [8, 128, TOK], BF16)
    KaT = P.din("KaT", [2, 128, NKB * 128], BF16)
    Va = P.din("Va", [2, 128, NKB, 128], BF16)
    KbT = P.din("KbT", [2, 128, NLB * 128], BF16)
    Vb = P.din("Vb", [2, 128, NLB, 128], BF16)
    bm = P.din("bm", [4, 128, 256], BF16)
    x = P.din("x", [TOK, D], F32)
    M = P.din("M", [128, 6 * D], F32)
    ng = P.din("ng", [128, 8 * D], F32)
    w_out = P.din("w_out", [D, D], F32)
    sink = P.din("sink", [128, 4], F32)
    ident = P.din("ident", [128, 128], BF16)
    x1o = P.dout("x1", [TOK, D], F32)
    h2T = P.dout("h2T", [8, 128, TOK], BF16)

    OT = P.sb("OT", [128, 8, TOK], BF16)
    ones = P.sb("ones", [128, 128], BF16)
    onesf = P.sb("onesf", [128, 128], F32)
    idt = P.sb("ident", [128, 128], BF16)
    eps_t = P.sb("eps", [128, 1], F32)
    S.op("dve", lambda e: e.memset(ones[:], 1.0), writes=[ones])
    S.op("dve", lambda e: e.memset(onesf[:], 1.0), writes=[onesf])
    S.op("dve", lambda e: e.memset(eps_t[:], EPS), writes=[eps_t])
    S.dma("pool", idt[:], ident, writes=[idt])

    with Scope(P):
        bS = [P.bank(f"bS{i}", [128, 1024], F32, nbanks=2) for i in range(3)]
        bO = [P.bank(f"bO{i}", [128, 512], F32) for i in range(1)]
        bD = [P.bank(f"bD{i}", [128, 512], F32) for i in range(1)]
        KT = P.sb("KT", [128, NKB * 128], BF16)
        Vt = P.sb("Vt", [128, NKB, 128], BF16)
        Qg = [P.sb(f"Qg{i}", [128, 2, 256], BF16) for i in range(2)]
        pt = [P.sb(f"pt{i}", [128, 1024], BF16) for i in range(8)]
        osb = [P.sb(f"osb{i}", [128, 512], F32) for i in range(2)]
        accd = [P.sb(f"accd{i}", [128, 512], F32) for i in range(2)]
        accp = [P.sb(f"accp{i}", [128, 512], F32) for i in range(2)]
        rden = P.sb("rden", [128, 512], F32)
        NPAIR = NKB // 2
        gi = 0
        for kv in range(2):
            for c in range(5):
                S.dma("sp", KT[:, c * 3328:(c + 1) * 3328], KaT[kv, :, c * 3328:(c + 1) * 3328], writes=[KT])
                S.dma("pool", Vt[:, c * 26:(c + 1) * 26, :], Va[kv, :, c * 26:(c + 1) * 26, :], writes=[Vt])
            for g in range(TOK // 256):
                q = Qg[gi % 2]
                o_b = bO[0]
                d_b = bD[0]
                a_d = accd[gi % 2]
                a_p = accp[gi % 2]
                gi += 1
                S.dma("sp", q[:], QT[2 * kv:2 * kv + 2, :, g * 256:(g + 1) * 256].rearrange("h d t -> d h t"), writes=[q])
                qf = q[:].rearrange("p h t -> p (h t)")

                def smm(jp, qf=qf, q=q):
                    sb_ = bS[jp % 3]
                    for u in range(2):
                        j = 2 * jp + u
                        S.op("pe", lambda e, j=j, u=u, sb_=sb_: e.matmul(sb_[:, u * 512:(u + 1) * 512], lhsT=KT[:, j * 128:(j + 1) * 128],
                                                                         rhs=qf, start=True, stop=True),
                             reads=[KT, q], writes=[sb_], inc=(u == 1))
                smm(0)
                smm(1)
                nd = 0
                npl = 0
                npe = 0
                for jp in range(NPAIR):
                    if jp + 2 < NPAIR:
                        smm(jp + 2)
                    p = pt[jp % 8]
                    sb_ = bS[jp % 3]
                    S.op("act", lambda e, p=p, sb_=sb_: e.activation(out=p[:], in_=sb_[:], func=AF.Exp, scale=SCALE),
                         reads=[sb_], writes=[p])
                    for u in range(2):
                        j = 2 * jp + u
                        S.op("pe", lambda e, j=j, u=u, p=p, o_b=o_b: e.matmul(o_b[:], lhsT=Vt[:, j, :], rhs=p[:, u * 512:(u + 1) * 512],
                                                                             start=(j == 0), stop=(j == NKB - 1)),
                             reads=[Vt, p], writes=[o_b], inc=(u == 1))
                        r6 = j % 10
                        if r6 in (2, 7):
                            S.op("pe", lambda e, u=u, p=p, d_b=d_b, npe=npe: e.matmul(d_b[:], lhsT=ones[:], rhs=p[:, u * 512:(u + 1) * 512],
                                                                                    start=(npe == 0), stop=False),
                                 reads=[ones, p], writes=[d_b])
                            npe += 1
                        elif r6 in (0, 3, 5, 6, 9):
                            if nd == 0:
                                S.op("dve", lambda e, u=u, p=p, a_d=a_d: e.tensor_copy(out=a_d[:], in_=p[:, u * 512:(u + 1) * 512]),
                                     reads=[p], writes=[a_d])
                            else:
                                S.op("dve", lambda e, u=u, p=p, a_d=a_d: e.tensor_tensor(out=a_d[:], in0=a_d[:], in1=p[:, u * 512:(u + 1) * 512],
                                                                                         op=ALU.add), reads=[p, a_d], writes=[a_d], self_sync=False)
                            nd += 1
                        else:
                            if npl == 0:
                                S.op("pool", lambda e, u=u, p=p, a_p=a_p: e.tensor_copy(out=a_p[:], in_=p[:, u * 512:(u + 1) * 512]),
                                     reads=[p], writes=[a_p])
                            else:
                                S.op("pool", lambda e, u=u, p=p, a_p=a_p: e.tensor_tensor(out=a_p[:], in0=a_p[:], in1=p[:, u * 512:(u + 1) * 512],
                                                                                          op=ALU.add), reads=[p, a_p], writes=[a_p], self_sync=False)
                            npl += 1
                o_s = osb[gi % 2]
                S.op("act", lambda e, o_b=o_b, o_s=o_s: e.copy(out=o_s[:], in_=o_b[:]), reads=[o_b], writes=[o_s])
                S.op("pe", lambda e, d_b=d_b, a_d=a_d: e.matmul(d_b[:], lhsT=onesf[:], rhs=a_d[:], start=False, stop=False),
                     reads=[onesf, a_d], writes=[d_b], inc=False)
                S.op("pe", lambda e, d_b=d_b, a_p=a_p: e.matmul(d_b[:], lhsT=onesf[:], rhs=a_p[:], start=False, stop=True),
                     reads=[onesf, a_p], writes=[d_b])
                S.op("dve", lambda e, d_b=d_b: e.reciprocal(out=rden[:], in_=d_b[:]), reads=[d_b], writes=[rden])
                S.op("dve", lambda e, o_s=o_s, kv=kv, g=g: e.tensor_tensor(
                    out=OT[:, 2 * kv:2 * kv + 2, g * 256:(g + 1) * 256], in0=o_s[:].rearrange("p (h t) -> p h t", h=2),
                    in1=rden[:].rearrange("p (h t) -> p h t", h=2), op=ALU.mult), reads=[o_s, rden], writes=[OT])

    with Scope(P):
        KTb = [P.sb(f"KTb{i}", [128, NLB * 128], BF16) for i in range(2)]
        Vtb = [P.sb(f"Vtb{i}", [128, NLB, 128], BF16) for i in range(2)]
        QBs = [P.sb(f"QB{i}", [128, 2, TOK], BF16) for i in range(2)]
        mk = P.sb("mk", [128, 4, 256], BF16)
        sk = P.sb("sk", [128, 4], F32)
        ske = P.sb("ske", [128, 4], F32)
        ptb = [P.sb(f"ptb{i}", [128, 5, 256], BF16) for i in range(2)]
        rdb = [P.sb(f"rdenb{i}", [128, 256], F32) for i in range(2)]
        S.dma("sp", mk[:], bm.rearrange("m p c -> p m c"), writes=[mk])
        S.dma("sp", sk[:], sink, writes=[sk])
        S.op("act", lambda e: e.activation(out=ske[:], in_=sk[:], func=AF.Exp), reads=[sk], writes=[ske])
        bSb = [P.bank(f"bSb{i}", [128, 1536], F32, nbanks=3) for i in range(2)]
        bOD = [P.bank(f"bOD{i}", [128, 512], F32) for i in range(2)]
        for kv in range(2):
            S.dma("sp", KTb[kv][:], KbT[kv], writes=[KTb[kv]])
            S.dma("pool", Vtb[kv][:], Vb[kv], writes=[Vtb[kv]])
            S.dma("sp", QBs[kv][:], QT[4 + 2 * kv:6 + 2 * kv].rearrange("h d t -> d h t"), writes=[QBs[kv]])

        def win_iter(it, kv, n):
            KT_, Vt_, QB_ = KTb[kv], Vtb[kv], QBs[kv]
            sb_, od, p, rden = bSb[it % 2], bOD[it % 2], ptb[it % 2], rdb[it % 2]
            kblocks = [0, 1, n + 2, n + 3, n + 4]
            for i, kb in enumerate(kblocks):
                mask = None
                if i == 2:
                    mask = 2 if n == 0 else 0
                if i == 4:
                    mask = 3 if n == NT - 1 else 1
                S.op("pe", lambda e, kb=kb, i=i, mask=mask: e.matmul(
                    sb_[:, i * 256:(i + 1) * 256].rearrange("p (h t) -> p h t", h=2), lhsT=KT_[:, kb * 128:(kb + 1) * 128],
                    rhs=QB_[:, :, n * 128:(n + 1) * 128], start=True, stop=(mask is None)),
                    reads=[KT_, QB_], writes=[sb_], inc=False)
                if mask is not None:
                    S.op("pe", lambda e, i=i, mask=mask: e.matmul(sb_[:, i * 256:(i + 1) * 256], lhsT=idt[:], rhs=mk[:, mask, :],
                                                                  start=False, stop=True),
                         reads=[idt, mk], writes=[sb_], inc=(i == 4))
            yield
            S.op("act", lambda e: e.activation(out=p[:].rearrange("p i c -> p (i c)"), in_=sb_[:, 0:1280], func=AF.Exp, scale=SCALE),
                 reads=[sb_], writes=[p])
            yield
            for i, kb in enumerate(kblocks):
                S.op("pe", lambda e, kb=kb, i=i: e.matmul(od[:, 0:256], lhsT=Vt_[:, kb, :], rhs=p[:, i, :],
                                                          start=(i == 0), stop=(i == 4)),
                     reads=[Vt_, p], writes=[od], inc=False)
            for i, kb in enumerate(kblocks):
                S.op("pe", lambda e, i=i: e.matmul(od[:, 256:512], lhsT=ones[:], rhs=p[:, i, :],
                                                   start=(i == 0), stop=(i == 4)),
                     reads=[ones, p], writes=[od], inc=(i == 4))
            for h in range(2):
                S.op("dve", lambda e, h=h: e.tensor_scalar(
                    out=rden[:, h * 128:(h + 1) * 128], in0=od[:, 256 + h * 128:256 + (h + 1) * 128],
                    scalar1=ske[:, 2 * kv + h:2 * kv + h + 1], scalar2=None, op0=ALU.add), reads=[od, ske], writes=[rden])
            S.op("dve", lambda e: e.reciprocal(out=rden[:], in_=rden[:]), reads=[rden], writes=[rden])
            S.op("dve", lambda e: e.tensor_tensor(
                out=OT[:, 4 + 2 * kv:6 + 2 * kv, n * 128:(n + 1) * 128], in0=od[:, 0:256].rearrange("p (h t) -> p h t", h=2),
                in1=rden[:].rearrange("p (h t) -> p h t", h=2), op=ALU.mult), reads=[od, rden], writes=[OT])
            yield

        its = [(kv, n) for kv in range(2) for n in range(NT)]
        pipeline(win_iter(i, kv, n) for i, (kv, n) in enumerate(its))

    with Scope(P):
        wbf = P.sb("wobf", [128, 8, D], BF16)
        wst = [P.sb(f"wost{i}", [128, D], F32) for i in range(2)]
        for k in range(8):
            ws = wst[k % 2]
            S.dma("pool", ws[:], w_out[k * 128:(k + 1) * 128, :], writes=[ws])
            S.op("pool", lambda e, k=k, ws=ws: e.tensor_copy(out=wbf[:, k, :], in_=ws[:]), reads=[ws], writes=[wbf])
        GG1 = P.sb("GG1", [128, D], F32)
        G2 = P.sb("G2", [128, D], F32)
        SH2 = P.sb("SH2", [128, D], F32)
        tmpa = P.sb("tmpa", [128, D], F32)
        tmpb = P.sb("tmpb", [128, D], F32)
        S.dma("sp", tmpa[:], M[:, 2 * D:3 * D], writes=[tmpa])
        S.dma("sp", tmpb[:], ng[:, 1 * D:2 * D], writes=[tmpb])
        S.op("dve", lambda e: e.tensor_tensor(out=GG1[:], in0=tmpa[:], in1=tmpb[:], op=ALU.mult), reads=[tmpa, tmpb], writes=[GG1])
        S.dma("sp", tmpa[:], M[:, 4 * D:5 * D], reads=[], writes=[tmpa])
        S.dma("sp", tmpb[:], ng[:, 2 * D:3 * D], writes=[tmpb])
        S.op("dve", lambda e: e.scalar_tensor_tensor(out=G2[:], in0=tmpa[:], scalar=1.0, in1=tmpb[:], op0=ALU.add, op1=ALU.mult),
             reads=[tmpa, tmpb], writes=[G2])
        S.dma("sp", SH2[:], M[:, 3 * D:4 * D], writes=[SH2])
        RD = 3
        xt = [P.sb(f"xt{i}", [128, D], F32) for i in range(RD)]
        ysb_ = [P.sb(f"ysb{i}", [128, D], F32) for i in range(RD)]
        junk_ = [P.sb(f"junk{i}", [128, D], BF16) for i in range(RD)]
        ms_ = [P.sb(f"ms{i}", [128, 2], F32) for i in range(RD)]
        rstd_ = [P.sb(f"rstd{i}", [128, 2], F32) for i in range(RD)]
        t1_ = [P.sb(f"t1{i}", [128, D], F32) for i in range(RD)]
        t2_ = [P.sb(f"t2{i}", [128, D], F32) for i in range(RD)]
        x1 = [P.sb(f"x1_{i}", [128, D], F32) for i in range(RD)]
        hb_ = [P.sb(f"hb{i}", [128, D], BF16) for i in range(RD)]
        hT_ = [P.sb(f"hT{i}", [128, 8, 128], BF16) for i in range(RD)]
        bYs = [[P.bank(f"bY{i}_{n}", [128, 512], F32) for n in range(2)] for i in range(2)]
        bTs = [P.bank(f"bT{i}", [128, 1024], BF16) for i in range(2)]
        pipeline(resid_tile(P, t, x, x1o, h2T, lambda h, t=t: OT[:, h, t * 128:(t + 1) * 128], OT, wbf, GG1, G2, SH2, eps_t, idt,
                            xt[t % RD], x1[t % RD], ysb_[t % RD], junk_[t % RD], ms_[t % RD], rstd_[t % RD], t1_[t % RD], hb_[t % RD], hT_[t % RD],
                            bYs[t % 2], bTs[t % 2], 1.0 / 32, t2_[t % RD]) for t in range(NT))
    return P.finish()


def window_masks(s):
    a = np.arange(128)[:, None]
    b = np.arange(128)[None, :]
    NEG = -30000.0
    mL = np.where(a >= b, 0.0, NEG).astype(np.float32)
    mR = np.where(a <= b, 0.0, NEG).astype(np.float32)
    full = np.full((128, 128), NEG, np.float32)
    ms = [mL, mR, full if s == 0 else mL, full if s == 3 else mR]
    return np.stack([np.concatenate([m, m], axis=1) for m in ms]).astype(NPBF)


def launch_attn(inp, r1):
    nc = build_attn()
    ident = np.eye(128, dtype=np.float32).astype(NPBF)
    ngb = np.ascontiguousarray(np.broadcast_to(inp["norm_g"].reshape(1, 8 * D), (128, 8 * D)))
    sinkb = np.ascontiguousarray(np.broadcast_to(inp["sink"].reshape(1, 4), (128, 4)))
    maps = []
    for core in range(NCORE):
        b, s = divmod(core, 4)
        grp = [r1[b * 4 + i] for i in range(4)]
        KT_all = np.concatenate([grp[0]["KTc"]] + [g["KT"] for g in grp], axis=2)
        V_all = np.concatenate([grp[0]["Vc"]] + [g["V"] for g in grp], axis=0)
        KaT = np.ascontiguousarray(KT_all[0:2])
        Va = np.ascontiguousarray(V_all[:, 0:256].reshape(NKB, 128, 2, 128).transpose(2, 1, 0, 3))
        zk = np.zeros((4, 128, 128), NPBF)
        zv = np.zeros((128, 512), NPBF)
        lo, hi = s * TOK + CTX - 128, (s + 1) * TOK + CTX + 128
        kt_parts = [KT_all[:, :, 0:CTX]]
        v_parts = [V_all[0:CTX]]
        if s == 0:
            kt_parts += [zk, KT_all[:, :, CTX:hi]]
            v_parts += [zv, V_all[CTX:hi]]
        elif s == 3:
            kt_parts += [KT_all[:, :, lo:], zk]
            v_parts += [V_all[lo:], zv]
        else:
            kt_parts += [KT_all[:, :, lo:hi]]
            v_parts += [V_all[lo:hi]]
        KbT = np.ascontiguousarray(np.concatenate(kt_parts, axis=2)[2:4])
        Vloc = np.concatenate(v_parts, axis=0)
        Vb = np.ascontiguousarray(Vloc[:, 256:512].reshape(NLB, 128, 2, 128).transpose(2, 1, 0, 3))
        maps.append(dict(QT=r1[core]["QT"], KaT=KaT, Va=Va, KbT=KbT, Vb=Vb, bm=window_masks(s),
                         x=np.ascontiguousarray(inp["x"][b, s * TOK:(s + 1) * TOK]),
                         M=np.ascontiguousarray(r1[core]["M"][:, 0, :]), ng=ngb, w_out=inp["attn_w_out"][0],
                         sink=sinkb, ident=ident))
    return run(nc, maps)


WIN = 256
NWIN = TOK // WIN
NFC = DFF // 128


def build_ffn(emit_next):
    P = Prog()
    S = P.S
    hT_in = P.din("hT", [8, 128, TOK + 2], BF16)
    x1 = P.din("x1", [TOK, D], F32)
    w_up = P.din("w_up", [D, 2 * DFF], F32)
    w_dn = P.din("w_dn", [DFF, D], F32)
    cw_d = P.din("cw", [128, 2 * NFC, 3], F32)
    cb_d = P.din("cb", [128, 2 * NFC], F32)
    Mg2 = P.din("Mg2", [128, D], F32)
    ng3 = P.din("ng3", [128, D], F32)
    if emit_next:
        Mn = P.din("Mn", [128, 2 * D], F32)
        ngn = P.din("ngn", [128, D], F32)
        h1n = P.dout("h1n", [TOK, D], BF16)
    x2o = P.dout("x2", [TOK, D], F32)

    banks = [P.bank(f"bk{i}", [128, 512], F32) for i in range(6)]
    eps_t = P.sb("eps", [128, 1], F32)
    S.op("dve", lambda e: e.memset(eps_t[:], EPS), writes=[eps_t])
    wup = P.sb("wup", [128, 8, 2 * DFF], BF16)
    wdn = P.sb("wdn", [128, NFC, D], BF16)
    stg = [P.sb(f"stg{i}", [128, 704], F32) for i in range(3)]
    cw = P.sb("cw", [128, 2 * NFC, 3], F32)
    cb = P.sb("cb", [128, 2 * NFC], F32)
    GG2 = P.sb("GG2", [128, D], F32)
    xt = [P.sb(f"xt{i}", [128, D], F32) for i in range(2)]
    t1 = P.sb("t1", [128, D], F32)
    S.dma("sp", cw[:], cw_d, writes=[cw])
    S.dma("sp", cb[:], cb_d, writes=[cb])
    S.dma("sp", t1[:], Mg2, writes=[t1])
    S.dma("sp", xt[0][:], ng3, writes=[xt[0]])
    S.op("dve", lambda e: e.tensor_tensor(out=GG2[:], in0=t1[:], in1=xt[0][:], op=ALU.mult), reads=[t1, xt[0]], writes=[GG2])
    if emit_next:
        G1n = P.sb("G1n", [128, D], F32)
        SH1n = P.sb("SH1n", [128, D], F32)
        S.dma("sp", t1[:], Mn[:, D:2 * D], writes=[t1])
        S.dma("sp", xt[1][:], ngn, writes=[xt[1]])
        S.op("dve", lambda e: e.scalar_tensor_tensor(out=G1n[:], in0=t1[:], scalar=1.0, in1=xt[1][:], op0=ALU.add, op1=ALU.mult),
             reads=[t1, xt[1]], writes=[G1n])
        S.dma("sp", SH1n[:], Mn[:, 0:D], writes=[SH1n])
    ci = 0
    engs = ("dve", "pool", "act")
    def cast(dst_ap, src_ap, dst_tile):
        nonlocal ci
        st_ = stg[ci % 3]
        eng = engs[ci % 3]
        S.dma("sp" if ci % 2 == 0 else "pool", st_[:, 0:src_ap.shape[1]], src_ap, writes=[st_])
        n = src_ap.shape[1]
        if eng == "act":
            S.op("act", lambda e: e.copy(out=dst_ap, in_=st_[:, 0:n]), reads=[st_], writes=[dst_tile])
        else:
            S.op(eng, lambda e: e.tensor_copy(out=dst_ap, in_=st_[:, 0:n]), reads=[st_], writes=[dst_tile])
        ci += 1
    for k in range(8):
        for pc in range(8):
            cast(wup[:, k, pc * 704:(pc + 1) * 704], w_up[k * 128:(k + 1) * 128, pc * 704:(pc + 1) * 704], wup)
    for c in range(NFC):
        for pc in range(2):
            cast(wdn[:, c, pc * 512:(pc + 1) * 512], w_dn[c * 128:(c + 1) * 128, pc * 512:(pc + 1) * 512], wdn)

    hw = [P.sb(f"hw{i}", [128, 8, WIN + 2], BF16) for i in range(2)]
    aT = P.sb("aT", [128, NFC, WIN], BF16)
    ag = [P.sb(f"ag{i}", [128, WIN], F32) for i in range(2)]
    av = [P.sb(f"av{i}", [128, WIN], F32) for i in range(2)]
    sg = [P.sb(f"sg{i}", [128, WIN], F32) for i in range(2)]
    junk = P.sb("junk", [128, D], BF16)
    ms2 = P.sb("ms2", [128, 2], F32)
    ms = P.sb("ms", [128, 1], F32)
    rstd = P.sb("rstd", [128, 1], F32)
    hb = P.sb("hb", [128, D], BF16)
    bG = banks[0:2]
    bV = banks[2:4]
    bY = banks[4:6]
    pi = 0
    ti = 0
    for w in range(NWIN):
        h = hw[w % 2]
        S.dma("sp", h[:], hT_in[:, :, w * WIN:w * WIN + WIN + 2].rearrange("k p t -> p k t"), writes=[h])
        for c in range(NFC):
            g_b = bG[pi % 2]
            v_b = bV[pi % 2]
            a_g = ag[pi % 2]
            a_v = av[pi % 2]
            s_g = sg[pi % 2]
            pi += 1
            for (bk, col0) in ((g_b, c * 128), (v_b, DFF + c * 128)):
                for k in range(8):
                    S.op("pe", lambda e, bk=bk, col0=col0, k=k, h=h: e.matmul(bk[:, 0:WIN + 2], lhsT=wup[:, k, col0:col0 + 128],
                                                                             rhs=h[:, k, :], start=(k == 0), stop=(k == 7)),
                         reads=[wup, h], writes=[bk], inc=(k == 7))
            for (bk, a_, cc) in ((g_b, a_g, c), (v_b, a_v, NFC + c)):
                S.op("act", lambda e, bk=bk, a_=a_, cc=cc: e.activation(out=a_[:], in_=bk[:, 1:WIN + 1], func=AF.Identity,
                                                                       bias=cb[:, cc:cc + 1], scale=cw[:, cc, 1:2]),
                     reads=[bk, cb, cw], writes=[a_])
                S.op("dve", lambda e, bk=bk, a_=a_, cc=cc: e.scalar_tensor_tensor(out=a_[:], in0=bk[:, 0:WIN], scalar=cw[:, cc, 0:1],
                                                                                 in1=a_[:], op0=ALU.mult, op1=ALU.add),
                     reads=[bk, cw, a_], writes=[a_])
                S.op("dve", lambda e, bk=bk, a_=a_, cc=cc: e.scalar_tensor_tensor(out=a_[:], in0=bk[:, 2:WIN + 2], scalar=cw[:, cc, 2:3],
                                                                                 in1=a_[:], op0=ALU.mult, op1=ALU.add),
                     reads=[bk, cw, a_], writes=[a_])
            S.op("act", lambda e, a_g=a_g, s_g=s_g: e.activation(out=s_g[:], in_=a_g[:], func=AF.Silu), reads=[a_g], writes=[s_g])
            S.op("pool", lambda e, s_g=s_g, a_v=a_v, c=c: e.tensor_tensor(out=aT[:, c, :], in0=s_g[:], in1=a_v[:], op=ALU.mult),
                 reads=[s_g, a_v], writes=[aT])
        for tt in range(WIN // 128):
            t = w * (WIN // 128) + tt
            xx = xt[ti % 2]
            ti += 1
            S.dma("sp", xx[:], x1[t * 128:(t + 1) * 128, :], writes=[xx])
            for n in range(2):
                for c in range(NFC):
                    S.op("pe", lambda e, n=n, c=c, tt=tt: e.matmul(bY[n][:], lhsT=aT[:, c, tt * 128:(tt + 1) * 128],
                                                                   rhs=wdn[:, c, n * 512:(n + 1) * 512], start=(c == 0), stop=(c == NFC - 1)),
                         reads=[aT, wdn], writes=[bY[n]], inc=(c == NFC - 1))
            for n in range(2):
                S.op("act", lambda e, n=n: e.activation(out=junk[:, 0:512], in_=bY[n][:], func=AF.Square, scale=1.0 / 32,
                                                        accum_out=ms2[:, n:n + 1]), reads=[bY[n]], writes=[junk, ms2])
            S.op("dve", lambda e: e.tensor_tensor(out=ms[:], in0=ms2[:, 0:1], in1=ms2[:, 1:2], op=ALU.add), reads=[ms2], writes=[ms])
            rstd_from_ms(S, ms, eps_t, rstd)
            for n in range(2):
                S.op("dve", lambda e, n=n: e.scalar_tensor_tensor(out=t1[:, n * 512:(n + 1) * 512], in0=bY[n][:], scalar=rstd[:, 0:1],
                                                                  in1=GG2[:, n * 512:(n + 1) * 512], op0=ALU.mult, op1=ALU.mult),
                     reads=[bY[n], rstd, GG2], writes=[t1])
            S.op("pool", lambda e, xx=xx: e.tensor_tensor(out=t1[:], in0=t1[:], in1=xx[:], op=ALU.add), reads=[t1, xx], writes=[t1])
            S.dma("pool", x2o[t * 128:(t + 1) * 128, :], t1[:], reads=[t1])
            if emit_next:
                S.op("act", lambda e: e.activation(out=junk[:], in_=t1[:], func=AF.Square, scale=1.0 / 32, accum_out=ms[:]),
                     reads=[t1], writes=[junk, ms])
                rstd_from_ms(S, ms, eps_t, rstd)
                S.op("dve", lambda e, xx=xx: e.scalar_tensor_tensor(out=xx[:], in0=t1[:], scalar=rstd[:, 0:1], in1=G1n[:],
                                                                    op0=ALU.mult, op1=ALU.mult), reads=[t1, rstd, G1n], writes=[xx])
                S.op("pool", lambda e, xx=xx: e.tensor_tensor(out=hb[:], in0=xx[:], in1=SH1n[:], op=ALU.add), reads=[xx, SH1n], writes=[hb])
                S.dma("pool", h1n[t * 128:(t + 1) * 128, :], hb[:], reads=[hb])
    return P.finish()


def bc(v):
    return np.ascontiguousarray(np.broadcast_to(np.asarray(v).reshape(1, -1), (128, np.asarray(v).size)))


def launch_ffn(inp, layer, hT_list, x1_list, M_list, emit_next):
    nc = build_ffn(emit_next)
    cw = np.ascontiguousarray(inp["ffn_conv_w"][layer].T.reshape(2 * NFC, 128, 3).transpose(1, 0, 2))
    cb = np.ascontiguousarray(inp["ffn_conv_b"][layer].reshape(2 * NFC, 128).T)
    maps = []
    z = np.zeros((8, 128, 1), NPBF)
    for core in range(NCORE):
        b, s = divmod(core, 4)
        lo = hT_list[core - 1][:, :, -1:] if s > 0 else z
        hi = hT_list[core + 1][:, :, 0:1] if s < 3 else z
        m = dict(hT=np.ascontiguousarray(np.concatenate([lo, hT_list[core], hi], axis=2)), x1=x1_list[core],
                 w_up=inp["ffn_w_up"][layer], w_dn=inp["ffn_w_down"][layer], cw=cw, cb=cb,
                 Mg2=np.ascontiguousarray(M_list[core][:, layer, 5 * D:6 * D]), ng3=bc(inp["norm_g"][layer, 3]))
        if emit_next:
            m["Mn"] = np.ascontiguousarray(M_list[core][:, layer + 1, 0:2 * D])
            m["ngn"] = bc(inp["norm_g"][layer + 1, 0])
        maps.append(m)
    return run(nc, maps)


def build_fnet():
    P = Prog()
    S = P.S
    h_all = P.din("h_all", [L, D], BF16)
    x2 = P.din("x2", [TOK, D], F32)
    Dcs_d = P.din("Dcs", [128, 256], BF16)
    E_d = P.din("E", [16, 128, 8, 128], BF16)
    CS_d = P.din("CS", [128, 2, 2, 2, 128], BF16)
    w_f = P.din("w_f", [D, D], F32)
    M = P.din("M", [128, 6 * D], F32)
    ng = P.din("ng", [128, 4 * D], F32)
    ident = P.din("ident", [128, 128], BF16)
    x3o = P.dout("x3", [TOK, D], F32)
    h2T = P.dout("h2T", [8, 128, TOK], BF16)

    FT = P.sb("FT", [128, 8, TOK], BF16)
    idt = P.sb("ident", [128, 128], BF16)
    eps_t = P.sb("eps", [128, 1], F32)
    S.op("dve", lambda e: e.memset(eps_t[:], EPS), writes=[eps_t])
    S.dma("pool", idt[:], ident, writes=[idt])
    hv = h_all.rearrange("(t2 t1) c -> t2 t1 c", t1=128)

    with Scope(P):
        banks = [P.bank(f"bk{i}", [128, 512], F32) for i in range(8)]
        Hc = P.sb("Hc", [128, 128, 128], BF16)
        B = P.sb("B", [128, 2, 128, 128], BF16)
        GT = P.sb("GT", [128, 2, 2, TOK], BF16)
        Eg = [P.sb(f"Eg{i}", [128, 8, 128], BF16) for i in range(2)]
        Dcs = P.sb("Dcs", [128, 256], BF16)
        CS = P.sb("CS", [128, 2, 2, 2, 128], BF16)
        S.dma("sp", Dcs[:], Dcs_d, writes=[Dcs])
        S.dma("sp", CS[:], CS_d, writes=[CS])
        ei = 0
        bi = 0
        for cb in range(8):
            cbl = cb % 2
            for q in range(4):
                S.dma("sp" if q % 2 == 0 else "pool", Hc[:, q * 32:(q + 1) * 32, :],
                      hv[:, q * 32:(q + 1) * 32, cb * 128:(cb + 1) * 128], writes=[Hc])
            for cp in range(64):
                bk = banks[bi % 4]
                bi += 1
                for j in range(2):
                    ch = 2 * cp + j
                    S.op("pe", lambda e, bk=bk, j=j, ch=ch: e.matmul(bk[:, j * 256:(j + 1) * 256], lhsT=Hc[:, :, ch], rhs=Dcs[:],
                                                                     start=True, stop=True),
                         reads=[Hc, Dcs], writes=[bk], inc=(j == 1))
                eng = "act" if cp % 2 == 0 else "dve"
                outv = lambda cp=cp: B[:].rearrange("p r k c -> p (r k) c")[:, :, 2 * cp:2 * cp + 2]
                inv = lambda bk=bk: bk[:].rearrange("p (j q) -> p q j", j=2)
                if eng == "act":
                    S.op("act", lambda e, outv=outv, inv=inv: e.copy(out=outv(), in_=inv()), reads=[bk], writes=[B])
                else:
                    S.op("dve", lambda e, outv=outv, inv=inv: e.tensor_copy(out=outv(), in_=inv()), reads=[bk], writes=[B])
            for kg in range(16):
                eg = Eg[ei % 2]
                ei += 1
                S.dma("sp", eg[:], E_d[kg], writes=[eg])
                bk = banks[4 + kg % 2]
                for j in range(8):
                    k2 = 8 * kg + j
                    S.op("pe", lambda e, bk=bk, j=j, k2=k2, eg=eg: e.matmul(bk[:, j * 64:(j + 1) * 64], lhsT=B[:, 0, k2, :],
                                                                           rhs=eg[:, j, 0:64], start=True, stop=False),
                         reads=[B, eg], writes=[bk], inc=False)
                    S.op("pe", lambda e, bk=bk, j=j, k2=k2, eg=eg: e.matmul(bk[:, j * 64:(j + 1) * 64], lhsT=B[:, 1, k2, :],
                                                                           rhs=eg[:, j, 64:128], start=False, stop=True),
                         reads=[B, eg], writes=[bk], inc=(j == 7))
                outv = lambda cbl=cbl, kg=kg: GT[:, cbl, :, :].rearrange("p r (a b) -> p b r a", b=128)[:, 8 * kg:8 * kg + 8, :, :]
                inv = lambda bk=bk: bk[:].rearrange("p (j r a) -> p j r a", j=8, r=2)
                if kg % 2 == 0:
                    S.op("act", lambda e, outv=outv, inv=inv: e.copy(out=outv(), in_=inv()), reads=[bk], writes=[GT])
                else:
                    S.op("dve", lambda e, outv=outv, inv=inv: e.tensor_copy(out=outv(), in_=inv()), reads=[bk], writes=[GT])
            if cbl == 1:
                g = cb // 2
                for mb in range(2):
                    for tg in range(TOK // 512):
                        bk = banks[6]
                        n = 0
                        for nb in range(2):
                            for ri in range(2):
                                S.op("pe", lambda e, nb=nb, ri=ri, mb=mb, tg=tg, n=n: e.matmul(
                                    banks[6][:], lhsT=CS[:, ri, nb, mb, :], rhs=GT[:, nb, ri, tg * 512:(tg + 1) * 512],
                                    start=(n == 0), stop=(n == 3)), reads=[CS, GT], writes=[bk], inc=(n == 3))
                                n += 1
                        if tg % 2 == 0:
                            S.op("act", lambda e, g=g, mb=mb, tg=tg: e.copy(out=FT[:, 2 * g + mb, tg * 512:(tg + 1) * 512], in_=banks[6][:]),
                                 reads=[bk], writes=[FT])
                        else:
                            S.op("dve", lambda e, g=g, mb=mb, tg=tg: e.tensor_copy(out=FT[:, 2 * g + mb, tg * 512:(tg + 1) * 512], in_=banks[6][:]),
                                 reads=[bk], writes=[FT])

    with Scope(P):
        wbf = P.sb("wfbf", [128, 8, D], BF16)
        wst = [P.sb(f"wfst{i}", [128, D], F32) for i in range(2)]
        for k in range(8):
            ws = wst[k % 2]
            S.dma("pool", ws[:], w_f[k * 128:(k + 1) * 128, :], writes=[ws])
            S.op("pool", lambda e, k=k, ws=ws: e.tensor_copy(out=wbf[:, k, :], in_=ws[:]), reads=[ws], writes=[wbf])
        GG1 = P.sb("GG1", [128, D], F32)
        G2 = P.sb("G2", [128, D], F32)
        SH2 = P.sb("SH2", [128, D], F32)
        tmpa = P.sb("tmpa", [128, D], F32)
        tmpb = P.sb("tmpb", [128, D], F32)
        S.dma("sp", tmpa[:], M[:, 2 * D:3 * D], writes=[tmpa])
        S.dma("sp", tmpb[:], ng[:, 1 * D:2 * D], writes=[tmpb])
        S.op("dve", lambda e: e.scalar_tensor_tensor(out=GG1[:], in0=tmpa[:], scalar=1.0 / 2048, in1=tmpb[:], op0=ALU.mult, op1=ALU.mult),
             reads=[tmpa, tmpb], writes=[GG1])
        S.dma("sp", tmpa[:], M[:, 4 * D:5 * D], writes=[tmpa])
        S.dma("sp", tmpb[:], ng[:, 2 * D:3 * D], writes=[tmpb])
        S.op("dve", lambda e: e.scalar_tensor_tensor(out=G2[:], in0=tmpa[:], scalar=1.0, in1=tmpb[:], op0=ALU.add, op1=ALU.mult),
             reads=[tmpa, tmpb], writes=[G2])
        S.dma("sp", SH2[:], M[:, 3 * D:4 * D], writes=[SH2])
        RD = 3
        xt = [P.sb(f"xt{i}", [128, D], F32) for i in range(RD)]
        ysb_ = [P.sb(f"ysb{i}", [128, D], F32) for i in range(RD)]
        junk_ = [P.sb(f"junk{i}", [128, D], BF16) for i in range(RD)]
        ms_ = [P.sb(f"ms{i}", [128, 2], F32) for i in range(RD)]
        rstd_ = [P.sb(f"rstd{i}", [128, 2], F32) for i in range(RD)]
        t1_ = [P.sb(f"t1{i}", [128, D], F32) for i in range(RD)]
        t2_ = [P.sb(f"t2{i}", [128, D], F32) for i in range(RD)]
        x1 = [P.sb(f"x1_{i}", [128, D], F32) for i in range(RD)]
        hb_ = [P.sb(f"hb{i}", [128, D], BF16) for i in range(RD)]
        hT_ = [P.sb(f"hT{i}", [128, 8, 128], BF16) for i in range(RD)]
        bYs = [[P.bank(f"bY{i}_{n}", [128, 512], F32) for n in range(2)] for i in range(2)]
        bTs = [P.bank(f"bT{i}", [128, 1024], BF16) for i in range(2)]
        pipeline(resid_tile(P, t, x2, x3o, h2T, lambda h, t=t: FT[:, h, t * 128:(t + 1) * 128], FT, wbf, GG1, G2, SH2, eps_t, idt,
                            xt[t % RD], x1[t % RD], ysb_[t % RD], junk_[t % RD], ms_[t % RD], rstd_[t % RD], t1_[t % RD], hb_[t % RD], hT_[t % RD],
                            bYs[t % 2], bTs[t % 2], 1.0 / (32 * 2048), t2_[t % RD]) for t in range(NT))
    return P.finish()


def dft_tables(s):
    t = np.arange(128)
    ang = 2 * np.pi * np.outer(t, t) / 128.0
    Dcs = np.concatenate([np.cos(ang), np.sin(ang)], axis=1)
    k1 = 32 * s + np.arange(32)
    k2 = np.arange(128)
    t1 = np.arange(128)
    k = 128 * k1[None, None, :] + k2[None, :, None]
    ph = (k.astype(np.int64) * t1[:, None, None].astype(np.int64)) % L
    th = 2 * np.pi * ph / float(L)
    Ec, Es = np.cos(th), np.sin(th)
    E = np.concatenate([Ec, Es, -Es, Ec], axis=2)
    E = E.reshape(128, 16, 8, 128).transpose(1, 0, 2, 3)
    n = np.arange(256)
    a = 2 * np.pi * ((np.outer(n, n)) % 256) / 256.0
    C = np.cos(a).reshape(2, 128, 2, 128)
    Sn = -np.sin(a).reshape(2, 128, 2, 128)
    CS = np.stack([C, Sn], axis=0).transpose(2, 0, 1, 3, 4)
    return (np.ascontiguousarray(Dcs).astype(NPBF), np.ascontiguousarray(E).astype(NPBF),
            np.ascontiguousarray(CS).astype(NPBF))


def launch_fnet(inp, h1_list, x2_list, M_list):
    nc = build_fnet()
    ident = np.eye(128, dtype=np.float32).astype(NPBF)
    ngb = np.ascontiguousarray(np.broadcast_to(inp["norm_g"][1].reshape(1, 4 * D), (128, 4 * D)))
    maps = []
    for core in range(NCORE):
        b, s = divmod(core, 4)
        h_all = np.ascontiguousarray(np.concatenate([h1_list[b * 4 + i] for i in range(4)], axis=0))
        Dcs, E, CS = dft_tables(s)
        maps.append(dict(h_all=h_all, x2=x2_list[core], Dcs=Dcs, E=E, CS=CS, w_f=inp["fourier_w_out"][0],
                         M=np.ascontiguousarray(M_list[core][:, 1, :]), ng=ngb, ident=ident))
    return run(nc, maps)


def _np(res):
    return [{k: np.asarray(v) for k, v in r.items()} for r in res]


def kernel(**inputs):
    inp = {k: np.asarray(v) for k, v in inputs.items()}
    r1 = _np(launch_proj(inp))
    Ms = [r["M"] for r in r1]
    r2 = _np(launch_attn(inp, r1))
    r3 = _np(launch_ffn(inp, 0, [r["h2T"] for r in r2], [r["x1"] for r in r2], Ms, True))
    del r2
    r4 = _np(launch_fnet(inp, [r["h1n"] for r in r3], [r["x2"] for r in r3], Ms))
    del r3
    r5 = _np(launch_ffn(inp, 1, [r["h2T"] for r in r4], [r["x3"] for r in r4], Ms, False))
    out = np.empty((NB, L, D), np.float32)
    for core in range(NCORE):
        b, s = divmod(core, 4)
        out[b, s * TOK:(s + 1) * TOK] = r5[core]["x2"]
    return out
```
